# Optimizing a Trainium2 kernel written in Bass

```python
import jax, jax.numpy as jnp
from jax import lax
import numpy as np

D_MODEL = 2048
BATCH = 8
SEQ = 2048
DEPTH = 4

MOBA_WIDTH = D_MODEL // 2
MOBA_HEAD_DIM = 64
MOBA_HEADS = MOBA_WIDTH // MOBA_HEAD_DIM
MOBA_BLOCK = 256
MOBA_TOPK = 3
MOBA_Q_CHUNK = 16
ROPE_THETA = 500000.0
ROPE_DIM = MOBA_HEAD_DIM // 4
GMLP_WIDTH = D_MODEL - MOBA_WIDTH
GMLP_CHUNK = 128
GMLP_GROUP_DIM = 128
GMLP_GROUPS = GMLP_WIDTH // GMLP_GROUP_DIM
GMLP_LN_EPS = 1e-5
EVEN_IN_WIDTH = 3 * MOBA_WIDTH + 2 * GMLP_WIDTH

RWKV_HEAD_DIM = 64
RWKV_HEADS = D_MODEL // RWKV_HEAD_DIM
DECAY_LORA = 96
AAA_LORA = 96
MV_LORA = 64
GATE_LORA = 256
RWKV_GN_EPS = 64e-5

D_FF = ((8 * D_MODEL // 3 + 255) // 256) * 256
CONV_WIDTH = 3
NORM_EPS = 1e-6

N_EVEN = (DEPTH + 1) // 2
N_ODD = DEPTH // 2
N_VRES = max(N_ODD - 1, 0)

kernel_name = 'hybrid_moba_gmlp_rwkv7_convffn'


def rms_norm(x, g):
    xf = x.astype(jnp.float32)
    y = xf * lax.rsqrt(jnp.mean(xf * xf, axis=-1, keepdims=True) + NORM_EPS)
    return (y * g.astype(jnp.float32)).astype(x.dtype)


def rope_tables(seq, dtype):
    inv = jnp.power(ROPE_THETA, -jnp.arange(0, ROPE_DIM, 2, dtype=jnp.float32) / ROPE_DIM)
    ang = jnp.arange(seq, dtype=jnp.float32)[:, None] * inv[None, :]
    return jnp.cos(ang).astype(dtype), jnp.sin(ang).astype(dtype)


def partial_rotary(t, cos, sin):
    half = ROPE_DIM // 2
    t1 = t[..., :half]
    t2 = t[..., half:ROPE_DIM]
    return jnp.concatenate([t1 * cos - t2 * sin, t1 * sin + t2 * cos, t[..., ROPE_DIM:]], axis=-1)


def moba_attention(q, k, v):
    B, H, S, Dh = q.shape
    nb = -(-S // MOBA_BLOCK)
    pad = nb * MOBA_BLOCK - S
    kp = jnp.pad(k, ((0, 0), (0, 0), (0, pad), (0, 0)))
    vp = jnp.pad(v, ((0, 0), (0, 0), (0, pad), (0, 0)))
    kb = kp.reshape(B, H, nb, MOBA_BLOCK, Dh)
    vb = vp.reshape(B, H, nb, MOBA_BLOCK, Dh)
    k_mean = jnp.mean(kb.astype(jnp.float32), axis=3).astype(q.dtype)
    n_sel = min(MOBA_TOPK, nb)
    scale = Dh ** -0.5
    bi = jnp.arange(B)[:, None, None, None]
    hi = jnp.arange(H)[None, :, None, None]

    def one_chunk(c):
        start = c * MOBA_Q_CHUNK
        blk = start // MOBA_BLOCK
        qc = lax.dynamic_slice_in_dim(q, start, MOBA_Q_CHUNK, axis=2)
        gate = jnp.einsum('bhqd,bhnd->bhqn', qc, k_mean).astype(jnp.float32)
        gate = jnp.where(jnp.arange(nb) < blk, gate, -jnp.inf)
        _, sel = lax.top_k(gate, n_sel)
        sel_ok = jnp.arange(n_sel) < blk
        k_sel = kb[bi, hi, sel]
        v_sel = vb[bi, hi, sel]
        s_sel = jnp.einsum('bhqd,bhqnkd->bhqnk', qc, k_sel).astype(jnp.float32) * scale
        s_sel = jnp.where(sel_ok[:, None], s_sel, -jnp.inf)
        k_own = lax.dynamic_slice_in_dim(kp, blk * MOBA_BLOCK, MOBA_BLOCK, axis=2)
        v_own = lax.dynamic_slice_in_dim(vp, blk * MOBA_BLOCK, MOBA_BLOCK, axis=2)
        s_own = jnp.einsum('bhqd,bhkd->bhqk', qc, k_own).astype(jnp.float32) * scale
        q_pos = start + jnp.arange(MOBA_Q_CHUNK)
        k_pos = blk * MOBA_BLOCK + jnp.arange(MOBA_BLOCK)
        s_own = jnp.where(k_pos[None, :] <= q_pos[:, None], s_own, -jnp.inf)
        scores = jnp.concatenate([s_sel.reshape(B, H, MOBA_Q_CHUNK, n_sel * MOBA_BLOCK), s_own], axis=-1)
        p = jax.nn.softmax(scores, axis=-1).astype(v.dtype)
        p_sel = p[..., :n_sel * MOBA_BLOCK].reshape(B, H, MOBA_Q_CHUNK, n_sel, MOBA_BLOCK)
        p_own = p[..., n_sel * MOBA_BLOCK:]
        return (jnp.einsum('bhqnk,bhqnkd->bhqd', p_sel, v_sel)
                + jnp.einsum('bhqk,bhkd->bhqd', p_own, v_own))

    out = lax.map(one_chunk, jnp.arange(S // MOBA_Q_CHUNK))
    return out.transpose(1, 2, 0, 3, 4).reshape(B, H, S, Dh)


def chunked_sgu(u, v, ln_g, ln_b, w_s, b_s):
    B, S, _ = v.shape
    nc = S // GMLP_CHUNK
    vg = v.reshape(B, nc, GMLP_CHUNK, GMLP_GROUPS, GMLP_GROUP_DIM).astype(jnp.float32)
    mu = jnp.mean(vg, axis=-1, keepdims=True)
    var = jnp.mean(jnp.square(vg - mu), axis=-1, keepdims=True)
    vn = (vg - mu) * lax.rsqrt(var + GMLP_LN_EPS)
    vn = (vn * ln_g.reshape(GMLP_GROUPS, GMLP_GROUP_DIM).astype(jnp.float32)
          + ln_b.reshape(GMLP_GROUPS, GMLP_GROUP_DIM).astype(jnp.float32)).astype(v.dtype)
    causal = jnp.tril(jnp.ones((GMLP_CHUNK, GMLP_CHUNK), dtype=bool))
    w = jnp.where(causal[None], w_s, 0.0)
    mixed = jnp.einsum('gts,bcsgd->bctgd', w, vn) + b_s.T[:, :, None]
    return u * mixed.reshape(B, S, GMLP_WIDTH).astype(u.dtype)


def even_mixer(h, w_in, w_out, ln_g, ln_b, w_s, b_s, cos, sin):
    B, S, _ = h.shape
    z = h @ w_in
    q, k, v, u_g, v_g = jnp.split(
        z, [MOBA_WIDTH, 2 * MOBA_WIDTH, 3 * MOBA_WIDTH, 3 * MOBA_WIDTH + GMLP_WIDTH], axis=-1)

    def heads(t):
        return t.reshape(B, S, MOBA_HEADS, MOBA_HEAD_DIM).transpose(0, 2, 1, 3)

    att = moba_attention(partial_rotary(heads(q), cos, sin), partial_rotary(heads(k), cos, sin), heads(v))
    att = att.transpose(0, 2, 1, 3).reshape(B, S, MOBA_WIDTH)
    sgu = chunked_sgu(jax.nn.gelu(u_g, approximate=False), jax.nn.gelu(v_g, approximate=False),
                      ln_g, ln_b, w_s, b_s)
    return jnp.concatenate([att, sgu], axis=-1) @ w_out


def rwkv7_time_mix(h, mu, w_rkv, w_o, w0, w1, w2, a0, a1, a2, g1, g2, k_k, k_a, r_k, gn_g, gn_b,
                   v_first, vres):
    B, S, D = h.shape
    H, N = RWKV_HEADS, RWKV_HEAD_DIM
    f32 = jnp.float32
    h_prev = jnp.pad(h, ((0, 0), (1, 0), (0, 0)))[:, :-1]
    dx = h_prev - h
    xr = h + dx * mu[0]
    xw = h + dx * mu[1]
    xk = h + dx * mu[2]
    xv = h + dx * mu[3]
    xa = h + dx * mu[4]
    xg = h + dx * mu[5]
    r = xr @ w_rkv[0]
    k = xk @ w_rkv[1]
    v = xv @ w_rkv[2]
    w_log = -jax.nn.softplus(-(w0 + jnp.tanh(xw @ w1) @ w2).astype(f32)) - 0.5
    decay = jnp.exp(-jnp.exp(w_log))
    a = jax.nn.sigmoid((a0 + (xa @ a1) @ a2).astype(f32))
    g = jax.nn.sigmoid(xg @ g1) @ g2
    if vres is None:
        v_first = v
    else:
        v0, v1, v2 = vres
        v = v + (v_first - v) * jax.nn.sigmoid(v0 + (xv @ v1) @ v2)
    kk = (k * k_k).astype(f32).reshape(B, S, H, N)
    kk = kk / jnp.maximum(jnp.sqrt(jnp.sum(kk * kk, axis=-1, keepdims=True)), 1e-12)
    k = k.astype(f32) * (1.0 + (a - 1.0) * k_a.astype(f32))

    r4 = r.astype(f32).reshape(B, S, H, N)
    k4 = k.reshape(B, S, H, N)
    v4 = v.astype(f32).reshape(B, S, H, N)
    w4 = decay.reshape(B, S, H, N)
    a4 = a.reshape(B, S, H, N)

    def tmaj(t):
        return t.transpose(1, 0, 2, 3)

    def step(state, inp):
        r_t, w_t, k_t, v_t, kk_t, a_t = inp
        sa = jnp.einsum('bhvk,bhk->bhv', state, -kk_t)
        state = (state * w_t[:, :, None, :] + sa[..., None] * (kk_t * a_t)[:, :, None, :]
                 + v_t[..., None] * k_t[:, :, None, :])
        return state, jnp.einsum('bhvk,bhk->bhv', state, r_t)

    s0 = jnp.zeros((B, H, N, N), f32)
    _, o = lax.scan(step, s0, (tmaj(r4), tmaj(w4), tmaj(k4), tmaj(v4), tmaj(kk), tmaj(a4)))
    o = o.transpose(1, 0, 2, 3)
    m = jnp.mean(o, axis=-1, keepdims=True)
    var = jnp.mean(jnp.square(o - m), axis=-1, keepdims=True)
    o = ((o - m) * lax.rsqrt(var + RWKV_GN_EPS) * gn_g.astype(f32).reshape(H, N)
         + gn_b.astype(f32).reshape(H, N))
    bonus = jnp.sum(r4 * k4 * r_k.astype(f32).reshape(H, N), axis=-1, keepdims=True) * v4
    o = (o + bonus).reshape(B, S, D).astype(h.dtype)
    return (o * g) @ w_o, v_first


def conv_ffn(h, w_up, conv_w, conv_b, w_down):
    S = h.shape[1]
    gate, up = jnp.split(h @ w_up, 2, axis=-1)
    gp = jnp.pad(gate, ((0, 0), (CONV_WIDTH - 1, 0), (0, 0)))
    conv = conv_b + gp[:, 0:S] * conv_w[0]
    for j in range(1, CONV_WIDTH):
        conv = conv + gp[:, j:j + S] * conv_w[j]
    return (jax.nn.silu(conv) * up) @ w_down


def setup_inputs(seed: int = 0) -> dict:
    key = jax.random.key(seed)
    ks = iter(jax.random.split(key, 40))
    f32 = jnp.float32
    D = D_MODEL

    def nrm(shape, scale):
        return jax.random.normal(next(ks), shape, f32) * scale

    def gain(shape):
        return 1.0 + nrm(shape, 0.02)

    return {
        'x': nrm((BATCH, SEQ, D), 1.0),
        'mix_norm_g': gain((DEPTH, D)),
        'ffn_norm_g': gain((DEPTH, D)),
        'final_norm_g': gain((D,)),
        'even_w_in': nrm((N_EVEN, D, EVEN_IN_WIDTH), D ** -0.5),
        'even_w_out': nrm((N_EVEN, MOBA_WIDTH + GMLP_WIDTH, D), (MOBA_WIDTH + GMLP_WIDTH) ** -0.5),
        'sgu_ln_g': gain((N_EVEN, GMLP_WIDTH)),
        'sgu_ln_b': nrm((N_EVEN, GMLP_WIDTH), 0.02),
        'sgu_w': nrm((N_EVEN, GMLP_GROUPS, GMLP_CHUNK, GMLP_CHUNK), GMLP_CHUNK ** -0.5),
        'sgu_b': nrm((N_EVEN, GMLP_GROUPS, GMLP_CHUNK), 0.02),
        'rwkv_mu': jax.random.uniform(next(ks), (N_ODD, 6, D), f32),
        'rwkv_w_rkv': nrm((N_ODD, 3, D, D), D ** -0.5),
        'rwkv_w_o': nrm((N_ODD, D, D), D ** -0.5),
        'rwkv_w0': jnp.linspace(-6.0, -1.0, D, dtype=f32)[None, :] + nrm((N_ODD, D), 0.1),
        'rwkv_w1': nrm((N_ODD, D, DECAY_LORA), D ** -0.5),
        'rwkv_w2': nrm((N_ODD, DECAY_LORA, D), 0.1 * DECAY_LORA ** -0.5),
        'rwkv_a0': nrm((N_ODD, D), 0.1),
        'rwkv_a1': nrm((N_ODD, D, AAA_LORA), D ** -0.5),
        'rwkv_a2': nrm((N_ODD, AAA_LORA, D), 0.5 * AAA_LORA ** -0.5),
        'rwkv_g1': nrm((N_ODD, D, GATE_LORA), D ** -0.5),
        'rwkv_g2': nrm((N_ODD, GATE_LORA, D), GATE_LORA ** -0.5),
        'rwkv_k_k': 0.85 + nrm((N_ODD, D), 0.02),
        'rwkv_k_a': gain((N_ODD, D)),
        'rwkv_r_k': nrm((N_ODD, D), 0.1),
        'rwkv_gn_g': gain((N_ODD, D)),
        'rwkv_gn_b': nrm((N_ODD, D), 0.02),
        'rwkv_v0': nrm((N_VRES, D), 0.1),
        'rwkv_v1': nrm((N_VRES, D, MV_LORA), D ** -0.5),
        'rwkv_v2': nrm((N_VRES, MV_LORA, D), 0.5 * MV_LORA ** -0.5),
        'ffn_w_up': nrm((DEPTH, D, 2 * D_FF), D ** -0.5),
        'ffn_conv_w': nrm((DEPTH, CONV_WIDTH, D_FF), CONV_WIDTH ** -0.5),
        'ffn_conv_b': nrm((DEPTH, D_FF), 0.02),
        'ffn_w_down': nrm((DEPTH, D_FF, D), D_FF ** -0.5),
    }


def reference(x, mix_norm_g, ffn_norm_g, final_norm_g,
              even_w_in, even_w_out, sgu_ln_g, sgu_ln_b, sgu_w, sgu_b,
              rwkv_mu, rwkv_w_rkv, rwkv_w_o, rwkv_w0, rwkv_w1, rwkv_w2,
              rwkv_a0, rwkv_a1, rwkv_a2, rwkv_g1, rwkv_g2, rwkv_k_k, rwkv_k_a, rwkv_r_k,
              rwkv_gn_g, rwkv_gn_b, rwkv_v0, rwkv_v1, rwkv_v2,
              ffn_w_up, ffn_conv_w, ffn_conv_b, ffn_w_down):
    S = x.shape[1]
    cos, sin = rope_tables(S, x.dtype)
    v_first = None
    for layer in range(DEPTH):
        i = layer // 2
        h = rms_norm(x, mix_norm_g[layer])
        if layer % 2 == 0:
            y = even_mixer(h, even_w_in[i], even_w_out[i], sgu_ln_g[i], sgu_ln_b[i],
                           sgu_w[i], sgu_b[i], cos, sin)
        else:
            vres = None if v_first is None else (rwkv_v0[i - 1], rwkv_v1[i - 1], rwkv_v2[i - 1])
            y, v_first = rwkv7_time_mix(h, rwkv_mu[i], rwkv_w_rkv[i], rwkv_w_o[i], rwkv_w0[i],
                                        rwkv_w1[i], rwkv_w2[i], rwkv_a0[i], rwkv_a1[i], rwkv_a2[i],
                                        rwkv_g1[i], rwkv_g2[i], rwkv_k_k[i], rwkv_k_a[i], rwkv_r_k[i],
                                        rwkv_gn_g[i], rwkv_gn_b[i], v_first, vres)
        x = x + y
        h = rms_norm(x, ffn_norm_g[layer])
        x = x + conv_ffn(h, ffn_w_up[layer], ffn_conv_w[layer], ffn_conv_b[layer], ffn_w_down[layer])
    return rms_norm(x, final_norm_g)
```

```python
import numpy as np
import ml_dtypes
import concourse.bass as bass
import concourse.mybir as mybir
from concourse.bass_utils import run_bass_kernel_spmd
from contextlib import ExitStack

F32 = mybir.dt.float32
BF16 = mybir.dt.bfloat16
ALU = mybir.AluOpType
AF = mybir.ActivationFunctionType
AX = mybir.AxisListType

D = 2048
T = 2048
DFF = 5632
NCORES = 8
COMPUTE = ("tensor", "vector", "scalar", "gpsimd")
SEM_ROLL = 30000
NEG = -30000.0
EXPM05 = float(np.exp(-0.5))


class Prog:
    def __init__(self, nc, stack, n_dma_sems=16):
        self.nc = nc
        self.stack = stack
        self.engs = {"tensor": nc.tensor, "vector": nc.vector, "scalar": nc.scalar,
                     "gpsimd": nc.gpsimd, "sync": nc.sync}
        self.sem_id = 0
        self.eng_sem = {}
        self.eng_cnt = {}
        for e in COMPUTE:
            self.eng_sem[e] = self._new_sem("c_" + e)
            self.eng_cnt[e] = 0
        self.dma_pool = {"sync": [[self._new_sem("d_sync%d" % i), 0] for i in range(n_dma_sems)],
                         "scalar": [[self._new_sem("d_act%d" % i), 0] for i in range(8)]}
        self.dma_rr = {"sync": 0, "scalar": 0}
        self.known = {e: {} for e in self.engs}
        self._reset()
        self.n_ops = 0
        self.n_waits = 0

    def _reset(self):
        self.ops = {e: [] for e in self.engs}
        self.last_write = {}
        self.readers = {}

    def _new_sem(self, name):
        self.sem_id += 1
        return self.stack.enter_context(self.nc.semaphore("%s_%d" % (name, self.sem_id)))

    def _deps(self, reads, writes):
        deps = set()
        for k in reads:
            deps |= self.last_write.get(k, set())
        for k in writes:
            deps |= self.last_write.get(k, set())
            deps |= self.readers.get(k, set())
        return deps

    def _waits(self, eng, deps):
        best = {}
        for (sem, val) in deps:
            if id(sem) not in best or best[id(sem)][1] < val:
                best[id(sem)] = (sem, val)
        out = []
        kn = self.known[eng]
        for sid, (sem, val) in best.items():
            if kn.get(sid, 0) >= val:
                continue
            kn[sid] = val
            out.append((sem, val))
        return out

    def _commit(self, ev, reads, writes):
        for k in reads:
            self.readers.setdefault(k, set()).add(ev)
        for k in writes:
            self.last_write[k] = {ev}
            self.readers[k] = set()

    def op(self, eng, fn, reads=(), writes=()):
        deps = self._deps(reads, writes)
        own = self.eng_sem[eng]
        if eng == "tensor":
            deps = {d for d in deps if d[0] is not own}
        waits = self._waits(eng, deps)
        if self.eng_cnt[eng] >= SEM_ROLL:
            self.eng_sem[eng] = self._new_sem("c_" + eng)
            self.eng_cnt[eng] = 0
        self.eng_cnt[eng] += 1
        ev = (self.eng_sem[eng], self.eng_cnt[eng])
        self.ops[eng].append((waits, fn, ev[0], 1))
        self._commit(ev, reads, writes)
        self.n_waits += len(waits)
        self.n_ops += 1
        return ev

    def dma(self, out, in_, reads=(), writes=(), q="sync"):
        deps = set(self._deps(reads, writes))
        pool = self.dma_pool[q]
        i = self.dma_rr[q]
        self.dma_rr[q] = (i + 1) % len(pool)
        slot = pool[i]
        if slot[1] > 0:
            deps.add((slot[0], slot[1]))
        if slot[1] >= SEM_ROLL:
            slot[0] = self._new_sem("d_%s" % q)
            slot[1] = 0
        sem = slot[0]
        waits = self._waits(q, deps)
        slot[1] += 16
        ev = (sem, slot[1])

        def fn(e, out=out, in_=in_):
            return e.dma_start(out=out, in_=in_)
        self.ops[q].append((waits, fn, sem, 16))
        self._commit(ev, reads, writes)
        self.n_waits += len(waits)
        self.n_ops += 1
        return ev

    def flush(self):
        self.n_blocks = getattr(self, "n_blocks", 0) + 1
        if self.n_blocks > getattr(self, "max_blocks", 10 ** 9):
            self._reset()
            return
        finals = set()
        for q, pool in self.dma_pool.items():
            for sem, cnt in pool:
                if cnt > 0:
                    finals.add((sem, cnt))
        for e in COMPUTE:
            if self.eng_cnt[e] > 0:
                finals.add((self.eng_sem[e], self.eng_cnt[e]))
        fw = self._waits("sync", finals)
        self.ops["sync"].append((fw, None, None, 0))
        ops = self.ops
        with self.nc.Block() as block:
            def mk(ename):
                def body(e):
                    for (waits, fn, sem, inc) in ops[ename]:
                        for (s, v) in waits:
                            e.wait_ge(s, v)
                        if fn is not None:
                            fn(e).then_inc(sem, inc)
                return body
            for ename in ("sync", "gpsimd", "scalar", "vector", "tensor"):
                if ops[ename]:
                    getattr(block, ename)(mk(ename))
        self._reset()


def TT(out, in0, in1, op):
    return lambda e: e.tensor_tensor(out=out, in0=in0, in1=in1, op=op)


def TS(out, in0, s1, s2=None, op0=ALU.mult, op1=None):
    if op1 is None:
        return lambda e: e.tensor_scalar(out=out, in0=in0, scalar1=s1, scalar2=None, op0=op0)
    return lambda e: e.tensor_scalar(out=out, in0=in0, scalar1=s1, scalar2=s2, op0=op0, op1=op1)


def STT(out, in0, scalar, in1, op0, op1):
    return lambda e: e.scalar_tensor_tensor(out=out, in0=in0, scalar=scalar, in1=in1, op0=op0, op1=op1)


def ACT(out, in_, func, bias=None, scale=None, accum=None):
    kw = {}
    if bias is not None:
        kw["bias"] = bias
    if scale is not None:
        kw["scale"] = scale
    if accum is not None:
        kw["accum_out"] = accum
    return lambda e: e.activation(out=out, in_=in_, func=func, **kw)


def CP(out, in_):
    return lambda e: e.tensor_copy(out=out, in_=in_)


def ACP(out, in_):
    return lambda e: e.copy(out=out, in_=in_)


def MM(lst):
    def fn(e):
        ins = None
        for (out, lhsT, rhs, start, stop) in lst:
            ins = e.matmul(out, lhsT=lhsT, rhs=rhs, start=start, stop=stop)
        return ins
    return fn


def TRS(lst):
    def fn(e):
        ins = None
        for (out, in_, ident) in lst:
            ins = e.transpose(out, in_, ident)
        return ins
    return fn


def RSUM(out, in_):
    return lambda e: e.reduce_sum(out=out, in_=in_, axis=AX.X)


def RMAX(out, in_):
    return lambda e: e.reduce_max(out=out, in_=in_, axis=AX.X)


def MSET(ap, val):
    return lambda e: e.memset(ap, val)


def RECIP(out, in_):
    return lambda e: e.reciprocal(out=out, in_=in_)


def MAX8(out, in_):
    return lambda e: e.max(out=out, in_=in_)


class Builder:
    def __init__(self, plan):
        self.plan = plan
        self.nc = bass.Bass("TRN2", target_bir_lowering=False)
        self.inputs = {}

    def din(self, name, shape, dt=F32):
        t = self.nc.dram_tensor(name, list(shape), dt, kind="ExternalInput").ap()
        self.inputs[name] = t
        return t

    def dscr(self, name, shape, dt):
        return self.nc.dram_tensor(name, list(shape), dt, kind="Internal").ap()

    def sb(self, st, name, shape, dt):
        self._uid += 1
        return st.enter_context(self.nc.sbuf_tensor("%s_%d" % (name, self._uid), list(shape), dt))

    def pp(self, st, name, shape, dt):
        self._uid += 1
        return st.enter_context(self.nc.psum_tensor("%s_%d" % (name, self._uid), list(shape), dt))

    def declare(self):
        nc = self.nc
        self._uid = 0
        self.xT = self.din("xT", [D, T])
        self.vfm = self.din("vfm", [128, 21, 16])
        self.convp = self.din("convp", [128, 4, 4, 44])
        self.rv = self.din("rv", [2, 8, D])
        self.sgu_ln = self.din("sgu_ln", [2, 2, 1024])
        self.sgu_wT = self.din("sgu_wT", [2, 8, 128, 128])
        self.sgu_b = self.din("sgu_b", [2, 1024])
        self.w_in = self.din("w_in", [2, D, 7168])
        self.w_out = self.din("w_out", [2, D, D])
        self.w_rkv = self.din("w_rkv", [2, 3, D, D])
        self.w_o = self.din("w_o", [2, D, D])
        self.w1 = self.din("w1", [2, D, 96])
        self.w2 = self.din("w2", [2, 96, D])
        self.a1 = self.din("a1", [2, D, 96])
        self.a2 = self.din("a2", [2, 96, D])
        self.g1 = self.din("g1", [2, D, 256])
        self.g2 = self.din("g2", [2, 256, D])
        self.v1 = self.din("v1", [1, D, 64])
        self.v2 = self.din("v2", [1, 64, D])
        self.w_up = self.din("w_up", [4, D, 2 * DFF])
        self.w_down = self.din("w_down", [4, DFF, D])
        self.consts = self.din("consts", [128, 8, 128])
        self.rope = self.din("rope", [2, 128, T])
        self.keep = self.din("keepm", [128, 16, 16])
        self.gmask = self.din("gmask", [128, 16, 8])
        self.out = nc.dram_tensor("out", [D, T], F32, kind="ExternalOutput").ap()
        self.xres = self.dscr("xres", [D, T], F32)
        self.hbuf = self.dscr("hbuf", [D, T], BF16)
        self.gT = self.dscr("gT", [DFF, T], BF16)
        self.qT = self.dscr("qT", [1024, T], BF16)
        self.kT = self.dscr("kT", [1024, T], BF16)
        self.vtm = self.dscr("vtm", [T, 1024], BF16)
        self.uT = self.dscr("uT", [1024, T], BF16)
        self.vn = self.dscr("vn", [T, 1024], BF16)
        self.ycat = self.dscr("ycat", [D, T], BF16)
        self.xmix = [self.dscr("xmix%d" % i, [D, T], BF16) for i in range(6)]
        self.r_tm = self.dscr("r_tm", [T, D], F32)
        self.k_tm = self.dscr("k_tm", [T, D], F32)
        self.v_tm = self.dscr("v_tm", [T, D], F32)
        self.vf_tm = self.dscr("vf_tm", [T, D], F32)
        self.zw_tm = self.dscr("zw_tm", [T, D], F32)
        self.za_tm = self.dscr("za_tm", [T, D], F32)
        self.zv_tm = self.dscr("zv_tm", [T, D], F32)
        self.g_tm = self.dscr("g_tm", [T, D], F32)

    def copy_in(self):
        P = self.P
        with ExitStack() as st:
            bufs = [self.sb(st, "cpy", [128, 16, 256], F32) for _ in range(2)]
            src = self.xT.rearrange("(c p) t -> p c t", p=128)
            dst = self.xres.rearrange("(c p) t -> p c t", p=128)
            for i in range(8):
                b = bufs[i % 2]
                P.dma(b[:], src[:, :, i * 256:(i + 1) * 256], writes=["cp%d" % (i % 2)])
                P.dma(dst[:, :, i * 256:(i + 1) * 256], b[:], reads=["cp%d" % (i % 2)])
            P.flush()

    def norm_phase(self, gidx, mode, mu_base=None):
        P = self.P
        TB = 256
        with ExitStack() as st:
            xt = [self.sb(st, "nx", [128, 16, TB], F32) for _ in range(2)]
            sq = self.sb(st, "nsq", [128, 16, TB], BF16)
            rstd = self.sb(st, "nrstd", [128, TB], F32)
            ones = self.sb(st, "nones", [128, 128], BF16)
            gv = self.sb(st, "ngv", [128, 21, 16], F32)
            eps = self.sb(st, "neps", [128, 1], F32)
            ps = self.pp(st, "nps", [128, 512], F32)
            P.op("vector", MSET(ones[:], 1.0), writes=["ones"])
            P.op("vector", MSET(eps[:], 1e-6), writes=["eps"])
            P.dma(gv[:], self.vfm, writes=["gv"])
            src = self.xres.rearrange("(c p) t -> p c t", p=128)
            if mode == "plain":
                ho = [self.sb(st, "nho", [128, 16, TB], BF16) for _ in range(2)]
                dst = self.hbuf.rearrange("(c p) t -> p c t", p=128)
            elif mode == "final":
                ho = [self.sb(st, "nho", [128, 16, TB], F32) for _ in range(2)]
                dst = self.out.rearrange("(c p) t -> p c t", p=128)
            else:
                hf = self.sb(st, "nhf", [128, 16, TB + 1], F32)
                dx = [self.sb(st, "ndx", [128, TB], F32) for _ in range(2)]
                xo = [self.sb(st, "nxo", [128, 16, TB], BF16) for _ in range(6)]
                dsts = [m.rearrange("(c p) t -> p c t", p=128) for m in self.xmix]
                P.op("vector", MSET(hf[:], 0.0), writes=["hf"])
            P.dma(xt[0][:], src[:, :, 0:TB], writes=["nx0"])
            for tb in range(T // TB):
                x_ = xt[tb % 2]
                xk = "nx%d" % (tb % 2)
                sl = slice(tb * TB, (tb + 1) * TB)
                if tb + 1 < T // TB:
                    P.dma(xt[(tb + 1) % 2][:], src[:, :, (tb + 1) * TB:(tb + 2) * TB], writes=["nx%d" % ((tb + 1) % 2)])
                P.op("scalar", ACT(sq[:], x_[:], AF.Square), reads=[xk], writes=["sq"])
                P.op("tensor", MM([(ps[:, 0:TB], ones[:], sq[:, c, :], c == 0, c == 15) for c in range(16)]),
                     reads=["ones", "sq"], writes=["ps"])
                P.op("scalar", ACT(rstd[:], ps[:, 0:TB], AF.Sqrt, bias=eps[:], scale=1.0 / D),
                     reads=["ps", "eps"], writes=["rstd"])
                P.op("vector", RECIP(rstd[:], rstd[:]), reads=["rstd"], writes=["rstd"])
                if mode in ("plain", "final"):
                    h_ = ho[tb % 2]
                    hk = "ho%d" % (tb % 2)
                    for c in range(16):
                        P.op("vector", STT(h_[:, c, :], x_[:, c, :], gv[:, gidx, c:c + 1], rstd[:], ALU.mult, ALU.mult),
                             reads=[xk, "gv", "rstd"], writes=[hk])
                    P.dma(dst[:, :, sl], h_[:], reads=[hk])
                else:
                    if tb > 0:
                        P.op("vector", CP(hf[:, :, 0:1], hf[:, :, TB:TB + 1]), reads=["hf"], writes=["hf"])
                    for c in range(16):
                        P.op("vector", STT(hf[:, c, 1:TB + 1], x_[:, c, :], gv[:, gidx, c:c + 1], rstd[:], ALU.mult, ALU.mult),
                             reads=[xk, "gv", "rstd"], writes=["hf"])
                    for c in range(16):
                        d_ = dx[c % 2]
                        dk = "dx%d" % (c % 2)
                        P.op("gpsimd", TT(d_[:], hf[:, c, 0:TB], hf[:, c, 1:TB + 1], ALU.subtract), reads=["hf"], writes=[dk])
                        for i in range(6):
                            P.op("vector", STT(xo[i][:, c, :], d_[:], gv[:, mu_base + i, c:c + 1], hf[:, c, 1:TB + 1],
                                               ALU.mult, ALU.add), reads=[dk, "hf", "gv"], writes=["xo%d" % i])
                    for i in range(6):
                        P.dma(dsts[i][:, :, sl], xo[i][:], reads=["xo%d" % i])
            P.flush()

    def load_w(self, wst, wbf, key, src_ap, KC, ncols, col0=0, ndma=4):
        P = self.P
        v = src_ap.rearrange("(c p) m -> p c m", p=128)
        step = (KC + ndma - 1) // ndma
        for c0 in range(0, KC, step):
            c1 = min(KC, c0 + step)
            P.dma(wst[:, c0:c1, col0:col0 + ncols], v[:, c0:c1, :], writes=[key + "s"])
        P.op("gpsimd", CP(wbf[:, 0:KC, col0:col0 + ncols], wst[:, 0:KC, col0:col0 + ncols]),
             reads=[key + "s"], writes=[key])

    def load_hT(self, hT, src, key="hT", KC=16, t0=0, tn=T):
        v = src.rearrange("(c p) t -> p c t", p=128)
        for c0 in range(0, KC, 4):
            c1 = min(KC, c0 + 4)
            self.P.dma(hT[:, c0:c1, 0:tn], v[:, c0:c1, t0:t0 + tn], writes=[key])

    FFN_A_NEW = False
    FFN_B_NEW = True

    def ffn_phase(self, l):
        self.ffn_A_v3(l)
        (self.ffn_B_new if self.FFN_B_NEW else self.ffn_B_old)(l)

    def ffn_A_new(self, l):
        P = self.P
        with ExitStack() as st:
            hT = self.sb(st, "hT", [128, 16, T], BF16)
            wst = [self.sb(st, "wst", [128, 16, 256], F32) for _ in range(2)]
            wbf = [self.sb(st, "wbf", [128, 16, 256], BF16) for _ in range(2)]
            cv = [self.sb(st, "cv", [128, 1024], F32) for _ in range(2)]
            sl = [self.sb(st, "sl", [128, 1024], F32) for _ in range(2)]
            go = [self.sb(st, "go", [128, 1024], BF16) for _ in range(2)]
            bnd = self.sb(st, "bnd", [128, 2], F32)
            cp = self.sb(st, "cp", [128, 4, 4, 44], F32)
            psG = [self.pp(st, "psG", [128, 1024], F32) for _ in range(2)]
            psU = [self.pp(st, "psU", [128, 1024], F32) for _ in range(2)]
            P.dma(cp[:], self.convp, writes=["cp"])
            self.load_hT(hT, self.hbuf)

            def loadW(fc):
                b = fc % 2
                self.load_w(wst[b], wbf[b], "w%d" % b, self.w_up[l][:, fc * 128:(fc + 1) * 128], 16, 128, col0=0, ndma=2)
                self.load_w(wst[b], wbf[b], "w%du" % b, self.w_up[l][:, DFF + fc * 128:DFF + (fc + 1) * 128], 16, 128, col0=128, ndma=2)
            loadW(0)
            for fc in range(44):
                b = fc % 2
                wk = "w%d" % b
                if fc + 1 < 44:
                    loadW(fc + 1)
                w0 = cp[:, l, 0, fc:fc + 1]
                w1 = cp[:, l, 1, fc:fc + 1]
                w2 = cp[:, l, 2, fc:fc + 1]
                bb = cp[:, l, 3, fc:fc + 1]
                for th in range(2):
                    g_, u_ = psG[th], psU[th]
                    gk, uk = "psG%d" % th, "psU%d" % th
                    for tb in range(2):
                        ts = slice(tb * 512, (tb + 1) * 512)
                        hs = slice(th * 1024 + tb * 512, th * 1024 + (tb + 1) * 512)
                        P.op("tensor", MM([(g_[:, ts], wbf[b][:, c, 0:128], hT[:, c, hs], c == 0, c == 15) for c in range(16)]),
                             reads=[wk, "hT"], writes=[gk])
                    for tb in range(2):
                        ts = slice(tb * 512, (tb + 1) * 512)
                        hs = slice(th * 1024 + tb * 512, th * 1024 + (tb + 1) * 512)
                        P.op("tensor", MM([(u_[:, ts], wbf[b][:, c, 128:256], hT[:, c, hs], c == 0, c == 15) for c in range(16)]),
                             reads=[wk + "u", "hT"], writes=[uk])
                    c_ = cv[th]
                    ck = "cv%d" % th
                    P.op("vector", TS(c_[:], g_[:], w2, bb, ALU.mult, ALU.add), reads=[gk, "cp"], writes=[ck])
                    P.op("vector", STT(c_[:, 1:1024], g_[:, 0:1023], w1, c_[:, 1:1024], ALU.mult, ALU.add), reads=[gk, "cp", ck], writes=[ck])
                    P.op("vector", STT(c_[:, 2:1024], g_[:, 0:1022], w0, c_[:, 2:1024], ALU.mult, ALU.add), reads=[gk, "cp", ck], writes=[ck])
                    if th == 0:
                        P.op("scalar", ACP(bnd[:], g_[:, 1022:1024]), reads=[gk], writes=["bnd"])
                    else:
                        P.op("vector", STT(c_[:, 0:1], bnd[:, 1:2], w1, c_[:, 0:1], ALU.mult, ALU.add), reads=["bnd", "cp", ck], writes=[ck])
                        P.op("vector", STT(c_[:, 0:2], bnd[:, 0:2], w0, c_[:, 0:2], ALU.mult, ALU.add), reads=["bnd", "cp", ck], writes=[ck])
                    P.op("scalar", ACT(sl[th][:], c_[:], AF.Silu), reads=[ck], writes=["sl%d" % th])
                    P.op("vector", TT(go[th][:], sl[th][:], u_[:], ALU.mult), reads=["sl%d" % th, uk], writes=["go%d" % th])
                    P.dma(self.gT[fc * 128:(fc + 1) * 128, th * 1024:(th + 1) * 1024], go[th][:], reads=["go%d" % th])
            P.flush()
    def ffn_B_new(self, l):
        P = self.P
        with ExitStack() as st:
            gTs = self.sb(st, "gTs", [128, 44, 1024], BF16)
            wst = [self.sb(st, "wst", [128, 44, 128], F32) for _ in range(2)]
            wbf = [self.sb(st, "wbf", [128, 44, 128], BF16) for _ in range(2)]
            xr = [self.sb(st, "xr", [128, 1024], F32) for _ in range(2)]
            ps = [self.pp(st, "ps", [128, 1024], F32) for _ in range(2)]
            gv = self.gT.rearrange("(c p) t -> p c t", p=128)

            def loadB(j):
                th, dc = j // 16, j % 16
                b = j % 2
                self.load_w(wst[b], wbf[b], "w%d" % b, self.w_down[l][:, dc * 128:(dc + 1) * 128], 44, 128, ndma=8)
                P.dma(xr[b][:], self.xres[dc * 128:(dc + 1) * 128, th * 1024:(th + 1) * 1024], writes=["xr%d" % b])
            loadB(0)
            for j in range(32):
                th, dc = j // 16, j % 16
                b = j % 2
                if dc == 0:
                    for c0 in range(0, 44, 4):
                        P.dma(gTs[:, c0:c0 + 4, :], gv[:, c0:c0 + 4, th * 1024:(th + 1) * 1024], writes=["gTs"])
                if j + 1 < 32:
                    loadB(j + 1)
                for tb in range(2):
                    ts = slice(tb * 512, (tb + 1) * 512)
                    P.op("tensor", MM([(ps[b][:, ts], wbf[b][:, c, :], gTs[:, c, ts], c == 0, c == 43) for c in range(44)]),
                         reads=["w%d" % b, "gTs"], writes=["ps%d" % b])
                xs = self.xres[dc * 128:(dc + 1) * 128, th * 1024:(th + 1) * 1024]
                P.op("vector", TT(xr[b][:], xr[b][:], ps[b][:], ALU.add), reads=["xr%d" % b, "ps%d" % b], writes=["xr%d" % b])
                P.dma(xs, xr[b][:], reads=["xr%d" % b])
            P.flush()

    def ffn_A_old(self, l):
        P = self.P
        with ExitStack() as st:
            hT = self.sb(st, "hT", [128, 16, T], BF16)
            wst = [self.sb(st, "wst", [128, 16, 256], F32) for _ in range(2)]
            wbf = [self.sb(st, "wbf", [128, 16, 256], BF16) for _ in range(2)]
            cv = self.sb(st, "cv", [128, T], F32)
            sl = self.sb(st, "sl", [128, T], F32)
            go = [self.sb(st, "go", [128, T], BF16) for _ in range(2)]
            cp = self.sb(st, "cp", [128, 4, 4, 44], F32)
            psA = self.pp(st, "psA", [128, T], F32)
            psB = self.pp(st, "psB", [128, T], F32)
            P.dma(cp[:], self.convp, writes=["cp"])
            self.load_hT(hT, self.hbuf)
            for fc in range(44):
                b = fc % 2
                wk = "w%d" % b
                self.load_w(wst[b], wbf[b], wk, self.w_up[l][:, fc * 128:(fc + 1) * 128], 16, 128, col0=0, ndma=2)
                self.load_w(wst[b], wbf[b], wk + "u", self.w_up[l][:, DFF + fc * 128:DFF + (fc + 1) * 128], 16, 128, col0=128, ndma=2)
                for tb in range(4):
                    ts = slice(tb * 512, (tb + 1) * 512)
                    P.op("tensor", MM([(psA[:, ts], wbf[b][:, c, 0:128], hT[:, c, ts], c == 0, c == 15) for c in range(16)]),
                         reads=[wk, "hT"], writes=["psA"])
                for tb in range(4):
                    ts = slice(tb * 512, (tb + 1) * 512)
                    P.op("tensor", MM([(psB[:, ts], wbf[b][:, c, 128:256], hT[:, c, ts], c == 0, c == 15) for c in range(16)]),
                         reads=[wk + "u", "hT"], writes=["psB"])
                P.op("vector", TS(cv[:], psA[:], cp[:, l, 2, fc:fc + 1], cp[:, l, 3, fc:fc + 1], ALU.mult, ALU.add),
                     reads=["psA", "cp"], writes=["cv"])
                P.op("vector", STT(cv[:, 1:T], psA[:, 0:T - 1], cp[:, l, 1, fc:fc + 1], cv[:, 1:T], ALU.mult, ALU.add),
                     reads=["psA", "cp", "cv"], writes=["cv"])
                P.op("vector", STT(cv[:, 2:T], psA[:, 0:T - 2], cp[:, l, 0, fc:fc + 1], cv[:, 2:T], ALU.mult, ALU.add),
                     reads=["psA", "cp", "cv"], writes=["cv"])
                P.op("scalar", ACT(sl[:], cv[:], AF.Silu), reads=["cv"], writes=["sl"])
                gk = "go%d" % b
                P.op("vector", TT(go[b][:], sl[:], psB[:], ALU.mult), reads=["sl", "psB"], writes=[gk])
                P.dma(self.gT[fc * 128:(fc + 1) * 128, :], go[b][:], reads=[gk])
            P.flush()
    def ffn_A_v3(self, l):
        P = self.P
        with ExitStack() as st:
            hT = self.sb(st, "hT", [128, 16, T], BF16)
            wst = [self.sb(st, "wst", [128, 16, 256], F32) for _ in range(2)]
            wbf = [self.sb(st, "wbf", [128, 16, 256], BF16) for _ in range(2)]
            cv = self.sb(st, "cv", [128, T], F32)
            sl = self.sb(st, "sl", [128, T], F32)
            go = [self.sb(st, "go", [128, T], BF16) for _ in range(2)]
            cp = self.sb(st, "cp", [128, 4, 4, 44], F32)
            psA = self.pp(st, "psA", [128, T], F32)
            psB = self.pp(st, "psB", [128, T], F32)
            P.dma(cp[:], self.convp, writes=["cp"])
            self.load_hT(hT, self.hbuf)
            def loadW(fc):
                b = fc % 2
                self.load_w(wst[b], wbf[b], "w%d" % b, self.w_up[l][:, fc * 128:(fc + 1) * 128], 16, 128, col0=0, ndma=2)
                self.load_w(wst[b], wbf[b], "w%du" % b, self.w_up[l][:, DFF + fc * 128:DFF + (fc + 1) * 128], 16, 128, col0=128, ndma=2)
            loadW(0)
            for fc in range(44):
                b = fc % 2
                wk = "w%d" % b
                if fc + 1 < 44:
                    loadW(fc + 1)
                for tb in range(4):
                    ts = slice(tb * 512, (tb + 1) * 512)
                    P.op("tensor", MM([(psA[:, ts], wbf[b][:, c, 0:128], hT[:, c, ts], c == 0, c == 15) for c in range(16)]),
                         reads=[wk, "hT"], writes=["psA"])
                for tb in range(4):
                    ts = slice(tb * 512, (tb + 1) * 512)
                    P.op("tensor", MM([(psB[:, ts], wbf[b][:, c, 128:256], hT[:, c, ts], c == 0, c == 15) for c in range(16)]),
                         reads=[wk + "u", "hT"], writes=["psB"])
                P.op("vector", TS(cv[:], psA[:], cp[:, l, 2, fc:fc + 1], cp[:, l, 3, fc:fc + 1], ALU.mult, ALU.add),
                     reads=["psA", "cp"], writes=["cv"])
                P.op("vector", STT(cv[:, 1:T], psA[:, 0:T - 1], cp[:, l, 1, fc:fc + 1], cv[:, 1:T], ALU.mult, ALU.add),
                     reads=["psA", "cp", "cv"], writes=["cv"])
                P.op("vector", STT(cv[:, 2:T], psA[:, 0:T - 2], cp[:, l, 0, fc:fc + 1], cv[:, 2:T], ALU.mult, ALU.add),
                     reads=["psA", "cp", "cv"], writes=["cv"])
                P.op("scalar", ACT(sl[:], cv[:], AF.Silu), reads=["cv"], writes=["sl"])
                gk = "go%d" % b
                P.op("vector", TT(go[b][:], sl[:], psB[:], ALU.mult), reads=["sl", "psB"], writes=[gk])
                P.dma(self.gT[fc * 128:(fc + 1) * 128, :], go[b][:], reads=[gk])
            P.flush()
    def ffn_B_old(self, l):
        P = self.P
        with ExitStack() as st:
            gTs = self.sb(st, "gTs", [128, 44, 1024], BF16)
            wst = [self.sb(st, "wst", [128, 44, 128], F32) for _ in range(2)]
            wbf = [self.sb(st, "wbf", [128, 44, 128], BF16) for _ in range(2)]
            xr = [self.sb(st, "xr", [128, 1024], F32) for _ in range(2)]
            ps = [self.pp(st, "ps", [128, 1024], F32) for _ in range(2)]
            gv = self.gT.rearrange("(c p) t -> p c t", p=128)
            cnt = 0
            for th in range(2):
                for c0 in range(0, 44, 4):
                    P.dma(gTs[:, c0:c0 + 4, :], gv[:, c0:c0 + 4, th * 1024:(th + 1) * 1024], writes=["gTs"])
                for dc in range(16):
                    b = cnt % 2
                    cnt += 1
                    wk = "w%d" % b
                    self.load_w(wst[b], wbf[b], wk, self.w_down[l][:, dc * 128:(dc + 1) * 128], 44, 128, ndma=8)
                    for tb in range(2):
                        ts = slice(tb * 512, (tb + 1) * 512)
                        P.op("tensor", MM([(ps[b][:, ts], wbf[b][:, c, :], gTs[:, c, ts], c == 0, c == 43) for c in range(44)]),
                             reads=[wk, "gTs"], writes=["ps%d" % b])
                    xs = self.xres[dc * 128:(dc + 1) * 128, th * 1024:(th + 1) * 1024]
                    P.dma(xr[b][:], xs, writes=["xr%d" % b])
                    P.op("vector", TT(xr[b][:], xr[b][:], ps[b][:], ALU.add), reads=["xr%d" % b, "ps%d" % b], writes=["xr%d" % b])
                    P.dma(xs, xr[b][:], reads=["xr%d" % b])
            P.flush()

    def proj_residual(self, src, w_ap):
        P = self.P
        with ExitStack() as st:
            hT = self.sb(st, "hT", [128, 16, T], BF16)
            wst = [self.sb(st, "wst", [128, 16, 256], F32) for _ in range(2)]
            wbf = [self.sb(st, "wbf", [128, 16, 256], BF16) for _ in range(2)]
            xr = [self.sb(st, "xr", [128, T], F32) for _ in range(2)]
            ps = [self.pp(st, "ps", [128, T], F32) for _ in range(2)]
            self.load_hT(hT, src)

            def loadW(cb):
                b = cb % 2
                self.load_w(wst[b], wbf[b], "w%d" % b, w_ap[:, cb * 256:(cb + 1) * 256], 16, 256)

            def loadX(dc):
                P.dma(xr[dc % 2][:], self.xres[dc * 128:(dc + 1) * 128, :], writes=["xr%d" % (dc % 2)])
            loadW(0)
            loadX(0)
            for cb in range(8):
                b = cb % 2
                wk = "w%d" % b
                if cb + 1 < 8:
                    loadW(cb + 1)
                for mc in range(2):
                    dc = cb * 2 + mc
                    pb = dc % 2
                    if dc + 1 < 16:
                        loadX(dc + 1)
                    for tb in range(4):
                        ts = slice(tb * 512, (tb + 1) * 512)
                        P.op("tensor", MM([(ps[pb][:, ts], wbf[b][:, c, mc * 128:(mc + 1) * 128], hT[:, c, ts], c == 0, c == 15)
                                           for c in range(16)]), reads=[wk, "hT"], writes=["ps%d" % pb])
                    xs = self.xres[dc * 128:(dc + 1) * 128, :]
                    P.op("vector", TT(xr[pb][:], xr[pb][:], ps[pb][:], ALU.add), reads=["xr%d" % pb, "ps%d" % pb], writes=["xr%d" % pb])
                    P.dma(xs, xr[pb][:], reads=["xr%d" % pb])
            P.flush()

    def even_proj(self, i):
        P = self.P
        win = self.w_in[i]
        with ExitStack() as st:
            hT = self.sb(st, "hT", [128, 16, T], BF16)
            wst = [self.sb(st, "wst", [128, 16, 256], F32) for _ in range(2)]
            wbf = [self.sb(st, "wbf", [128, 16, 256], BF16) for _ in range(2)]
            rc = self.sb(st, "ropec", [128, T], F32)
            rs = self.sb(st, "ropes", [128, T], F32)
            t1 = self.sb(st, "t1", [128, T], F32)
            t2 = self.sb(st, "t2", [128, T], F32)
            ob = [self.sb(st, "ob", [128, T], BF16) for _ in range(2)]
            lng = self.sb(st, "lng", [128, 2, 1024], F32)
            st8 = self.sb(st, "st8", [128, 8, 2], F32)
            st9 = self.sb(st, "st9", [128, 8, 2], F32)
            eps = self.sb(st, "eps", [128, 1], F32)
            psA = self.pp(st, "psA", [128, T], F32)
            psB = self.pp(st, "psB", [128, T], F32)
            P.dma(rc[:], self.rope[0], writes=["rc"])
            P.dma(rs[:], self.rope[1], writes=["rs"])
            for a in range(2):
                P.dma(lng[:, a, :], self.sgu_ln[i, a, :].partition_broadcast(128), writes=["lng"])
            P.op("vector", MSET(eps[:], 1e-5), writes=["eps"])
            self.load_hT(hT, self.hbuf)
            jobs = []
            pss = [psA, psB]
            state = {"pcnt": 0}

            def mk_qk(c_main, c_perm, dst, j):
                def load(b):
                    self.load_w(wst[b], wbf[b], "w%d" % b, win[:, c_main + j * 128:c_main + (j + 1) * 128], 16, 128, col0=0, ndma=2)
                    self.load_w(wst[b], wbf[b], "w%du" % b, win[:, c_perm + j * 128:c_perm + (j + 1) * 128], 16, 128, col0=128, ndma=2)

                def comp(b):
                    wk = "w%d" % b
                    for tb in range(4):
                        ts = slice(tb * 512, (tb + 1) * 512)
                        P.op("tensor", MM([(psA[:, ts], wbf[b][:, c, 0:128], hT[:, c, ts], c == 0, c == 15) for c in range(16)]),
                             reads=[wk, "hT"], writes=["psA"])
                    for tb in range(4):
                        ts = slice(tb * 512, (tb + 1) * 512)
                        P.op("tensor", MM([(psB[:, ts], wbf[b][:, c, 128:256], hT[:, c, ts], c == 0, c == 15) for c in range(16)]),
                             reads=[wk + "u", "hT"], writes=["psB"])
                    P.op("vector", TT(t1[:], psA[:], rc[:], ALU.mult), reads=["psA", "rc"], writes=["t1"])
                    P.op("vector", TT(t2[:], psB[:], rs[:], ALU.mult), reads=["psB", "rs"], writes=["t2"])
                    P.op("gpsimd", TT(ob[b][:], t1[:], t2[:], ALU.add), reads=["t1", "t2"], writes=["ob%d" % b])
                    P.dma(dst[j * 128:(j + 1) * 128, :], ob[b][:], reads=["ob%d" % b])
                return load, comp

            def mk_u(j):
                def load(b):
                    self.load_w(wst[b], wbf[b], "w%d" % b, win[:, 3072 + j * 128:3072 + (j + 1) * 128], 16, 128, col0=0, ndma=2)

                def comp(b):
                    wk = "w%d" % b
                    ps_ = pss[j % 2]
                    pk = "psA" if j % 2 == 0 else "psB"
                    for tb in range(4):
                        ts = slice(tb * 512, (tb + 1) * 512)
                        P.op("tensor", MM([(ps_[:, ts], wbf[b][:, c, 0:128], hT[:, c, ts], c == 0, c == 15) for c in range(16)]),
                             reads=[wk, "hT"], writes=[pk])
                    P.op("scalar", ACT(ob[b][:], ps_[:], AF.Gelu), reads=[pk], writes=["ob%d" % b])
                    P.dma(self.uT[j * 128:(j + 1) * 128, :], ob[b][:], reads=["ob%d" % b])
                return load, comp

            def mk_tm(which, c0, dst, mb):
                dv = dst.rearrange("(i p) m -> p i m", p=128)

                def load(b):
                    self.load_w(wst[b], wbf[b], "w%d" % b, win[:, c0 + mb * 256:c0 + (mb + 1) * 256], 16, 256)

                def comp(b):
                    wk = "w%d" % b
                    for half in range(2):
                        pcnt = state["pcnt"]
                        ps_ = pss[pcnt % 2]
                        pk = "psA" if pcnt % 2 == 0 else "psB"
                        state["pcnt"] = pcnt + 1
                        for i8 in range(8):
                            tt = half * 8 + i8
                            P.op("tensor", MM([(ps_[:, i8 * 256:(i8 + 1) * 256], hT[:, c, tt * 128:(tt + 1) * 128], wbf[b][:, c, :], c == 0, c == 15)
                                               for c in range(16)]), reads=[wk, "hT"], writes=[pk])
                        o_ = ob[pcnt % 2]
                        ok_ = "ob%d" % (pcnt % 2)
                        if which == 0:
                            P.op("scalar", ACP(o_[:], ps_[:]), reads=[pk], writes=[ok_])
                        else:
                            P.op("scalar", ACT(t1[:], ps_[:], AF.Gelu), reads=[pk], writes=["t1"])
                            v4 = t1[:].rearrange("p (a g d) -> p a g d", a=8, g=2)
                            w4 = t2[:].rearrange("p (a g d) -> p a g d", a=8, g=2)
                            P.op("vector", RSUM(st8[:], v4), reads=["t1"], writes=["st8"])
                            P.op("vector", TS(st8[:], st8[:], 1.0 / 128.0), reads=["st8"], writes=["st8"])
                            P.op("vector", TT(w4, v4, st8[:].unsqueeze(3).to_broadcast([128, 8, 2, 128]), ALU.subtract),
                                 reads=["t1", "st8"], writes=["t2"])
                            P.op("gpsimd", TT(t1[:], t2[:], t2[:], ALU.mult), reads=["t2"], writes=["t1"])
                            P.op("vector", RSUM(st9[:], v4), reads=["t1"], writes=["st9"])
                            P.op("scalar", ACT(st9[:], st9[:], AF.Sqrt, bias=eps[:], scale=1.0 / 128.0), reads=["st9", "eps"], writes=["st9"])
                            P.op("vector", RECIP(st9[:], st9[:]), reads=["st9"], writes=["st9"])
                            P.op("vector", TT(w4, w4, st9[:].unsqueeze(3).to_broadcast([128, 8, 2, 128]), ALU.mult),
                                 reads=["t2", "st9"], writes=["t2"])
                            w3 = t2[:].rearrange("p (a m) -> p a m", a=8)
                            gsl = lng[:, 0, mb * 256:(mb + 1) * 256].unsqueeze(1).to_broadcast([128, 8, 256])
                            bsl = lng[:, 1, mb * 256:(mb + 1) * 256].unsqueeze(1).to_broadcast([128, 8, 256])
                            P.op("vector", TT(w3, w3, gsl, ALU.mult), reads=["t2", "lng"], writes=["t2"])
                            P.op("vector", TT(o_[:].rearrange("p (a m) -> p a m", a=8), w3, bsl, ALU.add), reads=["t2", "lng"], writes=[ok_])
                        P.dma(dv[:, half * 8:(half + 1) * 8, mb * 256:(mb + 1) * 256], o_[:].rearrange("p (a m) -> p a m", a=8), reads=[ok_])
                return load, comp

            for (c_main, c_perm, dst) in ((0, 5120, self.qT), (1024, 6144, self.kT)):
                for j in range(8):
                    jobs.append(mk_qk(c_main, c_perm, dst, j))
            for j in range(8):
                jobs.append(mk_u(j))
            for which, (c0, dst) in enumerate(((2048, self.vtm), (4096, self.vn))):
                for mb in range(4):
                    jobs.append(mk_tm(which, c0, dst, mb))
            jobs[0][0](0)
            for n, (ld, cmp_) in enumerate(jobs):
                if n + 1 < len(jobs):
                    jobs[n + 1][0]((n + 1) % 2)
                cmp_(n % 2)
            P.flush()

    def even_attn(self, i):
        P = self.P
        with ExitStack() as st:
            V = self.sb(st, "V", [128, 16, 1024], BF16)
            att = self.sb(st, "att", [128, 16, 1024], BF16)
            cst = self.sb(st, "cst", [128, 8, 128], F32)
            ident = self.sb(st, "ident", [128, 128], BF16)
            keep = self.sb(st, "keep", [128, 16, 16], F32)
            gmask = self.sb(st, "gmask", [128, 16, 8], F32)
            qh = [self.sb(st, "qh", [64, T], BF16) for _ in range(2)]
            kh = [self.sb(st, "kh", [64, T], BF16) for _ in range(2)]
            km = self.sb(st, "km", [64, 8], F32)
            kmb = self.sb(st, "kmb", [64, 8], BF16)
            gate = self.sb(st, "gate", [128, 16, 8], F32)
            m8 = self.sb(st, "m8", [128, 16, 8], F32)
            bias8 = self.sb(st, "bias8", [128, 16, 8], F32)
            bias16 = self.sb(st, "bias16", [128, 16, 16], F32)
            sc = [self.sb(st, "sc", [128, T], F32) for _ in range(2)]
            pb = [self.sb(st, "pb", [128, T], BF16) for _ in range(2)]
            pT = [self.sb(st, "pT", [128, 16, 128], BF16) for _ in range(2)]
            mx = [self.sb(st, "mx", [128, 4], F32) for _ in range(3)]
            psS = self.pp(st, "psS", [128, T], F32)
            psT = self.pp(st, "psT", [128, T], BF16)
            psG = self.pp(st, "psG", [128, 512], F32)
            psO = self.pp(st, "psO", [128, 512], F32)
            P.dma(V[:], self.vtm.rearrange("(i p) m -> p i m", p=128), writes=["V"])
            P.dma(cst[:], self.consts, writes=["cst"])
            P.dma(keep[:], self.keep, writes=["keep"])
            P.dma(gmask[:], self.gmask, writes=["gmask"])
            P.op("vector", CP(ident[:], cst[:, 0, :]), reads=["cst"], writes=["ident"])
            causal = cst[:, 4, :]

            def load_qk(h):
                b = h % 2
                P.dma(qh[b][:], self.qT[h * 64:(h + 1) * 64, :], writes=["qh%d" % b])
                P.dma(kh[b][:], self.kT[h * 64:(h + 1) * 64, :], writes=["kh%d" % b])

            def S1(h, qt):
                b = h % 2
                par = qt % 2
                qk, kk_ = "qh%d" % b, "kh%d" % b
                sc_, mx_ = sc[par], mx[qt % 3]
                sk, mk = "sc%d" % par, "mx%d" % (qt % 3)
                nk = (qt + 1) * 128
                q_sl = qh[b][:, qt * 128:(qt + 1) * 128]
                mms = []
                for n0 in range(0, nk, 512):
                    n1 = min(nk, n0 + 512)
                    mms.append((psS[:, n0:n1], q_sl, kh[b][:, n0:n1], True, True))
                P.op("tensor", MM(mms), reads=[qk, kk_], writes=["psS"])
                if qt > 0:
                    P.op("vector", STT(sc_[:, 0:qt * 128].rearrange("p (a k) -> p a k", k=128),
                                       psS[:, 0:qt * 128].rearrange("p (a k) -> p a k", k=128), 0.125,
                                       bias16[:, qt, 0:qt].unsqueeze(2).to_broadcast([128, qt, 128]), ALU.mult, ALU.add),
                         reads=["psS", "bias16"], writes=[sk])
                P.op("vector", STT(sc_[:, qt * 128:nk], psS[:, qt * 128:nk], 0.125, causal, ALU.mult, ALU.add),
                     reads=["psS", "cst"], writes=[sk])
                P.op("vector", RMAX(mx_[:, 0:1], sc_[:, 0:nk]), reads=[sk], writes=[mk])
                P.op("vector", TS(mx_[:, 1:2], mx_[:, 0:1], -1.0), reads=[mk], writes=[mk])

            def S2(h, qt):
                par = qt % 2
                sc_, pb_, mx_ = sc[par], pb[par], mx[qt % 3]
                sk, pk, mk = "sc%d" % par, "pb%d" % par, "mx%d" % (qt % 3)
                nk = (qt + 1) * 128
                P.op("scalar", ACT(pb_[:, 0:nk], sc_[:, 0:nk], AF.Exp, bias=mx_[:, 1:2], accum=mx_[:, 2:3]), reads=[sk, mk], writes=[pk, mk])
                P.op("vector", RECIP(mx_[:, 3:4], mx_[:, 2:3]), reads=[mk], writes=[mk])
                P.op("tensor", TRS([(psT[:, kt * 128:(kt + 1) * 128], pb_[:, kt * 128:(kt + 1) * 128], ident[:]) for kt in range(qt + 1)]),
                     reads=[pk, "ident"], writes=["psT"])
                P.op("scalar", ACP(pT[par][:, 0:qt + 1, :], psT[:, 0:nk].rearrange("p (a k) -> p a k", k=128)), reads=["psT"], writes=["pT%d" % par])

            def S3(h, qt):
                par = qt % 2
                mx_ = mx[qt % 3]
                mk = "mx%d" % (qt % 3)
                P.op("tensor", MM([(psO[:, 0:64], pT[par][:, kt, :], V[:, kt, h * 64:(h + 1) * 64], kt == 0, kt == qt) for kt in range(qt + 1)]),
                     reads=["pT%d" % par, "V"], writes=["psO"])
                P.op("vector", TS(att[:, qt, h * 64:(h + 1) * 64], psO[:, 0:64], mx_[:, 3:4]), reads=["psO", mk], writes=["att"])

            load_qk(0)
            par = 0
            for h in range(16):
                b = h % 2
                qk, kk_ = "qh%d" % b, "kh%d" % b
                if h + 1 < 16:
                    load_qk(h + 1)
                P.op("vector", RSUM(km[:], kh[b][:].rearrange("p (n k) -> p n k", k=256)), reads=[kk_], writes=["km"])
                P.op("vector", TS(kmb[:], km[:], 1.0 / 256.0), reads=["km"], writes=["kmb"])
                P.op("tensor", MM([(psG[:, qt * 8:(qt + 1) * 8], qh[b][:, qt * 128:(qt + 1) * 128], kmb[:], True, True) for qt in range(16)]),
                     reads=[qk, "kmb"], writes=["psG"])
                P.op("vector", TT(gate[:], psG[:, 0:128].rearrange("p (a n) -> p a n", n=8), gmask[:], ALU.add),
                     reads=["psG", "gmask"], writes=["gate"])
                for qt in range(16):
                    P.op("vector", MAX8(m8[:, qt, :], gate[:, qt, :]), reads=["gate"], writes=["m8"])
                P.op("vector", TT(bias8[:], gate[:], m8[:, :, 2:3].to_broadcast([128, 16, 8]), ALU.is_ge), reads=["gate", "m8"], writes=["bias8"])
                P.op("vector", TS(bias8[:], bias8[:], -1.0, -NEG, ALU.add, ALU.mult), reads=["bias8"], writes=["bias8"])
                P.op("vector", CP(bias16[:].rearrange("p a (n two) -> p a n two", two=2), bias8[:].unsqueeze(3).to_broadcast([128, 16, 8, 2])),
                     reads=["bias8"], writes=["bias16"])
                P.op("vector", TT(bias16[:], bias16[:], keep[:], ALU.mult), reads=["bias16", "keep"], writes=["bias16"])
                S1(h, 0)
                S1(h, 1)
                S2(h, 0)
                for qt in range(16):
                    if qt + 2 < 16:
                        S1(h, qt + 2)
                    if qt + 1 < 16:
                        S2(h, qt + 1)
                    S3(h, qt)
            yv = self.ycat.rearrange("(c p) t -> p c t", p=128)
            aT = [self.sb(st, "aT", [128, T], BF16) for _ in range(2)]
            for c in range(8):
                P.op("tensor", TRS([(psT[:, qt * 128:(qt + 1) * 128], att[:, qt, c * 128:(c + 1) * 128], ident[:]) for qt in range(16)]),
                     reads=["att", "ident"], writes=["psT"])
                P.op("scalar", ACP(aT[c % 2][:], psT[:]), reads=["psT"], writes=["aT%d" % (c % 2)])
                P.dma(yv[:, c, :], aT[c % 2][:], reads=["aT%d" % (c % 2)])
            P.flush()

    def even_sgu(self, i):
        P = self.P
        with ExitStack() as st:
            vn = self.sb(st, "vn", [128, 16, 1024], BF16)
            uT = self.sb(st, "uT", [128, 8, T], BF16)
            wsf = self.sb(st, "wsf", [128, 8, 128], F32)
            wsb = self.sb(st, "wsb", [128, 8, 128], BF16)
            cst = self.sb(st, "cst", [128, 8, 128], F32)
            bsb = self.sb(st, "bsb", [128, 8, 128], F32)
            tmp = self.sb(st, "tmp", [128, T], F32)
            ob = [self.sb(st, "ob", [128, T], BF16) for _ in range(2)]
            ps = [self.pp(st, "ps", [128, T], F32) for _ in range(2)]
            P.dma(vn[:], self.vn.rearrange("(i p) m -> p i m", p=128), writes=["vn"])
            P.dma(uT[:], self.uT.rearrange("(c p) t -> p c t", p=128), writes=["uT"])
            P.dma(wsf[:], self.sgu_wT[i].rearrange("g s t -> s g t"), writes=["wsf"])
            P.dma(cst[:], self.consts, writes=["cst"])
            P.dma(bsb[:].rearrange("p g t -> p (g t)"), self.sgu_b[i, :].partition_broadcast(128), writes=["bsb"])
            P.op("vector", TT(wsb[:], wsf[:], cst[:, 3, :].unsqueeze(1).to_broadcast([128, 8, 128]), ALU.mult),
                 reads=["wsf", "cst"], writes=["wsb"])
            yv = self.ycat.rearrange("(c p) t -> p c t", p=128)
            for g in range(8):
                b = g % 2
                P.op("tensor", MM([(ps[b][:, c * 128:(c + 1) * 128], vn[:, c, g * 128:(g + 1) * 128], wsb[:, g, :], True, True) for c in range(16)]),
                     reads=["vn", "wsb"], writes=["ps%d" % b])
                P.op("vector", TT(tmp[:].rearrange("p (c t) -> p c t", t=128), ps[b][:].rearrange("p (c t) -> p c t", t=128),
                                  bsb[:, g, :].unsqueeze(1).to_broadcast([128, 16, 128]), ALU.add), reads=["ps%d" % b, "bsb"], writes=["tmp"])
                P.op("vector", TT(ob[b][:], tmp[:], uT[:, g, :], ALU.mult), reads=["tmp", "uT"], writes=["ob%d" % b])
                P.dma(yv[:, 8 + g, :], ob[b][:], reads=["ob%d" % b])
            P.flush()

    def lin_tm(self, src, w_ap, dst, KC=16):
        P = self.P
        with ExitStack() as st:
            hT = self.sb(st, "hT", [128, 16, T], BF16)
            wst = [self.sb(st, "wst", [128, 16, 256], F32) for _ in range(2)]
            wbf = [self.sb(st, "wbf", [128, 16, 256], BF16) for _ in range(2)]
            ob = [self.sb(st, "ob", [128, T], F32) for _ in range(2)]
            ps = [self.pp(st, "ps", [128, T], F32) for _ in range(2)]
            self.load_hT(hT, src)
            dv = dst.rearrange("(i p) m -> p i m", p=128)
            pcnt = 0
            self.load_w(wst[0], wbf[0], "w0", w_ap[:, 0:256], 16, 256)
            for mb in range(8):
                b = mb % 2
                wk = "w%d" % b
                if mb + 1 < 8:
                    self.load_w(wst[1 - b], wbf[1 - b], "w%d" % (1 - b), w_ap[:, (mb + 1) * 256:(mb + 2) * 256], 16, 256)
                for half in range(2):
                    pbk = pcnt % 2
                    pcnt += 1
                    for i8 in range(8):
                        tt = half * 8 + i8
                        P.op("tensor", MM([(ps[pbk][:, i8 * 256:(i8 + 1) * 256], hT[:, c, tt * 128:(tt + 1) * 128], wbf[b][:, c, :], c == 0, c == 15)
                                           for c in range(16)]), reads=[wk, "hT"], writes=["ps%d" % pbk])
                    eng = "scalar" if pbk == 0 else "vector"
                    P.op(eng, (ACP if eng == "scalar" else CP)(ob[pbk][:], ps[pbk][:]), reads=["ps%d" % pbk], writes=["ob%d" % pbk])
                    P.dma(dv[:, half * 8:(half + 1) * 8, mb * 256:(mb + 1) * 256], ob[pbk][:].rearrange("p (a m) -> p a m", a=8), reads=["ob%d" % pbk])
            P.flush()

    def lora_tm(self, src, w1_ap, R, func, w2_ap, dst):
        P = self.P
        RC = (R + 127) // 128
        rows = [min(128, R - rc * 128) for rc in range(RC)]
        with ExitStack() as st:
            hT = self.sb(st, "hT", [128, 16, T], BF16)
            w1s = self.sb(st, "w1s", [128, 16, R], F32)
            w1b = self.sb(st, "w1b", [128, 16, R], BF16)
            w2s = self.sb(st, "w2s", [128, RC, D], F32)
            w2b = self.sb(st, "w2b", [128, RC, D], BF16)
            lT = self.sb(st, "lT", [128, RC, T], BF16)
            ob = [self.sb(st, "ob", [128, T], F32) for _ in range(2)]
            ps = [self.pp(st, "ps", [128, T], F32) for _ in range(2)]
            self.load_hT(hT, src)
            self.load_w(w1s, w1b, "w1", w1_ap, 16, R)
            for rc in range(RC):
                P.dma(w2s[0:rows[rc], rc, :], w2_ap[rc * 128:rc * 128 + rows[rc], :], writes=["w2s"])
                P.op("gpsimd", CP(w2b[0:rows[rc], rc, :], w2s[0:rows[rc], rc, :]), reads=["w2s"], writes=["w2b"])
            for rc in range(RC):
                pbk = rc % 2
                for tb in range(4):
                    ts = slice(tb * 512, (tb + 1) * 512)
                    P.op("tensor", MM([(ps[pbk][0:rows[rc], ts], w1b[:, c, rc * 128:rc * 128 + rows[rc]], hT[:, c, ts], c == 0, c == 15)
                                       for c in range(16)]), reads=["w1", "hT"], writes=["ps%d" % pbk])
                P.op("scalar", ACT(lT[0:rows[rc], rc, :], ps[pbk][0:rows[rc], :], func), reads=["ps%d" % pbk], writes=["lT"])
            dv = dst.rearrange("(i p) m -> p i m", p=128)
            pcnt = 0
            for mb in range(8):
                for half in range(2):
                    pbk = pcnt % 2
                    pcnt += 1
                    for i8 in range(8):
                        tt = half * 8 + i8
                        P.op("tensor", MM([(ps[pbk][:, i8 * 256:(i8 + 1) * 256], lT[0:rows[rc], rc, tt * 128:(tt + 1) * 128],
                                            w2b[0:rows[rc], rc, mb * 256:(mb + 1) * 256], rc == 0, rc == RC - 1) for rc in range(RC)]),
                             reads=["lT", "w2b"], writes=["ps%d" % pbk])
                    eng = "scalar" if pbk == 0 else "vector"
                    P.op(eng, (ACP if eng == "scalar" else CP)(ob[pbk][:], ps[pbk][:]), reads=["ps%d" % pbk], writes=["ob%d" % pbk])
                    P.dma(dv[:, half * 8:(half + 1) * 8, mb * 256:(mb + 1) * 256], ob[pbk][:].rearrange("p (a m) -> p a m", a=8), reads=["ob%d" % pbk])
            P.flush()

    def rwkv_scan(self, i, use_vres):
        P = self.P
        W = 512
        vsrc = self.v_tm
        with ExitStack() as st:
            cst = self.sb(st, "cst", [128, 8, 128], F32)
            ident = self.sb(st, "ident", [128, 128], BF16)
            mL = self.sb(st, "mL", [128, 128], BF16)
            mU = self.sb(st, "mU", [128, 128], BF16)
            mUi = self.sb(st, "mUi", [128, 128], BF16)
            negcol = self.sb(st, "negcol", [128, 1], F32)
            eps = self.sb(st, "eps", [128, 1], F32)
            P.dma(cst[:], self.consts, writes=["cst"])
            P.op("vector", CP(ident[:], cst[:, 0, :]), reads=["cst"], writes=["ident"])
            P.op("vector", CP(mL[:], cst[:, 1, :]), reads=["cst"], writes=["mL"])
            P.op("vector", CP(mU[:], cst[:, 2, :]), reads=["cst"], writes=["mU"])
            P.op("vector", CP(mUi[:], cst[:, 3, :]), reads=["cst"], writes=["mUi"])
            P.op("vector", MSET(negcol[:], -EXPM05), writes=["negcol"])
            P.op("vector", MSET(eps[:], 64e-5), writes=["eps"])
            triN = cst[:, 5, :]
            allN = cst[:, 6, :]
            ogv = self.ycat.rearrange("(c p) t -> p c t", p=128)

            def h3(ap):
                return ap.rearrange("p (h d) -> p h d", d=64)

            def bc8(ap, n=64):
                return ap.unsqueeze(2).to_broadcast([128, 8, n])

            def mb8(m):
                return m[:].unsqueeze(1).to_broadcast([128, 8, 128])

            def v3(ap):
                return ap.rearrange("p (h t) -> p h t", t=128)

            sets = []
            for si in range(2):
                d = {}
                for nm in ("r_t", "k_t", "v_t", "zw", "za", "g_t", "cl", "E1", "E2", "tmp", "tmp2", "kk", "kmod", "b_", "Ut"):
                    d[nm] = self.sb(st, nm, [128, W], F32)
                if use_vres:
                    d["zv"] = self.sb(st, "zv", [128, W], F32)
                    d["vf"] = self.sb(st, "vf", [128, W], F32)
                for nm in ("At", "Bt", "Kt", "Rt", "Bh", "Kh", "Vb", "AkV", "Ub", "og"):
                    d[nm] = self.sb(st, nm, [128, W], BF16)
                for nm in ("ATf", "BTf", "KTf", "RTf", "WTf"):
                    d[nm] = self.sb(st, nm, [64, 8, 128], BF16)
                for nm in ("Mm0", "Mm1", "MT0", "MT1", "TT", "AakT", "ArbT", "ArkT"):
                    d[nm] = self.sb(st, nm, [128, 8, 128], BF16)
                d["bc"] = self.sb(st, "bc", [128, 8, W], F32)
                d["s8"] = self.sb(st, "s8", [128, 8, 4], F32)
                d["PC"] = self.sb(st, "PC", [64, 8], F32)
                d["S"] = self.sb(st, "S", [64, 8, 64], F32)
                d["Sb0"] = self.sb(st, "Sb0", [64, 8, 64], BF16)
                d["Sb1"] = self.sb(st, "Sb1", [64, 8, 64], BF16)
                d["ogT"] = self.sb(st, "ogT", [128, 4, 128], BF16)
                d["psM"] = self.pp(st, "psM", [128, 1024], F32)
                d["psW"] = self.pp(st, "psW", [128, 512], F32)
                d["psT"] = self.pp(st, "psT", [128, 1024], BF16)
                sets.append(d)

            def body(hg, si):
                d = sets[si]
                sfx = "_%d" % si

                def k(*names):
                    return [n + sfx if n not in ("cst", "ident", "mL", "mU", "mUi", "negcol", "eps") else n for n in names]

                def OP(eng, fn, r, w):
                    P.op(eng, fn, reads=k(*r), writes=k(*w))
                r_t, k_t, v_t, zw, za, g_t = d["r_t"], d["k_t"], d["v_t"], d["zw"], d["za"], d["g_t"]
                cl, E1, E2, tmp, tmp2, kk, kmod, b_, Ut = d["cl"], d["E1"], d["E2"], d["tmp"], d["tmp2"], d["kk"], d["kmod"], d["b_"], d["Ut"]
                At, Bt, Kt, Rt, Bh, Kh, Vb, AkV, Ub, og = (d[n] for n in ("At", "Bt", "Kt", "Rt", "Bh", "Kh", "Vb", "AkV", "Ub", "og"))
                ATf, BTf, KTf, RTf, WTf = (d[n] for n in ("ATf", "BTf", "KTf", "RTf", "WTf"))
                Mm = [d["Mm0"], d["Mm1"]]
                MT = [d["MT0"], d["MT1"]]
                TTm, AakT, ArbT, ArkT = d["TT"], d["AakT"], d["ArbT"], d["ArkT"]
                bc, s8, PC, S, ogT = d["bc"], d["s8"], d["PC"], d["S"], d["ogT"]
                Sb = [d["Sb0"], d["Sb1"]]
                psM, psW, psT = d["psM"], d["psW"], d["psT"]
                sg, a_, E3, E4, scr, o_ = zw, za, tmp, tmp2, cl, Ut
                cs = slice(hg * W, (hg + 1) * W)
                for j in range(8):
                    P.dma(bc[:, j, :], self.rv[i, j, cs].partition_broadcast(128), writes=k("bc"))
                OP("vector", MSET(S[:], 0.0), [], ["S"])
                OP("vector", MSET(Sb[0][:], 0.0), [], ["Sb0"])
                yield
                for tt in range(16):
                    rs_ = slice(tt * 128, (tt + 1) * 128)
                    for (tile_, src, key) in ((r_t, self.r_tm, "r_t"), (k_t, self.k_tm, "k_t"), (v_t, vsrc, "v_t"),
                                              (zw, self.zw_tm, "zw"), (za, self.za_tm, "za"), (g_t, self.g_tm, "g_t")):
                        P.dma(tile_[:], src[rs_, cs], writes=k(key))
                    if use_vres:
                        P.dma(d["zv"][:], self.zv_tm[rs_, cs], writes=k("zv"))
                        P.dma(d["vf"][:], self.vf_tm[rs_, cs], writes=k("vf"))
                    yield
                    OP("vector", TT(zw[:], zw[:], bc[:, 0, :], ALU.add), ["zw", "bc"], ["zw"])
                    OP("scalar", ACT(sg[:], zw[:], AF.Sigmoid), ["zw"], ["zw"])
                    yield
                    OP("tensor", MM([(psW[:], triN, sg[:], True, True)]), ["cst", "zw"], ["psW"])
                    OP("scalar", ACT(cl[:], psW[:], AF.Identity), ["psW"], ["cl"])
                    yield
                    OP("tensor", MM([(psW[:], allN, sg[:], True, True)]), ["cst", "zw"], ["psW"])
                    OP("vector", TT(tmp2[:], psW[:], cl[:], ALU.subtract), ["psW", "cl"], ["tmp2"])
                    OP("scalar", ACT(E4[:], tmp2[:], AF.Exp), ["tmp2"], ["tmp2"])
                    yield
                    OP("tensor", MM([(psW[0:64, hh:hh + 1], sg[:, hh * 64:(hh + 1) * 64], negcol[:], True, True) for hh in range(8)]),
                       ["zw", "negcol"], ["psW"])
                    OP("scalar", ACT(PC[:], psW[0:64, 0:8], AF.Exp), ["psW"], ["PC"])
                    yield
                    OP("scalar", ACT(E1[:], cl[:], AF.Exp), ["cl"], ["E1"])
                    OP("scalar", ACT(E2[:], cl[:], AF.Exp, scale=-1.0), ["cl"], ["E2"])
                    OP("vector", STT(tmp[:], sg[:], EXPM05, cl[:], ALU.mult, ALU.add), ["zw", "cl"], ["tmp"])
                    OP("scalar", ACT(E3[:], tmp[:], AF.Exp), ["tmp"], ["tmp"])
                    yield
                    OP("vector", TT(za[:], za[:], bc[:, 1, :], ALU.add), ["za", "bc"], ["za"])
                    OP("scalar", ACT(a_[:], za[:], AF.Sigmoid), ["za"], ["za"])
                    OP("vector", TT(kk[:], k_t[:], bc[:, 2, :], ALU.mult), ["k_t", "bc"], ["kk"])
                    yield
                    OP("gpsimd", TT(scr[:], kk[:], kk[:], ALU.mult), ["kk"], ["cl"])
                    OP("vector", RSUM(s8[:, :, 0], h3(scr[:])), ["cl"], ["s8"])
                    OP("scalar", ACT(s8[:, :, 0], s8[:, :, 0], AF.Sqrt), ["s8"], ["s8"])
                    yield
                    OP("vector", TS(s8[:, :, 0], s8[:, :, 0], 1e-12, None, ALU.max), ["s8"], ["s8"])
                    OP("vector", RECIP(s8[:, :, 0], s8[:, :, 0]), ["s8"], ["s8"])
                    OP("vector", TT(h3(kk[:]), h3(kk[:]), bc8(s8[:, :, 0]), ALU.mult), ["kk", "s8"], ["kk"])
                    yield
                    OP("vector", STT(scr[:], a_[:], -1.0, bc[:, 3, :], ALU.add, ALU.mult), ["za", "bc"], ["cl"])
                    OP("vector", STT(kmod[:], scr[:], 1.0, k_t[:], ALU.add, ALU.mult), ["cl", "k_t"], ["kmod"])
                    OP("gpsimd", TT(b_[:], kk[:], a_[:], ALU.mult), ["kk", "za"], ["b_"])
                    yield
                    if use_vres:
                        zv, vf = d["zv"], d["vf"]
                        OP("vector", TT(zv[:], zv[:], bc[:, 7, :], ALU.add), ["zv", "bc"], ["zv"])
                        OP("scalar", ACT(zv[:], zv[:], AF.Sigmoid), ["zv"], ["zv"])
                        OP("gpsimd", TT(vf[:], vf[:], v_t[:], ALU.subtract), ["vf", "v_t"], ["vf"])
                        yield
                        OP("vector", TT(vf[:], vf[:], zv[:], ALU.mult), ["vf", "zv"], ["vf"])
                        OP("vector", TT(v_t[:], v_t[:], vf[:], ALU.add), ["vf", "v_t"], ["v_t"])
                        yield
                    OP("vector", STT(At[:], kk[:], -1.0, E3[:], ALU.mult, ALU.mult), ["kk", "tmp"], ["At"])
                    OP("gpsimd", TT(Bt[:], b_[:], E2[:], ALU.mult), ["b_", "E2"], ["Bt"])
                    yield
                    OP("gpsimd", TT(Kt[:], kmod[:], E2[:], ALU.mult), ["kmod", "E2"], ["Kt"])
                    OP("gpsimd", TT(Rt[:], r_t[:], E1[:], ALU.mult), ["r_t", "E1"], ["Rt"])
                    yield
                    OP("vector", TT(Bh[:], b_[:], E4[:], ALU.mult), ["b_", "tmp2"], ["Bh"])
                    OP("gpsimd", TT(Kh[:], kmod[:], E4[:], ALU.mult), ["kmod", "tmp2"], ["Kh"])
                    OP("scalar", ACP(Vb[:], v_t[:]), ["v_t"], ["Vb"])
                    yield
                    for (src_, dst_, sk, dk) in ((At, ATf, "At", "ATf"), (Bt, BTf, "Bt", "BTf"), (Kt, KTf, "Kt", "KTf"), (Rt, RTf, "Rt", "RTf")):
                        OP("tensor", TRS([(psT[0:64, hh * 128:(hh + 1) * 128], src_[:, hh * 64:(hh + 1) * 64], ident[:]) for hh in range(8)]),
                           [sk, "ident"], ["psT"])
                        OP("scalar", ACP(dst_[:], v3(psT[0:64, 0:1024])), ["psT"], [dk])
                        yield
                    for (lf, rf, lk, rk_, mask, mk, dst_, dk, eng) in (
                            (ATf, BTf, "ATf", "BTf", mL, "mL", Mm[0], "Mm0", "vector"),
                            (BTf, ATf, "BTf", "ATf", mU, "mU", MT[0], "MT0", "gpsimd"),
                            (KTf, ATf, "KTf", "ATf", mU, "mU", AakT, "AakT", "vector"),
                            (BTf, RTf, "BTf", "RTf", mUi, "mUi", ArbT, "ArbT", "vector"),
                            (KTf, RTf, "KTf", "RTf", mUi, "mUi", ArkT, "ArkT", "vector")):
                        OP("tensor", MM([(psM[:, hh * 128:(hh + 1) * 128], lf[:, hh, :], rf[:, hh, :], True, True) for hh in range(8)]),
                           [lk, rk_], ["psM"])
                        if eng == "gpsimd":
                            OP("scalar", ACP(dst_[:], v3(psM[:])), ["psM"], [dk])
                            OP("gpsimd", TT(dst_[:], dst_[:], mb8(mask), ALU.mult), [dk, mk], [dk])
                        else:
                            OP("vector", TT(dst_[:], v3(psM[:]), mb8(mask), ALU.mult), ["psM", mk], [dk])
                        yield
                    OP("gpsimd", TT(TTm[:], MT[0][:], ident[:].unsqueeze(1).to_broadcast([128, 8, 128]), ALU.add), ["MT0", "ident"], ["TT"])
                    yield
                    cur = 0
                    for lev in range(1, 7):
                        nxt = 1 - cur
                        OP("tensor", MM([(psM[:, hh * 128:(hh + 1) * 128], MT[cur][:, hh, :], Mm[cur][:, hh, :], True, True) for hh in range(8)]),
                           ["MT%d" % cur, "Mm%d" % cur], ["psM"])
                        OP("scalar", ACP(Mm[nxt][:], v3(psM[:])), ["psM"], ["Mm%d" % nxt])
                        yield
                        if lev < 6:
                            OP("tensor", MM([(psM[:, hh * 128:(hh + 1) * 128], Mm[cur][:, hh, :], MT[cur][:, hh, :], True, True) for hh in range(8)]),
                               ["MT%d" % cur, "Mm%d" % cur], ["psM"])
                            OP("scalar", ACP(MT[nxt][:], v3(psM[:])), ["psM"], ["MT%d" % nxt])
                            yield
                        OP("tensor", MM([(psM[:, hh * 128:(hh + 1) * 128], Mm[nxt][:, hh, :], TTm[:, hh, :], True, True) for hh in range(8)]),
                           ["Mm%d" % nxt, "TT"], ["psM"])
                        OP("vector", TT(TTm[:], v3(psM[:]), TTm[:], ALU.add), ["psM", "TT"], ["TT"])
                        yield
                        cur = nxt
                    OP("tensor", MM([(psM[0:64, hh * 128:(hh + 1) * 128], At[:, hh * 64:(hh + 1) * 64], TTm[:, hh, :], True, True) for hh in range(8)]),
                       ["At", "TT"], ["psM"])
                    OP("scalar", ACP(WTf[:], v3(psM[0:64, :])), ["psM"], ["WTf"])
                    yield
                    OP("tensor", MM([(psW[:, hh * 64:(hh + 1) * 64], AakT[:, hh, :], Vb[:, hh * 64:(hh + 1) * 64], True, True) for hh in range(8)]),
                       ["AakT", "Vb"], ["psW"])
                    OP("scalar", ACP(AkV[:], psW[:]), ["psW"], ["AkV"])
                    yield
                    OP("tensor", MM([(psW[:, hh * 64:(hh + 1) * 64], TTm[:, hh, :], AkV[:, hh * 64:(hh + 1) * 64], True, True) for hh in range(8)]),
                       ["TT", "AkV"], ["psW"])
                    OP("scalar", ACP(Ut[:], psW[:]), ["psW"], ["Ut"])
                    yield
                    so = "Sb%d" % (tt % 2)
                    sn = "Sb%d" % ((tt + 1) % 2)
                    Sold = Sb[tt % 2]
                    Snew = Sb[(tt + 1) % 2]
                    OP("tensor", MM([(psW[:, hh * 64:(hh + 1) * 64], WTf[:, hh, :], Sold[:, hh, :], True, True) for hh in range(8)]),
                       ["WTf", so], ["psW"])
                    OP("vector", TT(Ub[:], psW[:], Ut[:], ALU.add), ["psW", "Ut"], ["Ub"])
                    yield
                    mm = []
                    for hh in range(8):
                        hs = slice(hh * 64, (hh + 1) * 64)
                        mm.append((psM[:, hs], RTf[:, hh, :], Sold[:, hh, :], True, False))
                        mm.append((psM[:, hs], ArbT[:, hh, :], Ub[:, hs], False, False))
                        mm.append((psM[:, hs], ArkT[:, hh, :], Vb[:, hs], False, True))
                    OP("tensor", MM(mm), ["RTf", so, "ArbT", "Ub", "ArkT", "Vb"], ["psM"])
                    mm = []
                    for hh in range(8):
                        hs = slice(hh * 64, (hh + 1) * 64)
                        mm.append((psW[0:64, hs], Kh[:, hs], Vb[:, hs], True, False))
                        mm.append((psW[0:64, hs], Bh[:, hs], Ub[:, hs], False, True))
                    OP("tensor", MM(mm), ["Kh", "Vb", "Bh", "Ub"], ["psW"])
                    yield
                    OP("vector", TT(S[:], S[:], PC[:].unsqueeze(2).to_broadcast([64, 8, 64]), ALU.mult), ["S", "PC"], ["S"])
                    OP("vector", TT(S[:], S[:], psW[0:64, :].rearrange("p (h v) -> p h v", v=64), ALU.add), ["S", "psW"], ["S"])
                    OP("scalar", ACP(Snew[:], S[:]), ["S"], [sn])
                    yield
                    OP("scalar", ACP(o_[:], psM[:, 0:512]), ["psM"], ["Ut"])
                    OP("vector", RSUM(s8[:, :, 1], h3(o_[:])), ["Ut"], ["s8"])
                    OP("vector", TS(s8[:, :, 1], s8[:, :, 1], 1.0 / 64.0), ["s8"], ["s8"])
                    yield
                    OP("vector", TT(h3(o_[:]), h3(o_[:]), bc8(s8[:, :, 1]), ALU.subtract), ["Ut", "s8"], ["Ut"])
                    OP("gpsimd", TT(tmp[:], o_[:], o_[:], ALU.mult), ["Ut"], ["tmp"])
                    OP("vector", RSUM(s8[:, :, 2], h3(tmp[:])), ["tmp"], ["s8"])
                    yield
                    OP("scalar", ACT(s8[:, :, 2], s8[:, :, 2], AF.Sqrt, bias=eps[:], scale=1.0 / 64.0), ["s8", "eps"], ["s8"])
                    OP("vector", RECIP(s8[:, :, 2], s8[:, :, 2]), ["s8"], ["s8"])
                    OP("vector", TT(h3(o_[:]), h3(o_[:]), bc8(s8[:, :, 2]), ALU.mult), ["Ut", "s8"], ["Ut"])
                    yield
                    OP("gpsimd", TT(tmp2[:], r_t[:], kmod[:], ALU.mult), ["r_t", "kmod"], ["tmp2"])
                    OP("gpsimd", TT(tmp2[:], tmp2[:], bc[:, 4, :], ALU.mult), ["tmp2", "bc"], ["tmp2"])
                    OP("vector", TT(o_[:], o_[:], bc[:, 5, :], ALU.mult), ["Ut", "bc"], ["Ut"])
                    OP("vector", TT(o_[:], o_[:], bc[:, 6, :], ALU.add), ["Ut", "bc"], ["Ut"])
                    yield
                    OP("vector", RSUM(s8[:, :, 3], h3(tmp2[:])), ["tmp2"], ["s8"])
                    OP("vector", TT(h3(tmp2[:]), h3(v_t[:]), bc8(s8[:, :, 3]), ALU.mult), ["v_t", "s8"], ["tmp2"])
                    OP("vector", TT(o_[:], o_[:], tmp2[:], ALU.add), ["Ut", "tmp2"], ["Ut"])
                    OP("vector", TT(og[:], o_[:], g_t[:], ALU.mult), ["Ut", "g_t"], ["og"])
                    yield
                    OP("tensor", TRS([(psT[:, c * 128:(c + 1) * 128], og[:, c * 128:(c + 1) * 128], ident[:]) for c in range(4)]),
                       ["og", "ident"], ["psT"])
                    OP("scalar", ACP(ogT[:], psT[:, 0:512].rearrange("p (c t) -> p c t", t=128)), ["psT"], ["ogT"])
                    P.dma(ogv[:, hg * 4:(hg + 1) * 4, rs_], ogT[:], reads=k("ogT"))
                    yield

            for pair in ((0, 1), (2, 3)):
                gens = [body(pair[0], 0), body(pair[1], 1)]
                alive = [True, True]
                while any(alive):
                    for gi in range(2):
                        if alive[gi]:
                            try:
                                next(gens[gi])
                            except StopIteration:
                                alive[gi] = False
            P.flush()

    def rwkv_layer(self, layer):
        i = layer // 2
        mu_base = 9 + 6 * i
        self.norm_phase(layer, "rwkv", mu_base=mu_base)
        self.lin_tm(self.xmix[0], self.w_rkv[i, 0], self.r_tm)
        self.lin_tm(self.xmix[2], self.w_rkv[i, 1], self.k_tm)
        self.lin_tm(self.xmix[3], self.w_rkv[i, 2], self.vf_tm if i == 0 else self.v_tm)
        self.lora_tm(self.xmix[1], self.w1[i], 96, AF.Tanh, self.w2[i], self.zw_tm)
        self.lora_tm(self.xmix[4], self.a1[i], 96, AF.Identity, self.a2[i], self.za_tm)
        self.lora_tm(self.xmix[5], self.g1[i], 256, AF.Sigmoid, self.g2[i], self.g_tm)
        if i > 0:
            self.lora_tm(self.xmix[3], self.v1[i - 1], 64, AF.Identity, self.v2[i - 1], self.zv_tm)

    def build(self):
        self.declare()
        with ExitStack() as gst:
            self.P = Prog(self.nc, gst)
            self.P.max_blocks = getattr(self, "max_blocks", 10 ** 9)
            plan = self.plan
            if "copy" in plan:
                self.copy_in()
            for layer in range(4):
                i = layer // 2
                if ("mix%d" % layer) in plan:
                    if layer % 2 == 0:
                        self.norm_phase(layer, "plain")
                        self.even_proj(i)
                        self.even_attn(i)
                        self.even_sgu(i)
                        self.proj_residual(self.ycat, self.w_out[i])
                    else:
                        self.rwkv_layer(layer)
                        if i == 0:
                            self._vsrc_first = True
                        self.rwkv_scan_wrap(i)
                        self.proj_residual(self.ycat, self.w_o[i])
                if ("ffn%d" % layer) in plan:
                    self.norm_phase(4 + layer, "plain")
                    self.ffn_phase(layer)
            if "final" in plan:
                self.norm_phase(8, "final")
        return self.nc

    def rwkv_scan_wrap(self, i):
        if i == 0:
            save = self.v_tm
            self.v_tm = self.vf_tm
            self.rwkv_scan(i, use_vres=False)
            self.v_tm = save
        else:
            self.rwkv_scan(i, use_vres=True)


def _fm16(v):
    return np.ascontiguousarray(v.reshape(-1, 128).T)


def make_shared(inp):
    f = np.float32
    sh = {}
    vfm = np.zeros((128, 21, 16), f)
    for l in range(4):
        vfm[:, l, :] = _fm16(inp["mix_norm_g"][l])
        vfm[:, 4 + l, :] = _fm16(inp["ffn_norm_g"][l])
    vfm[:, 8, :] = _fm16(inp["final_norm_g"])
    for i in range(2):
        for j in range(6):
            vfm[:, 9 + 6 * i + j, :] = _fm16(inp["rwkv_mu"][i, j])
    sh["vfm"] = vfm
    convp = np.zeros((128, 4, 4, 44), f)
    for l in range(4):
        for j in range(3):
            convp[:, l, j, :] = _fm16(inp["ffn_conv_w"][l, j])
        convp[:, l, 3, :] = _fm16(inp["ffn_conv_b"][l])
    sh["convp"] = convp
    rv = np.zeros((2, 8, D), f)
    for i in range(2):
        for j, nm in enumerate(["rwkv_w0", "rwkv_a0", "rwkv_k_k", "rwkv_k_a", "rwkv_r_k", "rwkv_gn_g", "rwkv_gn_b"]):
            rv[i, j] = inp[nm][i]
    rv[1, 7] = inp["rwkv_v0"][0]
    sh["rv"] = rv
    sh["sgu_ln"] = np.ascontiguousarray(np.stack([inp["sgu_ln_g"], inp["sgu_ln_b"]], axis=1)).astype(f)
    sh["sgu_wT"] = np.ascontiguousarray(np.transpose(inp["sgu_w"], (0, 1, 3, 2))).astype(f)
    sh["sgu_b"] = np.ascontiguousarray(inp["sgu_b"].reshape(2, 1024)).astype(f)
    perm = np.arange(1024)
    for h in range(16):
        base = h * 64
        perm[base:base + 8] = base + 8 + np.arange(8)
        perm[base + 8:base + 16] = base + np.arange(8)
    w_in = inp["even_w_in"]
    sh["w_in"] = np.ascontiguousarray(np.concatenate([w_in, w_in[:, :, 0:1024][:, :, perm], w_in[:, :, 1024:2048][:, :, perm]], axis=2))
    sh["w_out"] = inp["even_w_out"]
    sh["w_rkv"] = inp["rwkv_w_rkv"]
    sh["w_o"] = inp["rwkv_w_o"]
    for a, b in (("w1", "rwkv_w1"), ("w2", "rwkv_w2"), ("a1", "rwkv_a1"), ("a2", "rwkv_a2"), ("g1", "rwkv_g1"), ("g2", "rwkv_g2"),
                 ("v1", "rwkv_v1"), ("v2", "rwkv_v2"), ("w_up", "ffn_w_up"), ("w_down", "ffn_w_down")):
        sh[a] = inp[b]
    p = np.arange(128)[:, None]
    q = np.arange(128)[None, :]
    consts = np.zeros((128, 8, 128), f)
    consts[:, 0] = (p == q)
    consts[:, 1] = (p > q)
    consts[:, 2] = (q > p)
    consts[:, 3] = (q >= p)
    consts[:, 4] = np.where(q <= p, 0.0, NEG)
    consts[:, 5] = np.where(p <= q, -EXPM05, 0.0)
    consts[:, 6] = -EXPM05
    sh["consts"] = consts
    inv = np.power(np.float32(500000.0), -np.arange(0, 16, 2, dtype=f) / np.float32(16)).astype(f)
    ang = np.arange(T, dtype=f)[:, None] * inv[None, :]
    cos = np.cos(ang).astype(f).T
    sin = np.sin(ang).astype(f).T
    rope = np.zeros((2, 128, T), f)
    rope[0] = 1.0
    for hh in range(2):
        b = hh * 64
        rope[0, b:b + 8] = cos
        rope[0, b + 8:b + 16] = cos
        rope[1, b:b + 8] = -sin
        rope[1, b + 8:b + 16] = sin
    sh["rope"] = rope
    keep = np.ones((128, 16, 16), f)
    for qt in range(1, 16, 2):
        keep[:, qt, qt - 1] = 0.0
    sh["keepm"] = keep
    gm = np.zeros((128, 16, 8), f)
    for qt in range(16):
        gm[:, qt, qt // 2:] = -1e30
    sh["gmask"] = gm
    return {k: np.ascontiguousarray(v, dtype=np.float32) for k, v in sh.items()}


FULL_PLAN = ["copy"] + ["mix%d" % l for l in range(4)] + ["ffn%d" % l for l in range(4)] + ["final"]
_CACHE = {}


def run_plan(inp, plan, ncores=NCORES, max_blocks=None):
    key = tuple(plan) + (max_blocks,)
    if key not in _CACHE:
        bld = Builder(plan)
        if max_blocks is not None:
            bld.max_blocks = max_blocks
        _CACHE[key] = bld.build()
    nc = _CACHE[key]
    sh = make_shared(inp)
    x = inp["x"]
    in_maps = []
    for c in range(ncores):
        m = dict(sh)
        m["xT"] = np.ascontiguousarray(x[c].T)
        in_maps.append(m)
    res = run_bass_kernel_spmd(nc, in_maps, core_ids=list(range(ncores)))
    return res


def kernel(**inputs):
    inp = {k: np.asarray(v) for k, v in inputs.items()}
    res = run_plan(inp, FULL_PLAN)
    out = np.stack([np.ascontiguousarray(res.results[c]["out"].T) for c in range(NCORES)], axis=0)
    return out.astype(np.float32)
```

```python
import numpy as np
import ml_dtypes
import concourse.bass as bass
import concourse.mybir as mybir
from concourse.bass_utils import run_bass_kernel_spmd
from contextlib import ExitStack

F32 = mybir.dt.float32
BF16 = mybir.dt.bfloat16
ALU = mybir.AluOpType
AF = mybir.ActivationFunctionType
AX = mybir.AxisListType

D = 2048
T = 2048
DFF = 5632
NCORES = 8
COMPUTE = ("tensor", "vector", "scalar", "gpsimd")
SEM_ROLL = 30000
NEG = -30000.0
EXPM05 = float(np.exp(-0.5))


class Prog:
    def __init__(self, nc, stack, n_dma_sems=16):
        self.nc = nc
        self.stack = stack
        self.engs = {"tensor": nc.tensor, "vector": nc.vector, "scalar": nc.scalar,
                     "gpsimd": nc.gpsimd, "sync": nc.sync}
        self.sem_id = 0
        self.eng_sem = {}
        self.eng_cnt = {}
        for e in COMPUTE:
            self.eng_sem[e] = self._new_sem("c_" + e)
            self.eng_cnt[e] = 0
        self.dma_pool = {"sync": [[self._new_sem("d_sync%d" % i), 0] for i in range(n_dma_sems)],
                         "scalar": [[self._new_sem("d_act%d" % i), 0] for i in range(8)]}
        self.dma_rr = {"sync": 0, "scalar": 0}
        self.known = {e: {} for e in self.engs}
        self._reset()
        self.n_ops = 0
        self.n_waits = 0

    def _reset(self):
        self.ops = {e: [] for e in self.engs}
        self.last_write = {}
        self.readers = {}

    def _new_sem(self, name):
        self.sem_id += 1
        return self.stack.enter_context(self.nc.semaphore("%s_%d" % (name, self.sem_id)))

    def _deps(self, reads, writes):
        deps = set()
        for k in reads:
            deps |= self.last_write.get(k, set())
        for k in writes:
            deps |= self.last_write.get(k, set())
            deps |= self.readers.get(k, set())
        return deps

    def _waits(self, eng, deps):
        best = {}
        for (sem, val) in deps:
            if id(sem) not in best or best[id(sem)][1] < val:
                best[id(sem)] = (sem, val)
        out = []
        kn = self.known[eng]
        for sid, (sem, val) in best.items():
            if kn.get(sid, 0) >= val:
                continue
            kn[sid] = val
            out.append((sem, val))
        return out

    def _commit(self, ev, reads, writes):
        for k in reads:
            self.readers.setdefault(k, set()).add(ev)
        for k in writes:
            self.last_write[k] = {ev}
            self.readers[k] = set()

    def op(self, eng, fn, reads=(), writes=()):
        deps = self._deps(reads, writes)
        own = self.eng_sem[eng]
        if eng == "tensor":
            deps = {d for d in deps if d[0] is not own}
        waits = self._waits(eng, deps)
        if self.eng_cnt[eng] >= SEM_ROLL:
            self.eng_sem[eng] = self._new_sem("c_" + eng)
            self.eng_cnt[eng] = 0
        self.eng_cnt[eng] += 1
        ev = (self.eng_sem[eng], self.eng_cnt[eng])
        self.ops[eng].append((waits, fn, ev[0], 1))
        self._commit(ev, reads, writes)
        self.n_waits += len(waits)
        self.n_ops += 1
        return ev

    def dma(self, out, in_, reads=(), writes=(), q="sync"):
        deps = set(self._deps(reads, writes))
        pool = self.dma_pool[q]
        i = self.dma_rr[q]
        self.dma_rr[q] = (i + 1) % len(pool)
        slot = pool[i]
        if slot[1] > 0:
            deps.add((slot[0], slot[1]))
        if slot[1] >= SEM_ROLL:
            slot[0] = self._new_sem("d_%s" % q)
            slot[1] = 0
        sem = slot[0]
        waits = self._waits(q, deps)
        slot[1] += 16
        ev = (sem, slot[1])

        def fn(e, out=out, in_=in_):
            return e.dma_start(out=out, in_=in_)
        self.ops[q].append((waits, fn, sem, 16))
        self._commit(ev, reads, writes)
        self.n_waits += len(waits)
        self.n_ops += 1
        return ev

    def flush(self):
        self.n_blocks = getattr(self, "n_blocks", 0) + 1
        if self.n_blocks > getattr(self, "max_blocks", 10 ** 9):
            self._reset()
            return
        finals = set()
        for q, pool in self.dma_pool.items():
            for sem, cnt in pool:
                if cnt > 0:
                    finals.add((sem, cnt))
        for e in COMPUTE:
            if self.eng_cnt[e] > 0:
                finals.add((self.eng_sem[e], self.eng_cnt[e]))
        fw = self._waits("sync", finals)
        self.ops["sync"].append((fw, None, None, 0))
        ops = self.ops
        with self.nc.Block() as block:
            def mk(ename):
                def body(e):
                    for (waits, fn, sem, inc) in ops[ename]:
                        for (s, v) in waits:
                            e.wait_ge(s, v)
                        if fn is not None:
                            fn(e).then_inc(sem, inc)
                return body
            for ename in ("sync", "gpsimd", "scalar", "vector", "tensor"):
                if ops[ename]:
                    getattr(block, ename)(mk(ename))
        self._reset()


def TT(out, in0, in1, op):
    return lambda e: e.tensor_tensor(out=out, in0=in0, in1=in1, op=op)


def TS(out, in0, s1, s2=None, op0=ALU.mult, op1=None):
    if op1 is None:
        return lambda e: e.tensor_scalar(out=out, in0=in0, scalar1=s1, scalar2=None, op0=op0)
    return lambda e: e.tensor_scalar(out=out, in0=in0, scalar1=s1, scalar2=s2, op0=op0, op1=op1)


def STT(out, in0, scalar, in1, op0, op1):
    return lambda e: e.scalar_tensor_tensor(out=out, in0=in0, scalar=scalar, in1=in1, op0=op0, op1=op1)


def ACT(out, in_, func, bias=None, scale=None, accum=None):
    kw = {}
    if bias is not None:
        kw["bias"] = bias
    if scale is not None:
        kw["scale"] = scale
    if accum is not None:
        kw["accum_out"] = accum
    return lambda e: e.activation(out=out, in_=in_, func=func, **kw)


def CP(out, in_):
    return lambda e: e.tensor_copy(out=out, in_=in_)


def ACP(out, in_):
    return lambda e: e.copy(out=out, in_=in_)


def MM(lst):
    def fn(e):
        ins = None
        for (out, lhsT, rhs, start, stop) in lst:
            ins = e.matmul(out, lhsT=lhsT, rhs=rhs, start=start, stop=stop)
        return ins
    return fn


def TRS(lst):
    def fn(e):
        ins = None
        for (out, in_, ident) in lst:
            ins = e.transpose(out, in_, ident)
        return ins
    return fn


def RSUM(out, in_):
    return lambda e: e.reduce_sum(out=out, in_=in_, axis=AX.X)


def RMAX(out, in_):
    return lambda e: e.reduce_max(out=out, in_=in_, axis=AX.X)


def MSET(ap, val):
    return lambda e: e.memset(ap, val)


def RECIP(out, in_):
    return lambda e: e.reciprocal(out=out, in_=in_)


def MAX8(out, in_):
    return lambda e: e.max(out=out, in_=in_)


class Builder:
    def __init__(self, plan):
        self.plan = plan
        self.nc = bass.Bass("TRN2", target_bir_lowering=False)
        self.inputs = {}

    def din(self, name, shape, dt=F32):
        t = self.nc.dram_tensor(name, list(shape), dt, kind="ExternalInput").ap()
        self.inputs[name] = t
        return t

    def dscr(self, name, shape, dt):
        return self.nc.dram_tensor(name, list(shape), dt, kind="Internal").ap()

    def sb(self, st, name, shape, dt):
        self._uid += 1
        return st.enter_context(self.nc.sbuf_tensor("%s_%d" % (name, self._uid), list(shape), dt))

    def pp(self, st, name, shape, dt):
        self._uid += 1
        return st.enter_context(self.nc.psum_tensor("%s_%d" % (name, self._uid), list(shape), dt))

    def declare(self):
        nc = self.nc
        self._uid = 0
        self.xT = self.din("xT", [D, T])
        self.vfm = self.din("vfm", [128, 21, 16])
        self.convp = self.din("convp", [128, 4, 4, 44])
        self.rv = self.din("rv", [2, 8, D])
        self.sgu_ln = self.din("sgu_ln", [2, 2, 1024])
        self.sgu_wT = self.din("sgu_wT", [2, 8, 128, 128])
        self.sgu_b = self.din("sgu_b", [2, 1024])
        self.w_in = self.din("w_in", [2, D, 7168])
        self.w_out = self.din("w_out", [2, D, D])
        self.w_rkv = self.din("w_rkv", [2, 3, D, D])
        self.w_o = self.din("w_o", [2, D, D])
        self.w1 = self.din("w1", [2, D, 96])
        self.w2 = self.din("w2", [2, 96, D])
        self.a1 = self.din("a1", [2, D, 96])
        self.a2 = self.din("a2", [2, 96, D])
        self.g1 = self.din("g1", [2, D, 256])
        self.g2 = self.din("g2", [2, 256, D])
        self.v1 = self.din("v1", [1, D, 64])
        self.v2 = self.din("v2", [1, 64, D])
        self.w_up = self.din("w_up", [4, D, 2 * DFF])
        self.w_down = self.din("w_down", [4, DFF, D])
        self.consts = self.din("consts", [128, 8, 128])
        self.rope = self.din("rope", [2, 128, T])
        self.keep = self.din("keepm", [128, 16, 16])
        self.gmask = self.din("gmask", [128, 16, 8])
        self.out = nc.dram_tensor("out", [D, T], F32, kind="ExternalOutput").ap()
        self.xres = self.dscr("xres", [D, T], F32)
        self.hbuf = self.dscr("hbuf", [D, T], BF16)
        self.gT = self.dscr("gT", [DFF, T], BF16)
        self.qT = self.dscr("qT", [1024, T], BF16)
        self.kT = self.dscr("kT", [1024, T], BF16)
        self.vtm = self.dscr("vtm", [T, 1024], BF16)
        self.uT = self.dscr("uT", [1024, T], BF16)
        self.vn = self.dscr("vn", [T, 1024], BF16)
        self.ycat = self.dscr("ycat", [D, T], BF16)
        self.xmix = [self.dscr("xmix%d" % i, [D, T], BF16) for i in range(6)]
        self.r_tm = self.dscr("r_tm", [T, D], F32)
        self.k_tm = self.dscr("k_tm", [T, D], F32)
        self.v_tm = self.dscr("v_tm", [T, D], F32)
        self.vf_tm = self.dscr("vf_tm", [T, D], F32)
        self.zw_tm = self.dscr("zw_tm", [T, D], F32)
        self.za_tm = self.dscr("za_tm", [T, D], F32)
        self.zv_tm = self.dscr("zv_tm", [T, D], F32)
        self.g_tm = self.dscr("g_tm", [T, D], F32)

    def copy_in(self):
        P = self.P
        with ExitStack() as st:
            bufs = [self.sb(st, "cpy", [128, 16, 256], F32) for _ in range(2)]
            src = self.xT.rearrange("(c p) t -> p c t", p=128)
            dst = self.xres.rearrange("(c p) t -> p c t", p=128)
            for i in range(8):
                b = bufs[i % 2]
                P.dma(b[:], src[:, :, i * 256:(i + 1) * 256], writes=["cp%d" % (i % 2)])
                P.dma(dst[:, :, i * 256:(i + 1) * 256], b[:], reads=["cp%d" % (i % 2)])
            P.flush()

    def norm_phase(self, gidx, mode, mu_base=None):
        P = self.P
        TB = 256
        with ExitStack() as st:
            xt = [self.sb(st, "nx", [128, 16, TB], F32) for _ in range(2)]
            sq = self.sb(st, "nsq", [128, 16, TB], BF16)
            rstd = self.sb(st, "nrstd", [128, TB], F32)
            ones = self.sb(st, "nones", [128, 128], BF16)
            gv = self.sb(st, "ngv", [128, 21, 16], F32)
            eps = self.sb(st, "neps", [128, 1], F32)
            ps = self.pp(st, "nps", [128, 512], F32)
            P.op("vector", MSET(ones[:], 1.0), writes=["ones"])
            P.op("vector", MSET(eps[:], 1e-6), writes=["eps"])
            P.dma(gv[:], self.vfm, writes=["gv"])
            src = self.xres.rearrange("(c p) t -> p c t", p=128)
            if mode == "plain":
                ho = [self.sb(st, "nho", [128, 16, TB], BF16) for _ in range(2)]
                dst = self.hbuf.rearrange("(c p) t -> p c t", p=128)
            elif mode == "final":
                ho = [self.sb(st, "nho", [128, 16, TB], F32) for _ in range(2)]
                dst = self.out.rearrange("(c p) t -> p c t", p=128)
            else:
                hf = self.sb(st, "nhf", [128, 16, TB + 1], F32)
                dx = [self.sb(st, "ndx", [128, TB], F32) for _ in range(2)]
                xo = [self.sb(st, "nxo", [128, 16, TB], BF16) for _ in range(6)]
                dsts = [m.rearrange("(c p) t -> p c t", p=128) for m in self.xmix]
                P.op("vector", MSET(hf[:], 0.0), writes=["hf"])
            P.dma(xt[0][:], src[:, :, 0:TB], writes=["nx0"])
            for tb in range(T // TB):
                x_ = xt[tb % 2]
                xk = "nx%d" % (tb % 2)
                sl = slice(tb * TB, (tb + 1) * TB)
                if tb + 1 < T // TB:
                    P.dma(xt[(tb + 1) % 2][:], src[:, :, (tb + 1) * TB:(tb + 2) * TB], writes=["nx%d" % ((tb + 1) % 2)])
                P.op("scalar", ACT(sq[:], x_[:], AF.Square), reads=[xk], writes=["sq"])
                P.op("tensor", MM([(ps[:, 0:TB], ones[:], sq[:, c, :], c == 0, c == 15) for c in range(16)]),
                     reads=["ones", "sq"], writes=["ps"])
                P.op("scalar", ACT(rstd[:], ps[:, 0:TB], AF.Sqrt, bias=eps[:], scale=1.0 / D),
                     reads=["ps", "eps"], writes=["rstd"])
                P.op("vector", RECIP(rstd[:], rstd[:]), reads=["rstd"], writes=["rstd"])
                if mode in ("plain", "final"):
                    h_ = ho[tb % 2]
                    hk = "ho%d" % (tb % 2)
                    for c in range(16):
                        P.op("vector", STT(h_[:, c, :], x_[:, c, :], gv[:, gidx, c:c + 1], rstd[:], ALU.mult, ALU.mult),
                             reads=[xk, "gv", "rstd"], writes=[hk])
                    P.dma(dst[:, :, sl], h_[:], reads=[hk])
                else:
                    if tb > 0:
                        P.op("vector", CP(hf[:, :, 0:1], hf[:, :, TB:TB + 1]), reads=["hf"], writes=["hf"])
                    for c in range(16):
                        P.op("vector", STT(hf[:, c, 1:TB + 1], x_[:, c, :], gv[:, gidx, c:c + 1], rstd[:], ALU.mult, ALU.mult),
                             reads=[xk, "gv", "rstd"], writes=["hf"])
                    for c in range(16):
                        d_ = dx[c % 2]
                        dk = "dx%d" % (c % 2)
                        P.op("gpsimd", TT(d_[:], hf[:, c, 0:TB], hf[:, c, 1:TB + 1], ALU.subtract), reads=["hf"], writes=[dk])
                        for i in range(6):
                            P.op("vector", STT(xo[i][:, c, :], d_[:], gv[:, mu_base + i, c:c + 1], hf[:, c, 1:TB + 1],
                                               ALU.mult, ALU.add), reads=[dk, "hf", "gv"], writes=["xo%d" % i])
                    for i in range(6):
                        P.dma(dsts[i][:, :, sl], xo[i][:], reads=["xo%d" % i])
            P.flush()

    def load_w(self, wst, wbf, key, src_ap, KC, ncols, col0=0, ndma=4):
        P = self.P
        v = src_ap.rearrange("(c p) m -> p c m", p=128)
        step = (KC + ndma - 1) // ndma
        for c0 in range(0, KC, step):
            c1 = min(KC, c0 + step)
            P.dma(wst[:, c0:c1, col0:col0 + ncols], v[:, c0:c1, :], writes=[key + "s"])
        P.op("gpsimd", CP(wbf[:, 0:KC, col0:col0 + ncols], wst[:, 0:KC, col0:col0 + ncols]),
             reads=[key + "s"], writes=[key])

    def load_hT(self, hT, src, key="hT", KC=16, t0=0, tn=T):
        v = src.rearrange("(c p) t -> p c t", p=128)
        for c0 in range(0, KC, 4):
            c1 = min(KC, c0 + 4)
            self.P.dma(hT[:, c0:c1, 0:tn], v[:, c0:c1, t0:t0 + tn], writes=[key])

    FFN_A_NEW = False
    FFN_B_NEW = True

    def ffn_phase(self, l):
        self.ffn_A_v3(l)
        self.ffn_B_v2(l)

    def ffn_A_new(self, l):
        P = self.P
        with ExitStack() as st:
            hT = self.sb(st, "hT", [128, 16, T], BF16)
            wst = [self.sb(st, "wst", [128, 16, 256], F32) for _ in range(2)]
            wbf = [self.sb(st, "wbf", [128, 16, 256], BF16) for _ in range(2)]
            cv = [self.sb(st, "cv", [128, 1024], F32) for _ in range(2)]
            sl = [self.sb(st, "sl", [128, 1024], F32) for _ in range(2)]
            go = [self.sb(st, "go", [128, 1024], BF16) for _ in range(2)]
            bnd = self.sb(st, "bnd", [128, 2], F32)
            cp = self.sb(st, "cp", [128, 4, 4, 44], F32)
            psG = [self.pp(st, "psG", [128, 1024], F32) for _ in range(2)]
            psU = [self.pp(st, "psU", [128, 1024], F32) for _ in range(2)]
            P.dma(cp[:], self.convp, writes=["cp"])
            self.load_hT(hT, self.hbuf)

            def loadW(fc):
                b = fc % 2
                self.load_w(wst[b], wbf[b], "w%d" % b, self.w_up[l][:, fc * 128:(fc + 1) * 128], 16, 128, col0=0, ndma=2)
                self.load_w(wst[b], wbf[b], "w%du" % b, self.w_up[l][:, DFF + fc * 128:DFF + (fc + 1) * 128], 16, 128, col0=128, ndma=2)
            loadW(0)
            for fc in range(44):
                b = fc % 2
                wk = "w%d" % b
                if fc + 1 < 44:
                    loadW(fc + 1)
                w0 = cp[:, l, 0, fc:fc + 1]
                w1 = cp[:, l, 1, fc:fc + 1]
                w2 = cp[:, l, 2, fc:fc + 1]
                bb = cp[:, l, 3, fc:fc + 1]
                for th in range(2):
                    g_, u_ = psG[th], psU[th]
                    gk, uk = "psG%d" % th, "psU%d" % th
                    for tb in range(2):
                        ts = slice(tb * 512, (tb + 1) * 512)
                        hs = slice(th * 1024 + tb * 512, th * 1024 + (tb + 1) * 512)
                        P.op("tensor", MM([(g_[:, ts], wbf[b][:, c, 0:128], hT[:, c, hs], c == 0, c == 15) for c in range(16)]),
                             reads=[wk, "hT"], writes=[gk])
                    for tb in range(2):
                        ts = slice(tb * 512, (tb + 1) * 512)
                        hs = slice(th * 1024 + tb * 512, th * 1024 + (tb + 1) * 512)
                        P.op("tensor", MM([(u_[:, ts], wbf[b][:, c, 128:256], hT[:, c, hs], c == 0, c == 15) for c in range(16)]),
                             reads=[wk + "u", "hT"], writes=[uk])
                    c_ = cv[th]
                    ck = "cv%d" % th
                    P.op("vector", TS(c_[:], g_[:], w2, bb, ALU.mult, ALU.add), reads=[gk, "cp"], writes=[ck])
                    P.op("vector", STT(c_[:, 1:1024], g_[:, 0:1023], w1, c_[:, 1:1024], ALU.mult, ALU.add), reads=[gk, "cp", ck], writes=[ck])
                    P.op("vector", STT(c_[:, 2:1024], g_[:, 0:1022], w0, c_[:, 2:1024], ALU.mult, ALU.add), reads=[gk, "cp", ck], writes=[ck])
                    if th == 0:
                        P.op("scalar", ACP(bnd[:], g_[:, 1022:1024]), reads=[gk], writes=["bnd"])
                    else:
                        P.op("vector", STT(c_[:, 0:1], bnd[:, 1:2], w1, c_[:, 0:1], ALU.mult, ALU.add), reads=["bnd", "cp", ck], writes=[ck])
                        P.op("vector", STT(c_[:, 0:2], bnd[:, 0:2], w0, c_[:, 0:2], ALU.mult, ALU.add), reads=["bnd", "cp", ck], writes=[ck])
                    P.op("scalar", ACT(sl[th][:], c_[:], AF.Silu), reads=[ck], writes=["sl%d" % th])
                    P.op("vector", TT(go[th][:], sl[th][:], u_[:], ALU.mult), reads=["sl%d" % th, uk], writes=["go%d" % th])
                    P.dma(self.gT[fc * 128:(fc + 1) * 128, th * 1024:(th + 1) * 1024], go[th][:], reads=["go%d" % th])
            P.flush()
    def ffn_B_new(self, l):
        P = self.P
        with ExitStack() as st:
            gTs = self.sb(st, "gTs", [128, 44, 1024], BF16)
            wst = [self.sb(st, "wst", [128, 44, 128], F32) for _ in range(2)]
            wbf = [self.sb(st, "wbf", [128, 44, 128], BF16) for _ in range(2)]
            xr = [self.sb(st, "xr", [128, 1024], F32) for _ in range(2)]
            ps = [self.pp(st, "ps", [128, 1024], F32) for _ in range(2)]
            gv = self.gT.rearrange("(c p) t -> p c t", p=128)

            def loadB(j):
                th, dc = j // 16, j % 16
                b = j % 2
                self.load_w(wst[b], wbf[b], "w%d" % b, self.w_down[l][:, dc * 128:(dc + 1) * 128], 44, 128, ndma=8)
                P.dma(xr[b][:], self.xres[dc * 128:(dc + 1) * 128, th * 1024:(th + 1) * 1024], writes=["xr%d" % b])
            loadB(0)
            for j in range(32):
                th, dc = j // 16, j % 16
                b = j % 2
                if dc == 0:
                    for c0 in range(0, 44, 4):
                        P.dma(gTs[:, c0:c0 + 4, :], gv[:, c0:c0 + 4, th * 1024:(th + 1) * 1024], writes=["gTs"])
                if j + 1 < 32:
                    loadB(j + 1)
                for tb in range(2):
                    ts = slice(tb * 512, (tb + 1) * 512)
                    P.op("tensor", MM([(ps[b][:, ts], wbf[b][:, c, :], gTs[:, c, ts], c == 0, c == 43) for c in range(44)]),
                         reads=["w%d" % b, "gTs"], writes=["ps%d" % b])
                xs = self.xres[dc * 128:(dc + 1) * 128, th * 1024:(th + 1) * 1024]
                P.op("vector", TT(xr[b][:], xr[b][:], ps[b][:], ALU.add), reads=["xr%d" % b, "ps%d" % b], writes=["xr%d" % b])
                P.dma(xs, xr[b][:], reads=["xr%d" % b])
            P.flush()

    def ffn_B_v2(self, l):
        P = self.P
        with ExitStack() as st:
            gTs = self.sb(st, "gTs", [128, 44, 1024], BF16)
            wst = self.sb(st, "wst", [128, 44, 256], F32)
            wbf = [self.sb(st, "wbf", [128, 44, 256], BF16) for _ in range(2)]
            xr = [self.sb(st, "xr", [128, 1024], F32) for _ in range(2)]
            ps = [self.pp(st, "ps", [128, 1024], F32) for _ in range(2)]
            gv = self.gT.rearrange("(c p) t -> p c t", p=128)

            def loadW(j2):
                dp = j2 % 8
                b = j2 % 2
                v = self.w_down[l][:, dp * 256:(dp + 1) * 256].rearrange("(c p) m -> p c m", p=128)
                for c0 in range(0, 44, 4):
                    P.dma(wst[:, c0:c0 + 4, :], v[:, c0:c0 + 4, :], writes=["wst"])
                P.op("gpsimd", CP(wbf[b][:, 0:22, :], wst[:, 0:22, :]), reads=["wst"], writes=["w%da" % b])
                P.op("scalar", ACP(wbf[b][:, 22:44, :], wst[:, 22:44, :]), reads=["wst"], writes=["w%db" % b])

            def loadX(n):
                th, dc = n // 16, n % 16
                P.dma(xr[n % 2][:], self.xres[dc * 128:(dc + 1) * 128, th * 1024:(th + 1) * 1024], writes=["xr%d" % (n % 2)])
            loadW(0)
            loadX(0)
            for j2 in range(16):
                th, dp = j2 // 8, j2 % 8
                b = j2 % 2
                if dp == 0:
                    for c0 in range(0, 44, 4):
                        P.dma(gTs[:, c0:c0 + 4, :], gv[:, c0:c0 + 4, th * 1024:(th + 1) * 1024], writes=["gTs"])
                if j2 + 1 < 16:
                    loadW(j2 + 1)
                for mc in range(2):
                    n = th * 16 + dp * 2 + mc
                    dc = dp * 2 + mc
                    pb = n % 2
                    if n + 1 < 32:
                        loadX(n + 1)
                    for tb in range(2):
                        ts = slice(tb * 512, (tb + 1) * 512)
                        P.op("tensor", MM([(ps[pb][:, ts], wbf[b][:, c, mc * 128:(mc + 1) * 128], gTs[:, c, ts], c == 0, c == 43) for c in range(44)]),
                             reads=["w%da" % b, "w%db" % b, "gTs"], writes=["ps%d" % pb])
                    xs = self.xres[dc * 128:(dc + 1) * 128, th * 1024:(th + 1) * 1024]
                    P.op("vector", TT(xr[pb][:], xr[pb][:], ps[pb][:], ALU.add), reads=["xr%d" % pb, "ps%d" % pb], writes=["xr%d" % pb])
                    P.dma(xs, xr[pb][:], reads=["xr%d" % pb])
            P.flush()

    def ffn_A_old(self, l):
        P = self.P
        with ExitStack() as st:
            hT = self.sb(st, "hT", [128, 16, T], BF16)
            wst = [self.sb(st, "wst", [128, 16, 256], F32) for _ in range(2)]
            wbf = [self.sb(st, "wbf", [128, 16, 256], BF16) for _ in range(2)]
            cv = self.sb(st, "cv", [128, T], F32)
            sl = self.sb(st, "sl", [128, T], F32)
            go = [self.sb(st, "go", [128, T], BF16) for _ in range(2)]
            cp = self.sb(st, "cp", [128, 4, 4, 44], F32)
            psA = self.pp(st, "psA", [128, T], F32)
            psB = self.pp(st, "psB", [128, T], F32)
            P.dma(cp[:], self.convp, writes=["cp"])
            self.load_hT(hT, self.hbuf)
            for fc in range(44):
                b = fc % 2
                wk = "w%d" % b
                self.load_w(wst[b], wbf[b], wk, self.w_up[l][:, fc * 128:(fc + 1) * 128], 16, 128, col0=0, ndma=2)
                self.load_w(wst[b], wbf[b], wk + "u", self.w_up[l][:, DFF + fc * 128:DFF + (fc + 1) * 128], 16, 128, col0=128, ndma=2)
                for tb in range(4):
                    ts = slice(tb * 512, (tb + 1) * 512)
                    P.op("tensor", MM([(psA[:, ts], wbf[b][:, c, 0:128], hT[:, c, ts], c == 0, c == 15) for c in range(16)]),
                         reads=[wk, "hT"], writes=["psA"])
                for tb in range(4):
                    ts = slice(tb * 512, (tb + 1) * 512)
                    P.op("tensor", MM([(psB[:, ts], wbf[b][:, c, 128:256], hT[:, c, ts], c == 0, c == 15) for c in range(16)]),
                         reads=[wk + "u", "hT"], writes=["psB"])
                P.op("vector", TS(cv[:], psA[:], cp[:, l, 2, fc:fc + 1], cp[:, l, 3, fc:fc + 1], ALU.mult, ALU.add),
                     reads=["psA", "cp"], writes=["cv"])
                P.op("vector", STT(cv[:, 1:T], psA[:, 0:T - 1], cp[:, l, 1, fc:fc + 1], cv[:, 1:T], ALU.mult, ALU.add),
                     reads=["psA", "cp", "cv"], writes=["cv"])
                P.op("vector", STT(cv[:, 2:T], psA[:, 0:T - 2], cp[:, l, 0, fc:fc + 1], cv[:, 2:T], ALU.mult, ALU.add),
                     reads=["psA", "cp", "cv"], writes=["cv"])
                P.op("scalar", ACT(sl[:], cv[:], AF.Silu), reads=["cv"], writes=["sl"])
                gk = "go%d" % b
                P.op("vector", TT(go[b][:], sl[:], psB[:], ALU.mult), reads=["sl", "psB"], writes=[gk])
                P.dma(self.gT[fc * 128:(fc + 1) * 128, :], go[b][:], reads=[gk])
            P.flush()
    def ffn_A_v3(self, l):
        P = self.P
        with ExitStack() as st:
            hT = self.sb(st, "hT", [128, 16, T], BF16)
            wst = [self.sb(st, "wst", [128, 16, 256], F32) for _ in range(2)]
            wbf = [self.sb(st, "wbf", [128, 16, 256], BF16) for _ in range(2)]
            cv = self.sb(st, "cv", [128, T], F32)
            sl = self.sb(st, "sl", [128, T], F32)
            go = [self.sb(st, "go", [128, T], BF16) for _ in range(2)]
            cp = self.sb(st, "cp", [128, 4, 4, 44], F32)
            psA = self.pp(st, "psA", [128, T], F32)
            psB = self.pp(st, "psB", [128, T], F32)
            P.dma(cp[:], self.convp, writes=["cp"])
            self.load_hT(hT, self.hbuf)
            def loadW(fc):
                b = fc % 2
                self.load_w(wst[b], wbf[b], "w%d" % b, self.w_up[l][:, fc * 128:(fc + 1) * 128], 16, 128, col0=0, ndma=2)
                self.load_w(wst[b], wbf[b], "w%du" % b, self.w_up[l][:, DFF + fc * 128:DFF + (fc + 1) * 128], 16, 128, col0=128, ndma=2)
            loadW(0)
            for fc in range(44):
                b = fc % 2
                wk = "w%d" % b
                if fc + 1 < 44:
                    loadW(fc + 1)
                for tb in range(4):
                    ts = slice(tb * 512, (tb + 1) * 512)
                    P.op("tensor", MM([(psA[:, ts], wbf[b][:, c, 0:128], hT[:, c, ts], c == 0, c == 15) for c in range(16)]),
                         reads=[wk, "hT"], writes=["psA"])
                for tb in range(4):
                    ts = slice(tb * 512, (tb + 1) * 512)
                    P.op("tensor", MM([(psB[:, ts], wbf[b][:, c, 128:256], hT[:, c, ts], c == 0, c == 15) for c in range(16)]),
                         reads=[wk + "u", "hT"], writes=["psB"])
                P.op("vector", TS(cv[:], psA[:], cp[:, l, 2, fc:fc + 1], cp[:, l, 3, fc:fc + 1], ALU.mult, ALU.add),
                     reads=["psA", "cp"], writes=["cv"])
                P.op("vector", STT(cv[:, 1:T], psA[:, 0:T - 1], cp[:, l, 1, fc:fc + 1], cv[:, 1:T], ALU.mult, ALU.add),
                     reads=["psA", "cp", "cv"], writes=["cv"])
                P.op("vector", STT(cv[:, 2:T], psA[:, 0:T - 2], cp[:, l, 0, fc:fc + 1], cv[:, 2:T], ALU.mult, ALU.add),
                     reads=["psA", "cp", "cv"], writes=["cv"])
                P.op("scalar", ACT(sl[:], cv[:], AF.Silu), reads=["cv"], writes=["sl"])
                gk = "go%d" % b
                P.op("vector", TT(go[b][:], sl[:], psB[:], ALU.mult), reads=["sl", "psB"], writes=[gk])
                P.dma(self.gT[fc * 128:(fc + 1) * 128, :], go[b][:], reads=[gk])
            P.flush()
    def ffn_B_old(self, l):
        P = self.P
        with ExitStack() as st:
            gTs = self.sb(st, "gTs", [128, 44, 1024], BF16)
            wst = [self.sb(st, "wst", [128, 44, 128], F32) for _ in range(2)]
            wbf = [self.sb(st, "wbf", [128, 44, 128], BF16) for _ in range(2)]
            xr = [self.sb(st, "xr", [128, 1024], F32) for _ in range(2)]
            ps = [self.pp(st, "ps", [128, 1024], F32) for _ in range(2)]
            gv = self.gT.rearrange("(c p) t -> p c t", p=128)
            cnt = 0
            for th in range(2):
                for c0 in range(0, 44, 4):
                    P.dma(gTs[:, c0:c0 + 4, :], gv[:, c0:c0 + 4, th * 1024:(th + 1) * 1024], writes=["gTs"])
                for dc in range(16):
                    b = cnt % 2
                    cnt += 1
                    wk = "w%d" % b
                    self.load_w(wst[b], wbf[b], wk, self.w_down[l][:, dc * 128:(dc + 1) * 128], 44, 128, ndma=8)
                    for tb in range(2):
                        ts = slice(tb * 512, (tb + 1) * 512)
                        P.op("tensor", MM([(ps[b][:, ts], wbf[b][:, c, :], gTs[:, c, ts], c == 0, c == 43) for c in range(44)]),
                             reads=[wk, "gTs"], writes=["ps%d" % b])
                    xs = self.xres[dc * 128:(dc + 1) * 128, th * 1024:(th + 1) * 1024]
                    P.dma(xr[b][:], xs, writes=["xr%d" % b])
                    P.op("vector", TT(xr[b][:], xr[b][:], ps[b][:], ALU.add), reads=["xr%d" % b, "ps%d" % b], writes=["xr%d" % b])
                    P.dma(xs, xr[b][:], reads=["xr%d" % b])
            P.flush()

    def proj_residual(self, src, w_ap):
        P = self.P
        with ExitStack() as st:
            hT = self.sb(st, "hT", [128, 16, T], BF16)
            wst = [self.sb(st, "wst", [128, 16, 256], F32) for _ in range(2)]
            wbf = [self.sb(st, "wbf", [128, 16, 256], BF16) for _ in range(2)]
            xr = [self.sb(st, "xr", [128, T], F32) for _ in range(2)]
            ps = [self.pp(st, "ps", [128, T], F32) for _ in range(2)]
            self.load_hT(hT, src)

            def loadW(cb):
                b = cb % 2
                self.load_w(wst[b], wbf[b], "w%d" % b, w_ap[:, cb * 256:(cb + 1) * 256], 16, 256)

            def loadX(dc):
                P.dma(xr[dc % 2][:], self.xres[dc * 128:(dc + 1) * 128, :], writes=["xr%d" % (dc % 2)])
            loadW(0)
            loadX(0)
            for cb in range(8):
                b = cb % 2
                wk = "w%d" % b
                if cb + 1 < 8:
                    loadW(cb + 1)
                for mc in range(2):
                    dc = cb * 2 + mc
                    pb = dc % 2
                    if dc + 1 < 16:
                        loadX(dc + 1)
                    for tb in range(4):
                        ts = slice(tb * 512, (tb + 1) * 512)
                        P.op("tensor", MM([(ps[pb][:, ts], wbf[b][:, c, mc * 128:(mc + 1) * 128], hT[:, c, ts], c == 0, c == 15)
                                           for c in range(16)]), reads=[wk, "hT"], writes=["ps%d" % pb])
                    xs = self.xres[dc * 128:(dc + 1) * 128, :]
                    P.op("vector", TT(xr[pb][:], xr[pb][:], ps[pb][:], ALU.add), reads=["xr%d" % pb, "ps%d" % pb], writes=["xr%d" % pb])
                    P.dma(xs, xr[pb][:], reads=["xr%d" % pb])
            P.flush()

    def even_proj(self, i):
        P = self.P
        win = self.w_in[i]
        with ExitStack() as st:
            hT = self.sb(st, "hT", [128, 16, T], BF16)
            wst = [self.sb(st, "wst", [128, 16, 256], F32) for _ in range(2)]
            wbf = [self.sb(st, "wbf", [128, 16, 256], BF16) for _ in range(2)]
            rc = self.sb(st, "ropec", [128, T], F32)
            rs = self.sb(st, "ropes", [128, T], F32)
            t1 = self.sb(st, "t1", [128, T], F32)
            t2 = self.sb(st, "t2", [128, T], F32)
            ob = [self.sb(st, "ob", [128, T], BF16) for _ in range(2)]
            lng = self.sb(st, "lng", [128, 2, 1024], F32)
            st8 = self.sb(st, "st8", [128, 8, 2], F32)
            st9 = self.sb(st, "st9", [128, 8, 2], F32)
            eps = self.sb(st, "eps", [128, 1], F32)
            psA = self.pp(st, "psA", [128, T], F32)
            psB = self.pp(st, "psB", [128, T], F32)
            P.dma(rc[:], self.rope[0], writes=["rc"])
            P.dma(rs[:], self.rope[1], writes=["rs"])
            for a in range(2):
                P.dma(lng[:, a, :], self.sgu_ln[i, a, :].partition_broadcast(128), writes=["lng"])
            P.op("vector", MSET(eps[:], 1e-5), writes=["eps"])
            self.load_hT(hT, self.hbuf)
            jobs = []
            pss = [psA, psB]
            state = {"pcnt": 0}

            def mk_qk(c_main, c_perm, dst, j):
                def load(b):
                    self.load_w(wst[b], wbf[b], "w%d" % b, win[:, c_main + j * 128:c_main + (j + 1) * 128], 16, 128, col0=0, ndma=2)
                    self.load_w(wst[b], wbf[b], "w%du" % b, win[:, c_perm + j * 128:c_perm + (j + 1) * 128], 16, 128, col0=128, ndma=2)

                def comp(b):
                    wk = "w%d" % b
                    for tb in range(4):
                        ts = slice(tb * 512, (tb + 1) * 512)
                        P.op("tensor", MM([(psA[:, ts], wbf[b][:, c, 0:128], hT[:, c, ts], c == 0, c == 15) for c in range(16)]),
                             reads=[wk, "hT"], writes=["psA"])
                    for tb in range(4):
                        ts = slice(tb * 512, (tb + 1) * 512)
                        P.op("tensor", MM([(psB[:, ts], wbf[b][:, c, 128:256], hT[:, c, ts], c == 0, c == 15) for c in range(16)]),
                             reads=[wk + "u", "hT"], writes=["psB"])
                    P.op("vector", TT(t1[:], psA[:], rc[:], ALU.mult), reads=["psA", "rc"], writes=["t1"])
                    P.op("vector", TT(t2[:], psB[:], rs[:], ALU.mult), reads=["psB", "rs"], writes=["t2"])
                    P.op("gpsimd", TT(ob[b][:], t1[:], t2[:], ALU.add), reads=["t1", "t2"], writes=["ob%d" % b])
                    P.dma(dst[j * 128:(j + 1) * 128, :], ob[b][:], reads=["ob%d" % b])
                return load, comp

            def mk_u(j):
                def load(b):
                    self.load_w(wst[b], wbf[b], "w%d" % b, win[:, 3072 + j * 128:3072 + (j + 1) * 128], 16, 128, col0=0, ndma=2)

                def comp(b):
                    wk = "w%d" % b
                    ps_ = pss[j % 2]
                    pk = "psA" if j % 2 == 0 else "psB"
                    for tb in range(4):
                        ts = slice(tb * 512, (tb + 1) * 512)
                        P.op("tensor", MM([(ps_[:, ts], wbf[b][:, c, 0:128], hT[:, c, ts], c == 0, c == 15) for c in range(16)]),
                             reads=[wk, "hT"], writes=[pk])
                    P.op("scalar", ACT(ob[b][:], ps_[:], AF.Gelu), reads=[pk], writes=["ob%d" % b])
                    P.dma(self.uT[j * 128:(j + 1) * 128, :], ob[b][:], reads=["ob%d" % b])
                return load, comp

            def mk_tm(which, c0, dst, mb):
                dv = dst.rearrange("(i p) m -> p i m", p=128)

                def load(b):
                    self.load_w(wst[b], wbf[b], "w%d" % b, win[:, c0 + mb * 256:c0 + (mb + 1) * 256], 16, 256)

                def comp(b):
                    wk = "w%d" % b
                    for half in range(2):
                        pcnt = state["pcnt"]
                        ps_ = pss[pcnt % 2]
                        pk = "psA" if pcnt % 2 == 0 else "psB"
                        state["pcnt"] = pcnt + 1
                        for i8 in range(8):
                            tt = half * 8 + i8
                            P.op("tensor", MM([(ps_[:, i8 * 256:(i8 + 1) * 256], hT[:, c, tt * 128:(tt + 1) * 128], wbf[b][:, c, :], c == 0, c == 15)
                                               for c in range(16)]), reads=[wk, "hT"], writes=[pk])
                        o_ = ob[pcnt % 2]
                        ok_ = "ob%d" % (pcnt % 2)
                        if which == 0:
                            P.op("scalar", ACP(o_[:], ps_[:]), reads=[pk], writes=[ok_])
                        else:
                            P.op("scalar", ACT(t1[:], ps_[:], AF.Gelu), reads=[pk], writes=["t1"])
                            v4 = t1[:].rearrange("p (a g d) -> p a g d", a=8, g=2)
                            w4 = t2[:].rearrange("p (a g d) -> p a g d", a=8, g=2)
                            P.op("vector", RSUM(st8[:], v4), reads=["t1"], writes=["st8"])
                            P.op("vector", TS(st8[:], st8[:], 1.0 / 128.0), reads=["st8"], writes=["st8"])
                            P.op("vector", TT(w4, v4, st8[:].unsqueeze(3).to_broadcast([128, 8, 2, 128]), ALU.subtract),
                                 reads=["t1", "st8"], writes=["t2"])
                            P.op("gpsimd", TT(t1[:], t2[:], t2[:], ALU.mult), reads=["t2"], writes=["t1"])
                            P.op("vector", RSUM(st9[:], v4), reads=["t1"], writes=["st9"])
                            P.op("scalar", ACT(st9[:], st9[:], AF.Sqrt, bias=eps[:], scale=1.0 / 128.0), reads=["st9", "eps"], writes=["st9"])
                            P.op("vector", RECIP(st9[:], st9[:]), reads=["st9"], writes=["st9"])
                            P.op("vector", TT(w4, w4, st9[:].unsqueeze(3).to_broadcast([128, 8, 2, 128]), ALU.mult),
                                 reads=["t2", "st9"], writes=["t2"])
                            w3 = t2[:].rearrange("p (a m) -> p a m", a=8)
                            gsl = lng[:, 0, mb * 256:(mb + 1) * 256].unsqueeze(1).to_broadcast([128, 8, 256])
                            bsl = lng[:, 1, mb * 256:(mb + 1) * 256].unsqueeze(1).to_broadcast([128, 8, 256])
                            P.op("vector", TT(w3, w3, gsl, ALU.mult), reads=["t2", "lng"], writes=["t2"])
                            P.op("vector", TT(o_[:].rearrange("p (a m) -> p a m", a=8), w3, bsl, ALU.add), reads=["t2", "lng"], writes=[ok_])
                        P.dma(dv[:, half * 8:(half + 1) * 8, mb * 256:(mb + 1) * 256], o_[:].rearrange("p (a m) -> p a m", a=8), reads=[ok_])
                return load, comp

            for (c_main, c_perm, dst) in ((0, 5120, self.qT), (1024, 6144, self.kT)):
                for j in range(8):
                    jobs.append(mk_qk(c_main, c_perm, dst, j))
            for j in range(8):
                jobs.append(mk_u(j))
            for which, (c0, dst) in enumerate(((2048, self.vtm), (4096, self.vn))):
                for mb in range(4):
                    jobs.append(mk_tm(which, c0, dst, mb))
            jobs[0][0](0)
            for n, (ld, cmp_) in enumerate(jobs):
                if n + 1 < len(jobs):
                    jobs[n + 1][0]((n + 1) % 2)
                cmp_(n % 2)
            P.flush()

    def even_attn(self, i):
        P = self.P
        with ExitStack() as st:
            V = self.sb(st, "V", [128, 16, 1024], BF16)
            att = self.sb(st, "att", [128, 16, 1024], BF16)
            cst = self.sb(st, "cst", [128, 8, 128], F32)
            ident = self.sb(st, "ident", [128, 128], BF16)
            keep = self.sb(st, "keep", [128, 16, 16], F32)
            gmask = self.sb(st, "gmask", [128, 16, 8], F32)
            qh = [self.sb(st, "qh", [64, T], BF16) for _ in range(2)]
            kh = [self.sb(st, "kh", [64, T], BF16) for _ in range(2)]
            km = self.sb(st, "km", [64, 8], F32)
            kmb = self.sb(st, "kmb", [64, 8], BF16)
            gate = self.sb(st, "gate", [128, 16, 8], F32)
            m8 = self.sb(st, "m8", [128, 16, 8], F32)
            bias8 = self.sb(st, "bias8", [128, 16, 8], F32)
            bias16 = self.sb(st, "bias16", [128, 16, 16], F32)
            sc = [self.sb(st, "sc", [128, T], F32) for _ in range(2)]
            pb = [self.sb(st, "pb", [128, T], BF16) for _ in range(2)]
            pT = [self.sb(st, "pT", [128, 16, 128], BF16) for _ in range(2)]
            mx = [self.sb(st, "mx", [128, 4], F32) for _ in range(3)]
            psS = self.pp(st, "psS", [128, T], F32)
            psT = self.pp(st, "psT", [128, T], BF16)
            psG = self.pp(st, "psG", [128, 512], F32)
            psO = self.pp(st, "psO", [128, 512], F32)
            P.dma(V[:], self.vtm.rearrange("(i p) m -> p i m", p=128), writes=["V"])
            P.dma(cst[:], self.consts, writes=["cst"])
            P.dma(keep[:], self.keep, writes=["keep"])
            P.dma(gmask[:], self.gmask, writes=["gmask"])
            P.op("vector", CP(ident[:], cst[:, 0, :]), reads=["cst"], writes=["ident"])
            causal = cst[:, 4, :]

            def load_qk(h):
                b = h % 2
                P.dma(qh[b][:], self.qT[h * 64:(h + 1) * 64, :], writes=["qh%d" % b])
                P.dma(kh[b][:], self.kT[h * 64:(h + 1) * 64, :], writes=["kh%d" % b])

            def S1(h, qt):
                b = h % 2
                par = qt % 2
                qk, kk_ = "qh%d" % b, "kh%d" % b
                sc_, mx_ = sc[par], mx[qt % 3]
                sk, mk = "sc%d" % par, "mx%d" % (qt % 3)
                nk = (qt + 1) * 128
                q_sl = qh[b][:, qt * 128:(qt + 1) * 128]
                mms = []
                for n0 in range(0, nk, 512):
                    n1 = min(nk, n0 + 512)
                    mms.append((psS[:, n0:n1], q_sl, kh[b][:, n0:n1], True, True))
                P.op("tensor", MM(mms), reads=[qk, kk_], writes=["psS"])
                if qt > 0:
                    P.op("vector", STT(sc_[:, 0:qt * 128].rearrange("p (a k) -> p a k", k=128),
                                       psS[:, 0:qt * 128].rearrange("p (a k) -> p a k", k=128), 0.125,
                                       bias16[:, qt, 0:qt].unsqueeze(2).to_broadcast([128, qt, 128]), ALU.mult, ALU.add),
                         reads=["psS", "bias16"], writes=[sk])
                P.op("vector", STT(sc_[:, qt * 128:nk], psS[:, qt * 128:nk], 0.125, causal, ALU.mult, ALU.add),
                     reads=["psS", "cst"], writes=[sk])
                P.op("vector", RMAX(mx_[:, 0:1], sc_[:, 0:nk]), reads=[sk], writes=[mk])
                P.op("vector", TS(mx_[:, 1:2], mx_[:, 0:1], -1.0), reads=[mk], writes=[mk])

            def S2(h, qt):
                par = qt % 2
                sc_, pb_, mx_ = sc[par], pb[par], mx[qt % 3]
                sk, pk, mk = "sc%d" % par, "pb%d" % par, "mx%d" % (qt % 3)
                nk = (qt + 1) * 128
                P.op("scalar", ACT(pb_[:, 0:nk], sc_[:, 0:nk], AF.Exp, bias=mx_[:, 1:2], accum=mx_[:, 2:3]), reads=[sk, mk], writes=[pk, mk])
                P.op("vector", RECIP(mx_[:, 3:4], mx_[:, 2:3]), reads=[mk], writes=[mk])
                P.op("tensor", TRS([(psT[:, kt * 128:(kt + 1) * 128], pb_[:, kt * 128:(kt + 1) * 128], ident[:]) for kt in range(qt + 1)]),
                     reads=[pk, "ident"], writes=["psT"])
                P.op("scalar", ACP(pT[par][:, 0:qt + 1, :], psT[:, 0:nk].rearrange("p (a k) -> p a k", k=128)), reads=["psT"], writes=["pT%d" % par])

            def S3(h, qt):
                par = qt % 2
                mx_ = mx[qt % 3]
                mk = "mx%d" % (qt % 3)
                P.op("tensor", MM([(psO[:, 0:64], pT[par][:, kt, :], V[:, kt, h * 64:(h + 1) * 64], kt == 0, kt == qt) for kt in range(qt + 1)]),
                     reads=["pT%d" % par, "V"], writes=["psO"])
                P.op("vector", TS(att[:, qt, h * 64:(h + 1) * 64], psO[:, 0:64], mx_[:, 3:4]), reads=["psO", mk], writes=["att"])

            load_qk(0)
            par = 0
            for h in range(16):
                b = h % 2
                qk, kk_ = "qh%d" % b, "kh%d" % b
                if h + 1 < 16:
                    load_qk(h + 1)
                P.op("vector", RSUM(km[:], kh[b][:].rearrange("p (n k) -> p n k", k=256)), reads=[kk_], writes=["km"])
                P.op("vector", TS(kmb[:], km[:], 1.0 / 256.0), reads=["km"], writes=["kmb"])
                P.op("tensor", MM([(psG[:, qt * 8:(qt + 1) * 8], qh[b][:, qt * 128:(qt + 1) * 128], kmb[:], True, True) for qt in range(16)]),
                     reads=[qk, "kmb"], writes=["psG"])
                P.op("vector", TT(gate[:], psG[:, 0:128].rearrange("p (a n) -> p a n", n=8), gmask[:], ALU.add),
                     reads=["psG", "gmask"], writes=["gate"])
                for qt in range(16):
                    P.op("vector", MAX8(m8[:, qt, :], gate[:, qt, :]), reads=["gate"], writes=["m8"])
                P.op("vector", TT(bias8[:], gate[:], m8[:, :, 2:3].to_broadcast([128, 16, 8]), ALU.is_ge), reads=["gate", "m8"], writes=["bias8"])
                P.op("vector", TS(bias8[:], bias8[:], -1.0, -NEG, ALU.add, ALU.mult), reads=["bias8"], writes=["bias8"])
                P.op("vector", CP(bias16[:].rearrange("p a (n two) -> p a n two", two=2), bias8[:].unsqueeze(3).to_broadcast([128, 16, 8, 2])),
                     reads=["bias8"], writes=["bias16"])
                P.op("vector", TT(bias16[:], bias16[:], keep[:], ALU.mult), reads=["bias16", "keep"], writes=["bias16"])
                S1(h, 0)
                S1(h, 1)
                S2(h, 0)
                for qt in range(16):
                    if qt + 2 < 16:
                        S1(h, qt + 2)
                    if qt + 1 < 16:
                        S2(h, qt + 1)
                    S3(h, qt)
            yv = self.ycat.rearrange("(c p) t -> p c t", p=128)
            aT = [self.sb(st, "aT", [128, T], BF16) for _ in range(2)]
            for c in range(8):
                P.op("tensor", TRS([(psT[:, qt * 128:(qt + 1) * 128], att[:, qt, c * 128:(c + 1) * 128], ident[:]) for qt in range(16)]),
                     reads=["att", "ident"], writes=["psT"])
                P.op("scalar", ACP(aT[c % 2][:], psT[:]), reads=["psT"], writes=["aT%d" % (c % 2)])
                P.dma(yv[:, c, :], aT[c % 2][:], reads=["aT%d" % (c % 2)])
            P.flush()

    def even_sgu(self, i):
        P = self.P
        with ExitStack() as st:
            vn = self.sb(st, "vn", [128, 16, 1024], BF16)
            uT = self.sb(st, "uT", [128, 8, T], BF16)
            wsf = self.sb(st, "wsf", [128, 8, 128], F32)
            wsb = self.sb(st, "wsb", [128, 8, 128], BF16)
            cst = self.sb(st, "cst", [128, 8, 128], F32)
            bsb = self.sb(st, "bsb", [128, 8, 128], F32)
            tmp = self.sb(st, "tmp", [128, T], F32)
            ob = [self.sb(st, "ob", [128, T], BF16) for _ in range(2)]
            ps = [self.pp(st, "ps", [128, T], F32) for _ in range(2)]
            P.dma(vn[:], self.vn.rearrange("(i p) m -> p i m", p=128), writes=["vn"])
            P.dma(uT[:], self.uT.rearrange("(c p) t -> p c t", p=128), writes=["uT"])
            P.dma(wsf[:], self.sgu_wT[i].rearrange("g s t -> s g t"), writes=["wsf"])
            P.dma(cst[:], self.consts, writes=["cst"])
            P.dma(bsb[:].rearrange("p g t -> p (g t)"), self.sgu_b[i, :].partition_broadcast(128), writes=["bsb"])
            P.op("vector", TT(wsb[:], wsf[:], cst[:, 3, :].unsqueeze(1).to_broadcast([128, 8, 128]), ALU.mult),
                 reads=["wsf", "cst"], writes=["wsb"])
            yv = self.ycat.rearrange("(c p) t -> p c t", p=128)
            for g in range(8):
                b = g % 2
                P.op("tensor", MM([(ps[b][:, c * 128:(c + 1) * 128], vn[:, c, g * 128:(g + 1) * 128], wsb[:, g, :], True, True) for c in range(16)]),
                     reads=["vn", "wsb"], writes=["ps%d" % b])
                P.op("vector", TT(tmp[:].rearrange("p (c t) -> p c t", t=128), ps[b][:].rearrange("p (c t) -> p c t", t=128),
                                  bsb[:, g, :].unsqueeze(1).to_broadcast([128, 16, 128]), ALU.add), reads=["ps%d" % b, "bsb"], writes=["tmp"])
                P.op("vector", TT(ob[b][:], tmp[:], uT[:, g, :], ALU.mult), reads=["tmp", "uT"], writes=["ob%d" % b])
                P.dma(yv[:, 8 + g, :], ob[b][:], reads=["ob%d" % b])
            P.flush()

    def lin_tm(self, src, w_ap, dst, KC=16):
        P = self.P
        with ExitStack() as st:
            hT = self.sb(st, "hT", [128, 16, T], BF16)
            wst = [self.sb(st, "wst", [128, 16, 256], F32) for _ in range(2)]
            wbf = [self.sb(st, "wbf", [128, 16, 256], BF16) for _ in range(2)]
            ob = [self.sb(st, "ob", [128, T], F32) for _ in range(2)]
            ps = [self.pp(st, "ps", [128, T], F32) for _ in range(2)]
            self.load_hT(hT, src)
            dv = dst.rearrange("(i p) m -> p i m", p=128)
            pcnt = 0
            self.load_w(wst[0], wbf[0], "w0", w_ap[:, 0:256], 16, 256)
            for mb in range(8):
                b = mb % 2
                wk = "w%d" % b
                if mb + 1 < 8:
                    self.load_w(wst[1 - b], wbf[1 - b], "w%d" % (1 - b), w_ap[:, (mb + 1) * 256:(mb + 2) * 256], 16, 256)
                for half in range(2):
                    pbk = pcnt % 2
                    pcnt += 1
                    for i8 in range(8):
                        tt = half * 8 + i8
                        P.op("tensor", MM([(ps[pbk][:, i8 * 256:(i8 + 1) * 256], hT[:, c, tt * 128:(tt + 1) * 128], wbf[b][:, c, :], c == 0, c == 15)
                                           for c in range(16)]), reads=[wk, "hT"], writes=["ps%d" % pbk])
                    eng = "scalar" if pbk == 0 else "vector"
                    P.op(eng, (ACP if eng == "scalar" else CP)(ob[pbk][:], ps[pbk][:]), reads=["ps%d" % pbk], writes=["ob%d" % pbk])
                    P.dma(dv[:, half * 8:(half + 1) * 8, mb * 256:(mb + 1) * 256], ob[pbk][:].rearrange("p (a m) -> p a m", a=8), reads=["ob%d" % pbk])
            P.flush()

    def lora_tm(self, src, w1_ap, R, func, w2_ap, dst):
        P = self.P
        RC = (R + 127) // 128
        rows = [min(128, R - rc * 128) for rc in range(RC)]
        with ExitStack() as st:
            hT = self.sb(st, "hT", [128, 16, T], BF16)
            w1s = self.sb(st, "w1s", [128, 16, R], F32)
            w1b = self.sb(st, "w1b", [128, 16, R], BF16)
            w2s = self.sb(st, "w2s", [128, RC, D], F32)
            w2b = self.sb(st, "w2b", [128, RC, D], BF16)
            lT = self.sb(st, "lT", [128, RC, T], BF16)
            ob = [self.sb(st, "ob", [128, T], F32) for _ in range(2)]
            ps = [self.pp(st, "ps", [128, T], F32) for _ in range(2)]
            self.load_hT(hT, src)
            self.load_w(w1s, w1b, "w1", w1_ap, 16, R)
            for rc in range(RC):
                P.dma(w2s[0:rows[rc], rc, :], w2_ap[rc * 128:rc * 128 + rows[rc], :], writes=["w2s"])
                P.op("gpsimd", CP(w2b[0:rows[rc], rc, :], w2s[0:rows[rc], rc, :]), reads=["w2s"], writes=["w2b"])
            for rc in range(RC):
                pbk = rc % 2
                for tb in range(4):
                    ts = slice(tb * 512, (tb + 1) * 512)
                    P.op("tensor", MM([(ps[pbk][0:rows[rc], ts], w1b[:, c, rc * 128:rc * 128 + rows[rc]], hT[:, c, ts], c == 0, c == 15)
                                       for c in range(16)]), reads=["w1", "hT"], writes=["ps%d" % pbk])
                P.op("scalar", ACT(lT[0:rows[rc], rc, :], ps[pbk][0:rows[rc], :], func), reads=["ps%d" % pbk], writes=["lT"])
            dv = dst.rearrange("(i p) m -> p i m", p=128)
            pcnt = 0
            for mb in range(8):
                for half in range(2):
                    pbk = pcnt % 2
                    pcnt += 1
                    for i8 in range(8):
                        tt = half * 8 + i8
                        P.op("tensor", MM([(ps[pbk][:, i8 * 256:(i8 + 1) * 256], lT[0:rows[rc], rc, tt * 128:(tt + 1) * 128],
                                            w2b[0:rows[rc], rc, mb * 256:(mb + 1) * 256], rc == 0, rc == RC - 1) for rc in range(RC)]),
                             reads=["lT", "w2b"], writes=["ps%d" % pbk])
                    eng = "scalar" if pbk == 0 else "vector"
                    P.op(eng, (ACP if eng == "scalar" else CP)(ob[pbk][:], ps[pbk][:]), reads=["ps%d" % pbk], writes=["ob%d" % pbk])
                    P.dma(dv[:, half * 8:(half + 1) * 8, mb * 256:(mb + 1) * 256], ob[pbk][:].rearrange("p (a m) -> p a m", a=8), reads=["ob%d" % pbk])
            P.flush()

    def rwkv_scan(self, i, use_vres):
        P = self.P
        W = 512
        vsrc = self.v_tm
        with ExitStack() as st:
            cst = self.sb(st, "cst", [128, 8, 128], F32)
            ident = self.sb(st, "ident", [128, 128], BF16)
            mL = self.sb(st, "mL", [128, 128], BF16)
            mU = self.sb(st, "mU", [128, 128], BF16)
            mUi = self.sb(st, "mUi", [128, 128], BF16)
            negcol = self.sb(st, "negcol", [128, 1], F32)
            eps = self.sb(st, "eps", [128, 1], F32)
            P.dma(cst[:], self.consts, writes=["cst"])
            P.op("vector", CP(ident[:], cst[:, 0, :]), reads=["cst"], writes=["ident"])
            P.op("vector", CP(mL[:], cst[:, 1, :]), reads=["cst"], writes=["mL"])
            P.op("vector", CP(mU[:], cst[:, 2, :]), reads=["cst"], writes=["mU"])
            P.op("vector", CP(mUi[:], cst[:, 3, :]), reads=["cst"], writes=["mUi"])
            P.op("vector", MSET(negcol[:], -EXPM05), writes=["negcol"])
            P.op("vector", MSET(eps[:], 64e-5), writes=["eps"])
            triN = cst[:, 5, :]
            allN = cst[:, 6, :]
            ogv = self.ycat.rearrange("(c p) t -> p c t", p=128)

            def h3(ap):
                return ap.rearrange("p (h d) -> p h d", d=64)

            def bc8(ap, n=64):
                return ap.unsqueeze(2).to_broadcast([128, 8, n])

            def mb8(m):
                return m[:].unsqueeze(1).to_broadcast([128, 8, 128])

            def v3(ap):
                return ap.rearrange("p (h t) -> p h t", t=128)

            sets = []
            for si in range(2):
                d = {}
                for nm in ("r_t", "k_t", "v_t", "zw", "za", "g_t", "cl", "E1", "E2", "tmp", "tmp2", "kk", "kmod", "b_", "Ut"):
                    d[nm] = self.sb(st, nm, [128, W], F32)
                if use_vres:
                    d["zv"] = self.sb(st, "zv", [128, W], F32)
                    d["vf"] = self.sb(st, "vf", [128, W], F32)
                for nm in ("At", "Bt", "Kt", "Rt", "Bh", "Kh", "Vb", "AkV", "Ub", "og"):
                    d[nm] = self.sb(st, nm, [128, W], BF16)
                for nm in ("ATf", "BTf", "KTf", "RTf", "WTf"):
                    d[nm] = self.sb(st, nm, [64, 8, 128], BF16)
                for nm in ("Mm0", "Mm1", "MT0", "MT1", "TT", "AakT", "ArbT", "ArkT"):
                    d[nm] = self.sb(st, nm, [128, 8, 128], BF16)
                d["bc"] = self.sb(st, "bc", [128, 8, W], F32)
                d["s8"] = self.sb(st, "s8", [128, 8, 4], F32)
                d["PC"] = self.sb(st, "PC", [64, 8], F32)
                d["S"] = self.sb(st, "S", [64, 8, 64], F32)
                d["Sb0"] = self.sb(st, "Sb0", [64, 8, 64], BF16)
                d["Sb1"] = self.sb(st, "Sb1", [64, 8, 64], BF16)
                d["ogT"] = self.sb(st, "ogT", [128, 4, 128], BF16)
                d["psM"] = self.pp(st, "psM", [128, 1024], F32)
                d["psW"] = self.pp(st, "psW", [128, 512], F32)
                d["psT"] = self.pp(st, "psT", [128, 1024], BF16)
                sets.append(d)

            def body(hg, si):
                d = sets[si]
                sfx = "_%d" % si

                def k(*names):
                    return [n + sfx if n not in ("cst", "ident", "mL", "mU", "mUi", "negcol", "eps") else n for n in names]

                def OP(eng, fn, r, w):
                    P.op(eng, fn, reads=k(*r), writes=k(*w))
                r_t, k_t, v_t, zw, za, g_t = d["r_t"], d["k_t"], d["v_t"], d["zw"], d["za"], d["g_t"]
                cl, E1, E2, tmp, tmp2, kk, kmod, b_, Ut = d["cl"], d["E1"], d["E2"], d["tmp"], d["tmp2"], d["kk"], d["kmod"], d["b_"], d["Ut"]
                At, Bt, Kt, Rt, Bh, Kh, Vb, AkV, Ub, og = (d[n] for n in ("At", "Bt", "Kt", "Rt", "Bh", "Kh", "Vb", "AkV", "Ub", "og"))
                ATf, BTf, KTf, RTf, WTf = (d[n] for n in ("ATf", "BTf", "KTf", "RTf", "WTf"))
                Mm = [d["Mm0"], d["Mm1"]]
                MT = [d["MT0"], d["MT1"]]
                TTm, AakT, ArbT, ArkT = d["TT"], d["AakT"], d["ArbT"], d["ArkT"]
                bc, s8, PC, S, ogT = d["bc"], d["s8"], d["PC"], d["S"], d["ogT"]
                Sb = [d["Sb0"], d["Sb1"]]
                psM, psW, psT = d["psM"], d["psW"], d["psT"]
                sg, a_, E3, E4, scr, o_ = zw, za, tmp, tmp2, cl, Ut
                cs = slice(hg * W, (hg + 1) * W)
                for j in range(8):
                    P.dma(bc[:, j, :], self.rv[i, j, cs].partition_broadcast(128), writes=k("bc"))
                OP("vector", MSET(S[:], 0.0), [], ["S"])
                OP("vector", MSET(Sb[0][:], 0.0), [], ["Sb0"])
                yield
                for tt in range(16):
                    rs_ = slice(tt * 128, (tt + 1) * 128)
                    for (tile_, src, key) in ((r_t, self.r_tm, "r_t"), (k_t, self.k_tm, "k_t"), (v_t, vsrc, "v_t"),
                                              (zw, self.zw_tm, "zw"), (za, self.za_tm, "za"), (g_t, self.g_tm, "g_t")):
                        P.dma(tile_[:], src[rs_, cs], writes=k(key))
                    if use_vres:
                        P.dma(d["zv"][:], self.zv_tm[rs_, cs], writes=k("zv"))
                        P.dma(d["vf"][:], self.vf_tm[rs_, cs], writes=k("vf"))
                    yield
                    OP("vector", TT(zw[:], zw[:], bc[:, 0, :], ALU.add), ["zw", "bc"], ["zw"])
                    OP("scalar", ACT(sg[:], zw[:], AF.Sigmoid), ["zw"], ["zw"])
                    yield
                    OP("tensor", MM([(psW[:], triN, sg[:], True, True)]), ["cst", "zw"], ["psW"])
                    OP("scalar", ACT(cl[:], psW[:], AF.Identity), ["psW"], ["cl"])
                    yield
                    OP("tensor", MM([(psW[:], allN, sg[:], True, True)]), ["cst", "zw"], ["psW"])
                    OP("vector", TT(tmp2[:], psW[:], cl[:], ALU.subtract), ["psW", "cl"], ["tmp2"])
                    OP("scalar", ACT(E4[:], tmp2[:], AF.Exp), ["tmp2"], ["tmp2"])
                    yield
                    OP("tensor", MM([(psW[0:64, hh:hh + 1], sg[:, hh * 64:(hh + 1) * 64], negcol[:], True, True) for hh in range(8)]),
                       ["zw", "negcol"], ["psW"])
                    OP("scalar", ACT(PC[:], psW[0:64, 0:8], AF.Exp), ["psW"], ["PC"])
                    yield
                    OP("scalar", ACT(E1[:], cl[:], AF.Exp), ["cl"], ["E1"])
                    OP("scalar", ACT(E2[:], cl[:], AF.Exp, scale=-1.0), ["cl"], ["E2"])
                    OP("vector", STT(tmp[:], sg[:], EXPM05, cl[:], ALU.mult, ALU.add), ["zw", "cl"], ["tmp"])
                    OP("scalar", ACT(E3[:], tmp[:], AF.Exp), ["tmp"], ["tmp"])
                    yield
                    OP("vector", TT(za[:], za[:], bc[:, 1, :], ALU.add), ["za", "bc"], ["za"])
                    OP("scalar", ACT(a_[:], za[:], AF.Sigmoid), ["za"], ["za"])
                    OP("vector", TT(kk[:], k_t[:], bc[:, 2, :], ALU.mult), ["k_t", "bc"], ["kk"])
                    yield
                    OP("gpsimd", TT(scr[:], kk[:], kk[:], ALU.mult), ["kk"], ["cl"])
                    OP("vector", RSUM(s8[:, :, 0], h3(scr[:])), ["cl"], ["s8"])
                    OP("scalar", ACT(s8[:, :, 0], s8[:, :, 0], AF.Sqrt), ["s8"], ["s8"])
                    yield
                    OP("vector", TS(s8[:, :, 0], s8[:, :, 0], 1e-12, None, ALU.max), ["s8"], ["s8"])
                    OP("vector", RECIP(s8[:, :, 0], s8[:, :, 0]), ["s8"], ["s8"])
                    OP("vector", TT(h3(kk[:]), h3(kk[:]), bc8(s8[:, :, 0]), ALU.mult), ["kk", "s8"], ["kk"])
                    yield
                    OP("vector", STT(scr[:], a_[:], -1.0, bc[:, 3, :], ALU.add, ALU.mult), ["za", "bc"], ["cl"])
                    OP("vector", STT(kmod[:], scr[:], 1.0, k_t[:], ALU.add, ALU.mult), ["cl", "k_t"], ["kmod"])
                    OP("gpsimd", TT(b_[:], kk[:], a_[:], ALU.mult), ["kk", "za"], ["b_"])
                    yield
                    if use_vres:
                        zv, vf = d["zv"], d["vf"]
                        OP("vector", TT(zv[:], zv[:], bc[:, 7, :], ALU.add), ["zv", "bc"], ["zv"])
                        OP("scalar", ACT(zv[:], zv[:], AF.Sigmoid), ["zv"], ["zv"])
                        OP("gpsimd", TT(vf[:], vf[:], v_t[:], ALU.subtract), ["vf", "v_t"], ["vf"])
                        yield
                        OP("vector", TT(vf[:], vf[:], zv[:], ALU.mult), ["vf", "zv"], ["vf"])
                        OP("vector", TT(v_t[:], v_t[:], vf[:], ALU.add), ["vf", "v_t"], ["v_t"])
                        yield
                    OP("vector", STT(At[:], kk[:], -1.0, E3[:], ALU.mult, ALU.mult), ["kk", "tmp"], ["At"])
                    OP("gpsimd", TT(Bt[:], b_[:], E2[:], ALU.mult), ["b_", "E2"], ["Bt"])
                    yield
                    OP("gpsimd", TT(Kt[:], kmod[:], E2[:], ALU.mult), ["kmod", "E2"], ["Kt"])
                    OP("gpsimd", TT(Rt[:], r_t[:], E1[:], ALU.mult), ["r_t", "E1"], ["Rt"])
                    yield
                    OP("vector", TT(Bh[:], b_[:], E4[:], ALU.mult), ["b_", "tmp2"], ["Bh"])
                    OP("gpsimd", TT(Kh[:], kmod[:], E4[:], ALU.mult), ["kmod", "tmp2"], ["Kh"])
                    OP("scalar", ACP(Vb[:], v_t[:]), ["v_t"], ["Vb"])
                    yield
                    for (src_, dst_, sk, dk) in ((At, ATf, "At", "ATf"), (Bt, BTf, "Bt", "BTf"), (Kt, KTf, "Kt", "KTf"), (Rt, RTf, "Rt", "RTf")):
                        OP("tensor", TRS([(psT[0:64, hh * 128:(hh + 1) * 128], src_[:, hh * 64:(hh + 1) * 64], ident[:]) for hh in range(8)]),
                           [sk, "ident"], ["psT"])
                        OP("scalar", ACP(dst_[:], v3(psT[0:64, 0:1024])), ["psT"], [dk])
                        yield
                    for (lf, rf, lk, rk_, mask, mk, dst_, dk, eng) in (
                            (ATf, BTf, "ATf", "BTf", mL, "mL", Mm[0], "Mm0", "vector"),
                            (BTf, ATf, "BTf", "ATf", mU, "mU", MT[0], "MT0", "gpsimd"),
                            (KTf, ATf, "KTf", "ATf", mU, "mU", AakT, "AakT", "vector"),
                            (BTf, RTf, "BTf", "RTf", mUi, "mUi", ArbT, "ArbT", "vector"),
                            (KTf, RTf, "KTf", "RTf", mUi, "mUi", ArkT, "ArkT", "vector")):
                        OP("tensor", MM([(psM[:, hh * 128:(hh + 1) * 128], lf[:, hh, :], rf[:, hh, :], True, True) for hh in range(8)]),
                           [lk, rk_], ["psM"])
                        if eng == "gpsimd":
                            OP("scalar", ACP(dst_[:], v3(psM[:])), ["psM"], [dk])
                            OP("gpsimd", TT(dst_[:], dst_[:], mb8(mask), ALU.mult), [dk, mk], [dk])
                        else:
                            OP("vector", TT(dst_[:], v3(psM[:]), mb8(mask), ALU.mult), ["psM", mk], [dk])
                        yield
                    OP("gpsimd", TT(TTm[:], MT[0][:], ident[:].unsqueeze(1).to_broadcast([128, 8, 128]), ALU.add), ["MT0", "ident"], ["TT"])
                    yield
                    cur = 0
                    for lev in range(1, 7):
                        nxt = 1 - cur
                        OP("tensor", MM([(psM[:, hh * 128:(hh + 1) * 128], MT[cur][:, hh, :], Mm[cur][:, hh, :], True, True) for hh in range(8)]),
                           ["MT%d" % cur, "Mm%d" % cur], ["psM"])
                        OP("scalar", ACP(Mm[nxt][:], v3(psM[:])), ["psM"], ["Mm%d" % nxt])
                        yield
                        if lev < 6:
                            OP("tensor", MM([(psM[:, hh * 128:(hh + 1) * 128], Mm[cur][:, hh, :], MT[cur][:, hh, :], True, True) for hh in range(8)]),
                               ["MT%d" % cur, "Mm%d" % cur], ["psM"])
                            OP("scalar", ACP(MT[nxt][:], v3(psM[:])), ["psM"], ["MT%d" % nxt])
                            yield
                        OP("tensor", MM([(psM[:, hh * 128:(hh + 1) * 128], Mm[nxt][:, hh, :], TTm[:, hh, :], True, True) for hh in range(8)]),
                           ["Mm%d" % nxt, "TT"], ["psM"])
                        OP("vector", TT(TTm[:], v3(psM[:]), TTm[:], ALU.add), ["psM", "TT"], ["TT"])
                        yield
                        cur = nxt
                    OP("tensor", MM([(psM[0:64, hh * 128:(hh + 1) * 128], At[:, hh * 64:(hh + 1) * 64], TTm[:, hh, :], True, True) for hh in range(8)]),
                       ["At", "TT"], ["psM"])
                    OP("scalar", ACP(WTf[:], v3(psM[0:64, :])), ["psM"], ["WTf"])
                    yield
                    OP("tensor", MM([(psW[:, hh * 64:(hh + 1) * 64], AakT[:, hh, :], Vb[:, hh * 64:(hh + 1) * 64], True, True) for hh in range(8)]),
                       ["AakT", "Vb"], ["psW"])
                    OP("scalar", ACP(AkV[:], psW[:]), ["psW"], ["AkV"])
                    yield
                    OP("tensor", MM([(psW[:, hh * 64:(hh + 1) * 64], TTm[:, hh, :], AkV[:, hh * 64:(hh + 1) * 64], True, True) for hh in range(8)]),
                       ["TT", "AkV"], ["psW"])
                    OP("scalar", ACP(Ut[:], psW[:]), ["psW"], ["Ut"])
                    yield
                    so = "Sb%d" % (tt % 2)
                    sn = "Sb%d" % ((tt + 1) % 2)
                    Sold = Sb[tt % 2]
                    Snew = Sb[(tt + 1) % 2]
                    OP("tensor", MM([(psW[:, hh * 64:(hh + 1) * 64], WTf[:, hh, :], Sold[:, hh, :], True, True) for hh in range(8)]),
                       ["WTf", so], ["psW"])
                    OP("vector", TT(Ub[:], psW[:], Ut[:], ALU.add), ["psW", "Ut"], ["Ub"])
                    yield
                    mm = []
                    for hh in range(8):
                        hs = slice(hh * 64, (hh + 1) * 64)
                        mm.append((psM[:, hs], RTf[:, hh, :], Sold[:, hh, :], True, False))
                        mm.append((psM[:, hs], ArbT[:, hh, :], Ub[:, hs], False, False))
                        mm.append((psM[:, hs], ArkT[:, hh, :], Vb[:, hs], False, True))
                    OP("tensor", MM(mm), ["RTf", so, "ArbT", "Ub", "ArkT", "Vb"], ["psM"])
                    mm = []
                    for hh in range(8):
                        hs = slice(hh * 64, (hh + 1) * 64)
                        mm.append((psW[0:64, hs], Kh[:, hs], Vb[:, hs], True, False))
                        mm.append((psW[0:64, hs], Bh[:, hs], Ub[:, hs], False, True))
                    OP("tensor", MM(mm), ["Kh", "Vb", "Bh", "Ub"], ["psW"])
                    yield
                    OP("vector", TT(S[:], S[:], PC[:].unsqueeze(2).to_broadcast([64, 8, 64]), ALU.mult), ["S", "PC"], ["S"])
                    OP("vector", TT(S[:], S[:], psW[0:64, :].rearrange("p (h v) -> p h v", v=64), ALU.add), ["S", "psW"], ["S"])
                    OP("scalar", ACP(Snew[:], S[:]), ["S"], [sn])
                    yield
                    OP("scalar", ACP(o_[:], psM[:, 0:512]), ["psM"], ["Ut"])
                    OP("vector", RSUM(s8[:, :, 1], h3(o_[:])), ["Ut"], ["s8"])
                    OP("vector", TS(s8[:, :, 1], s8[:, :, 1], 1.0 / 64.0), ["s8"], ["s8"])
                    yield
                    OP("vector", TT(h3(o_[:]), h3(o_[:]), bc8(s8[:, :, 1]), ALU.subtract), ["Ut", "s8"], ["Ut"])
                    OP("gpsimd", TT(tmp[:], o_[:], o_[:], ALU.mult), ["Ut"], ["tmp"])
                    OP("vector", RSUM(s8[:, :, 2], h3(tmp[:])), ["tmp"], ["s8"])
                    yield
                    OP("scalar", ACT(s8[:, :, 2], s8[:, :, 2], AF.Sqrt, bias=eps[:], scale=1.0 / 64.0), ["s8", "eps"], ["s8"])
                    OP("vector", RECIP(s8[:, :, 2], s8[:, :, 2]), ["s8"], ["s8"])
                    OP("vector", TT(h3(o_[:]), h3(o_[:]), bc8(s8[:, :, 2]), ALU.mult), ["Ut", "s8"], ["Ut"])
                    yield
                    OP("gpsimd", TT(tmp2[:], r_t[:], kmod[:], ALU.mult), ["r_t", "kmod"], ["tmp2"])
                    OP("gpsimd", TT(tmp2[:], tmp2[:], bc[:, 4, :], ALU.mult), ["tmp2", "bc"], ["tmp2"])
                    OP("vector", TT(o_[:], o_[:], bc[:, 5, :], ALU.mult), ["Ut", "bc"], ["Ut"])
                    OP("vector", TT(o_[:], o_[:], bc[:, 6, :], ALU.add), ["Ut", "bc"], ["Ut"])
                    yield
                    OP("vector", RSUM(s8[:, :, 3], h3(tmp2[:])), ["tmp2"], ["s8"])
                    OP("vector", TT(h3(tmp2[:]), h3(v_t[:]), bc8(s8[:, :, 3]), ALU.mult), ["v_t", "s8"], ["tmp2"])
                    OP("vector", TT(o_[:], o_[:], tmp2[:], ALU.add), ["Ut", "tmp2"], ["Ut"])
                    OP("vector", TT(og[:], o_[:], g_t[:], ALU.mult), ["Ut", "g_t"], ["og"])
                    yield
                    OP("tensor", TRS([(psT[:, c * 128:(c + 1) * 128], og[:, c * 128:(c + 1) * 128], ident[:]) for c in range(4)]),
                       ["og", "ident"], ["psT"])
                    OP("scalar", ACP(ogT[:], psT[:, 0:512].rearrange("p (c t) -> p c t", t=128)), ["psT"], ["ogT"])
                    P.dma(ogv[:, hg * 4:(hg + 1) * 4, rs_], ogT[:], reads=k("ogT"))
                    yield

            for pair in ((0, 1), (2, 3)):
                gens = [body(pair[0], 0), body(pair[1], 1)]
                alive = [True, True]
                while any(alive):
                    for gi in range(2):
                        if alive[gi]:
                            try:
                                next(gens[gi])
                            except StopIteration:
                                alive[gi] = False
            P.flush()

    def rwkv_layer(self, layer):
        i = layer // 2
        mu_base = 9 + 6 * i
        self.norm_phase(layer, "rwkv", mu_base=mu_base)
        self.lin_tm(self.xmix[0], self.w_rkv[i, 0], self.r_tm)
        self.lin_tm(self.xmix[2], self.w_rkv[i, 1], self.k_tm)
        self.lin_tm(self.xmix[3], self.w_rkv[i, 2], self.vf_tm if i == 0 else self.v_tm)
        self.lora_tm(self.xmix[1], self.w1[i], 96, AF.Tanh, self.w2[i], self.zw_tm)
        self.lora_tm(self.xmix[4], self.a1[i], 96, AF.Identity, self.a2[i], self.za_tm)
        self.lora_tm(self.xmix[5], self.g1[i], 256, AF.Sigmoid, self.g2[i], self.g_tm)
        if i > 0:
            self.lora_tm(self.xmix[3], self.v1[i - 1], 64, AF.Identity, self.v2[i - 1], self.zv_tm)

    def build(self):
        self.declare()
        with ExitStack() as gst:
            self.P = Prog(self.nc, gst)
            self.P.max_blocks = getattr(self, "max_blocks", 10 ** 9)
            plan = self.plan
            if "copy" in plan:
                self.copy_in()
            for layer in range(4):
                i = layer // 2
                if ("mix%d" % layer) in plan:
                    if layer % 2 == 0:
                        self.norm_phase(layer, "plain")
                        self.even_proj(i)
                        self.even_attn(i)
                        self.even_sgu(i)
                        self.proj_residual(self.ycat, self.w_out[i])
                    else:
                        self.rwkv_layer(layer)
                        if i == 0:
                            self._vsrc_first = True
                        self.rwkv_scan_wrap(i)
                        self.proj_residual(self.ycat, self.w_o[i])
                if ("ffn%d" % layer) in plan:
                    self.norm_phase(4 + layer, "plain")
                    self.ffn_phase(layer)
            if "final" in plan:
                self.norm_phase(8, "final")
        return self.nc

    def rwkv_scan_wrap(self, i):
        if i == 0:
            save = self.v_tm
            self.v_tm = self.vf_tm
            self.rwkv_scan(i, use_vres=False)
            self.v_tm = save
        else:
            self.rwkv_scan(i, use_vres=True)


def _fm16(v):
    return np.ascontiguousarray(v.reshape(-1, 128).T)


def make_shared(inp):
    f = np.float32
    sh = {}
    vfm = np.zeros((128, 21, 16), f)
    for l in range(4):
        vfm[:, l, :] = _fm16(inp["mix_norm_g"][l])
        vfm[:, 4 + l, :] = _fm16(inp["ffn_norm_g"][l])
    vfm[:, 8, :] = _fm16(inp["final_norm_g"])
    for i in range(2):
        for j in range(6):
            vfm[:, 9 + 6 * i + j, :] = _fm16(inp["rwkv_mu"][i, j])
    sh["vfm"] = vfm
    convp = np.zeros((128, 4, 4, 44), f)
    for l in range(4):
        for j in range(3):
            convp[:, l, j, :] = _fm16(inp["ffn_conv_w"][l, j])
        convp[:, l, 3, :] = _fm16(inp["ffn_conv_b"][l])
    sh["convp"] = convp
    rv = np.zeros((2, 8, D), f)
    for i in range(2):
        for j, nm in enumerate(["rwkv_w0", "rwkv_a0", "rwkv_k_k", "rwkv_k_a", "rwkv_r_k", "rwkv_gn_g", "rwkv_gn_b"]):
            rv[i, j] = inp[nm][i]
    rv[1, 7] = inp["rwkv_v0"][0]
    sh["rv"] = rv
    sh["sgu_ln"] = np.ascontiguousarray(np.stack([inp["sgu_ln_g"], inp["sgu_ln_b"]], axis=1)).astype(f)
    sh["sgu_wT"] = np.ascontiguousarray(np.transpose(inp["sgu_w"], (0, 1, 3, 2))).astype(f)
    sh["sgu_b"] = np.ascontiguousarray(inp["sgu_b"].reshape(2, 1024)).astype(f)
    perm = np.arange(1024)
    for h in range(16):
        base = h * 64
        perm[base:base + 8] = base + 8 + np.arange(8)
        perm[base + 8:base + 16] = base + np.arange(8)
    w_in = inp["even_w_in"]
    sh["w_in"] = np.ascontiguousarray(np.concatenate([w_in, w_in[:, :, 0:1024][:, :, perm], w_in[:, :, 1024:2048][:, :, perm]], axis=2))
    sh["w_out"] = inp["even_w_out"]
    sh["w_rkv"] = inp["rwkv_w_rkv"]
    sh["w_o"] = inp["rwkv_w_o"]
    for a, b in (("w1", "rwkv_w1"), ("w2", "rwkv_w2"), ("a1", "rwkv_a1"), ("a2", "rwkv_a2"), ("g1", "rwkv_g1"), ("g2", "rwkv_g2"),
                 ("v1", "rwkv_v1"), ("v2", "rwkv_v2"), ("w_up", "ffn_w_up"), ("w_down", "ffn_w_down")):
        sh[a] = inp[b]
    p = np.arange(128)[:, None]
    q = np.arange(128)[None, :]
    consts = np.zeros((128, 8, 128), f)
    consts[:, 0] = (p == q)
    consts[:, 1] = (p > q)
    consts[:, 2] = (q > p)
    consts[:, 3] = (q >= p)
    consts[:, 4] = np.where(q <= p, 0.0, NEG)
    consts[:, 5] = np.where(p <= q, -EXPM05, 0.0)
    consts[:, 6] = -EXPM05
    sh["consts"] = consts
    inv = np.power(np.float32(500000.0), -np.arange(0, 16, 2, dtype=f) / np.float32(16)).astype(f)
    ang = np.arange(T, dtype=f)[:, None] * inv[None, :]
    cos = np.cos(ang).astype(f).T
    sin = np.sin(ang).astype(f).T
    rope = np.zeros((2, 128, T), f)
    rope[0] = 1.0
    for hh in range(2):
        b = hh * 64
        rope[0, b:b + 8] = cos
        rope[0, b + 8:b + 16] = cos
        rope[1, b:b + 8] = -sin
        rope[1, b + 8:b + 16] = sin
    sh["rope"] = rope
    keep = np.ones((128, 16, 16), f)
    for qt in range(1, 16, 2):
        keep[:, qt, qt - 1] = 0.0
    sh["keepm"] = keep
    gm = np.zeros((128, 16, 8), f)
    for qt in range(16):
        gm[:, qt, qt // 2:] = -1e30
    sh["gmask"] = gm
    return {k: np.ascontiguousarray(v, dtype=np.float32) for k, v in sh.items()}


FULL_PLAN = ["copy"] + ["mix%d" % l for l in range(4)] + ["ffn%d" % l for l in range(4)] + ["final"]
_CACHE = {}


def run_plan(inp, plan, ncores=NCORES, max_blocks=None):
    key = tuple(plan) + (max_blocks,)
    if key not in _CACHE:
        bld = Builder(plan)
        if max_blocks is not None:
            bld.max_blocks = max_blocks
        _CACHE[key] = bld.build()
    nc = _CACHE[key]
    sh = make_shared(inp)
    x = inp["x"]
    in_maps = []
    for c in range(ncores):
        m = dict(sh)
        m["xT"] = np.ascontiguousarray(x[c].T)
        in_maps.append(m)
    res = run_bass_kernel_spmd(nc, in_maps, core_ids=list(range(ncores)))
    return res


def kernel(**inputs):
    inp = {k: np.asarray(v) for k, v in inputs.items()}
    res = run_plan(inp, FULL_PLAN)
    out = np.stack([np.ascontiguousarray(res.results[c]["out"].T) for c in range(NCORES)], axis=0)
    return out.astype(np.float32)
```

```python
import numpy as np
import ml_dtypes
import concourse.bass as bass
import concourse.mybir as mybir
from concourse.bass_utils import run_bass_kernel_spmd
from contextlib import ExitStack

F32 = mybir.dt.float32
BF16 = mybir.dt.bfloat16
ALU = mybir.AluOpType
AF = mybir.ActivationFunctionType
AX = mybir.AxisListType

D = 2048
T = 2048
DFF = 5632
NCORES = 8
COMPUTE = ("tensor", "vector", "scalar", "gpsimd")
SEM_ROLL = 30000
NEG = -30000.0
EXPM05 = float(np.exp(-0.5))


class Prog:
    def __init__(self, nc, stack, n_dma_sems=24):
        self.nc = nc
        self.stack = stack
        self.engs = {"tensor": nc.tensor, "vector": nc.vector, "scalar": nc.scalar,
                     "gpsimd": nc.gpsimd, "sync": nc.sync}
        self.sem_id = 0
        self.eng_sem = {}
        self.eng_cnt = {}
        for e in COMPUTE:
            self.eng_sem[e] = self._new_sem("c_" + e)
            self.eng_cnt[e] = 0
        self.dma_pool = {"sync": [[self._new_sem("d_sync%d" % i), 0] for i in range(n_dma_sems)],
                         "scalar": [[self._new_sem("d_act%d" % i), 0] for i in range(8)]}
        self.dma_rr = {"sync": 0, "scalar": 0}
        self.known = {e: {} for e in self.engs}
        self._reset()
        self.n_ops = 0
        self.n_waits = 0

    def _reset(self):
        self.ops = {e: [] for e in self.engs}
        self.last_write = {}
        self.readers = {}

    def _new_sem(self, name):
        self.sem_id += 1
        return self.stack.enter_context(self.nc.semaphore("%s_%d" % (name, self.sem_id)))

    def _deps(self, reads, writes):
        deps = set()
        for k in reads:
            deps |= self.last_write.get(k, set())
        for k in writes:
            deps |= self.last_write.get(k, set())
            deps |= self.readers.get(k, set())
        return deps

    def _waits(self, eng, deps):
        best = {}
        for (sem, val) in deps:
            if id(sem) not in best or best[id(sem)][1] < val:
                best[id(sem)] = (sem, val)
        out = []
        kn = self.known[eng]
        for sid, (sem, val) in best.items():
            if kn.get(sid, 0) >= val:
                continue
            kn[sid] = val
            out.append((sem, val))
        return out

    def _commit(self, ev, reads, writes):
        for k in reads:
            self.readers.setdefault(k, set()).add(ev)
        for k in writes:
            self.last_write[k] = {ev}
            self.readers[k] = set()

    def op(self, eng, fn, reads=(), writes=()):
        deps = self._deps(reads, writes)
        own = self.eng_sem[eng]
        if eng == "tensor":
            deps = {d for d in deps if d[0] is not own}
        waits = self._waits(eng, deps)
        if self.eng_cnt[eng] >= SEM_ROLL:
            self.eng_sem[eng] = self._new_sem("c_" + eng)
            self.eng_cnt[eng] = 0
        self.eng_cnt[eng] += 1
        ev = (self.eng_sem[eng], self.eng_cnt[eng])
        self.ops[eng].append((waits, fn, ev[0], 1))
        self._commit(ev, reads, writes)
        self.n_waits += len(waits)
        self.n_ops += 1
        return ev

    def dma(self, out, in_, reads=(), writes=(), q="sync"):
        deps = set(self._deps(reads, writes))
        pool = self.dma_pool[q]
        i = self.dma_rr[q]
        self.dma_rr[q] = (i + 1) % len(pool)
        slot = pool[i]
        if slot[1] > 0:
            deps.add((slot[0], slot[1]))
        if slot[1] >= SEM_ROLL:
            slot[0] = self._new_sem("d_%s" % q)
            slot[1] = 0
        sem = slot[0]
        waits = self._waits(q, deps)
        slot[1] += 16
        ev = (sem, slot[1])

        def fn(e, out=out, in_=in_):
            return e.dma_start(out=out, in_=in_)
        self.ops[q].append((waits, fn, sem, 16))
        self._commit(ev, reads, writes)
        self.n_waits += len(waits)
        self.n_ops += 1
        return ev

    def flush(self):
        self.n_blocks = getattr(self, "n_blocks", 0) + 1
        if self.n_blocks > getattr(self, "max_blocks", 10 ** 9):
            self._reset()
            return
        finals = set()
        for q, pool in self.dma_pool.items():
            for sem, cnt in pool:
                if cnt > 0:
                    finals.add((sem, cnt))
        for e in COMPUTE:
            if self.eng_cnt[e] > 0:
                finals.add((self.eng_sem[e], self.eng_cnt[e]))
        fw = self._waits("sync", finals)
        self.ops["sync"].append((fw, None, None, 0))
        ops = self.ops
        with self.nc.Block() as block:
            def mk(ename):
                def body(e):
                    for (waits, fn, sem, inc) in ops[ename]:
                        for (s, v) in waits:
                            e.wait_ge(s, v)
                        if fn is not None:
                            fn(e).then_inc(sem, inc)
                return body
            for ename in ("sync", "gpsimd", "scalar", "vector", "tensor"):
                if ops[ename]:
                    getattr(block, ename)(mk(ename))
        self._reset()


def TT(out, in0, in1, op):
    return lambda e: e.tensor_tensor(out=out, in0=in0, in1=in1, op=op)


def TS(out, in0, s1, s2=None, op0=ALU.mult, op1=None):
    if op1 is None:
        return lambda e: e.tensor_scalar(out=out, in0=in0, scalar1=s1, scalar2=None, op0=op0)
    return lambda e: e.tensor_scalar(out=out, in0=in0, scalar1=s1, scalar2=s2, op0=op0, op1=op1)


def STT(out, in0, scalar, in1, op0, op1):
    return lambda e: e.scalar_tensor_tensor(out=out, in0=in0, scalar=scalar, in1=in1, op0=op0, op1=op1)


def ACT(out, in_, func, bias=None, scale=None, accum=None):
    kw = {}
    if bias is not None:
        kw["bias"] = bias
    if scale is not None:
        kw["scale"] = scale
    if accum is not None:
        kw["accum_out"] = accum
    return lambda e: e.activation(out=out, in_=in_, func=func, **kw)


def CP(out, in_):
    return lambda e: e.tensor_copy(out=out, in_=in_)


def ACP(out, in_):
    return lambda e: e.copy(out=out, in_=in_)


def MM(lst):
    def fn(e):
        ins = None
        for (out, lhsT, rhs, start, stop) in lst:
            ins = e.matmul(out, lhsT=lhsT, rhs=rhs, start=start, stop=stop)
        return ins
    return fn


def TRS(lst):
    def fn(e):
        ins = None
        for (out, in_, ident) in lst:
            ins = e.transpose(out, in_, ident)
        return ins
    return fn


def RSUM(out, in_):
    return lambda e: e.reduce_sum(out=out, in_=in_, axis=AX.X)


def RMAX(out, in_):
    return lambda e: e.reduce_max(out=out, in_=in_, axis=AX.X)


def MSET(ap, val):
    return lambda e: e.memset(ap, val)


def RECIP(out, in_):
    return lambda e: e.reciprocal(out=out, in_=in_)


def MAX8(out, in_):
    return lambda e: e.max(out=out, in_=in_)


class Builder:
    def __init__(self, plan):
        self.plan = plan
        self.nc = bass.Bass("TRN2", target_bir_lowering=False)
        self.inputs = {}

    def din(self, name, shape, dt=F32):
        t = self.nc.dram_tensor(name, list(shape), dt, kind="ExternalInput").ap()
        self.inputs[name] = t
        return t

    def dscr(self, name, shape, dt):
        return self.nc.dram_tensor(name, list(shape), dt, kind="Internal").ap()

    def sb(self, st, name, shape, dt):
        self._uid += 1
        return st.enter_context(self.nc.sbuf_tensor("%s_%d" % (name, self._uid), list(shape), dt))

    def pp(self, st, name, shape, dt):
        self._uid += 1
        return st.enter_context(self.nc.psum_tensor("%s_%d" % (name, self._uid), list(shape), dt))

    def declare(self):
        nc = self.nc
        self._uid = 0
        self.xT = self.din("xT", [D, T])
        self.vfm = self.din("vfm", [128, 21, 16])
        self.convp = self.din("convp", [128, 4, 4, 44])
        self.rv = self.din("rv", [2, 8, D])
        self.sgu_ln = self.din("sgu_ln", [2, 2, 1024])
        self.sgu_wT = self.din("sgu_wT", [2, 8, 128, 128])
        self.sgu_b = self.din("sgu_b", [2, 1024])
        self.w_in = self.din("w_in", [2, D, 7168])
        self.w_out = self.din("w_out", [2, D, D])
        self.w_rkv = self.din("w_rkv", [2, 3, D, D])
        self.w_o = self.din("w_o", [2, D, D])
        self.w1 = self.din("w1", [2, D, 96])
        self.w2 = self.din("w2", [2, 96, D])
        self.a1 = self.din("a1", [2, D, 96])
        self.a2 = self.din("a2", [2, 96, D])
        self.g1 = self.din("g1", [2, D, 256])
        self.g2 = self.din("g2", [2, 256, D])
        self.v1 = self.din("v1", [1, D, 64])
        self.v2 = self.din("v2", [1, 64, D])
        self.w_up = self.din("w_up", [4, D, 2 * DFF])
        self.w_down = self.din("w_down", [4, DFF, D])
        self.consts = self.din("consts", [128, 8, 128])
        self.rope = self.din("rope", [2, 128, T])
        self.keep = self.din("keepm", [128, 16, 16])
        self.gmask = self.din("gmask", [128, 16, 8])
        self.out = nc.dram_tensor("out", [D, T], F32, kind="ExternalOutput").ap()
        self.xres = self.dscr("xres", [D, T], F32)
        self.hbuf = self.dscr("hbuf", [D, T], BF16)
        self.gT = self.dscr("gT", [DFF, T], BF16)
        self.qT = self.dscr("qT", [1024, T], BF16)
        self.kT = self.dscr("kT", [1024, T], BF16)
        self.vtm = self.dscr("vtm", [T, 1024], BF16)
        self.uT = self.dscr("uT", [1024, T], BF16)
        self.vn = self.dscr("vn", [T, 1024], BF16)
        self.ycat = self.dscr("ycat", [D, T], BF16)
        self.xmix = [self.dscr("xmix%d" % i, [D, T], BF16) for i in range(6)]
        self.r_tm = self.dscr("r_tm", [T, D], F32)
        self.k_tm = self.dscr("k_tm", [T, D], F32)
        self.v_tm = self.dscr("v_tm", [T, D], F32)
        self.vf_tm = self.dscr("vf_tm", [T, D], F32)
        self.zw_tm = self.dscr("zw_tm", [T, D], F32)
        self.za_tm = self.dscr("za_tm", [T, D], F32)
        self.zv_tm = self.dscr("zv_tm", [T, D], F32)
        self.g_tm = self.dscr("g_tm", [T, D], F32)

    def copy_in(self):
        P = self.P
        with ExitStack() as st:
            bufs = [self.sb(st, "cpy", [128, 16, 256], F32) for _ in range(2)]
            src = self.xT.rearrange("(c p) t -> p c t", p=128)
            dst = self.xres.rearrange("(c p) t -> p c t", p=128)
            for i in range(8):
                b = bufs[i % 2]
                P.dma(b[:], src[:, :, i * 256:(i + 1) * 256], writes=["cp%d" % (i % 2)])
                P.dma(dst[:, :, i * 256:(i + 1) * 256], b[:], reads=["cp%d" % (i % 2)])
            P.flush()

    def norm_phase(self, gidx, mode, mu_base=None):
        P = self.P
        TB = 256
        with ExitStack() as st:
            xt = [self.sb(st, "nx", [128, 16, TB], F32) for _ in range(2)]
            sq = self.sb(st, "nsq", [128, 16, TB], BF16)
            rstd = self.sb(st, "nrstd", [128, TB], F32)
            ones = self.sb(st, "nones", [128, 128], BF16)
            gv = self.sb(st, "ngv", [128, 21, 16], F32)
            eps = self.sb(st, "neps", [128, 1], F32)
            ps = self.pp(st, "nps", [128, 512], F32)
            P.op("vector", MSET(ones[:], 1.0), writes=["ones"])
            P.op("vector", MSET(eps[:], 1e-6), writes=["eps"])
            P.dma(gv[:], self.vfm, writes=["gv"])
            src = self.xres.rearrange("(c p) t -> p c t", p=128)
            if mode == "plain":
                ho = [self.sb(st, "nho", [128, 16, TB], BF16) for _ in range(2)]
                dst = self.hbuf.rearrange("(c p) t -> p c t", p=128)
            elif mode == "final":
                ho = [self.sb(st, "nho", [128, 16, TB], F32) for _ in range(2)]
                dst = self.out.rearrange("(c p) t -> p c t", p=128)
            else:
                hf = self.sb(st, "nhf", [128, 16, TB + 1], F32)
                dx = [self.sb(st, "ndx", [128, TB], F32) for _ in range(2)]
                xo = [self.sb(st, "nxo", [128, 16, TB], BF16) for _ in range(6)]
                dsts = [m.rearrange("(c p) t -> p c t", p=128) for m in self.xmix]
                P.op("vector", MSET(hf[:], 0.0), writes=["hf"])
            P.dma(xt[0][:], src[:, :, 0:TB], writes=["nx0"])
            for tb in range(T // TB):
                x_ = xt[tb % 2]
                xk = "nx%d" % (tb % 2)
                sl = slice(tb * TB, (tb + 1) * TB)
                if tb + 1 < T // TB:
                    P.dma(xt[(tb + 1) % 2][:], src[:, :, (tb + 1) * TB:(tb + 2) * TB], writes=["nx%d" % ((tb + 1) % 2)])
                P.op("scalar", ACT(sq[:], x_[:], AF.Square), reads=[xk], writes=["sq"])
                P.op("tensor", MM([(ps[:, 0:TB], ones[:], sq[:, c, :], c == 0, c == 15) for c in range(16)]),
                     reads=["ones", "sq"], writes=["ps"])
                P.op("scalar", ACT(rstd[:], ps[:, 0:TB], AF.Sqrt, bias=eps[:], scale=1.0 / D),
                     reads=["ps", "eps"], writes=["rstd"])
                P.op("vector", RECIP(rstd[:], rstd[:]), reads=["rstd"], writes=["rstd"])
                if mode in ("plain", "final"):
                    h_ = ho[tb % 2]
                    hk = "ho%d" % (tb % 2)
                    for c in range(16):
                        P.op("vector", STT(h_[:, c, :], x_[:, c, :], gv[:, gidx, c:c + 1], rstd[:], ALU.mult, ALU.mult),
                             reads=[xk, "gv", "rstd"], writes=[hk])
                    P.dma(dst[:, :, sl], h_[:], reads=[hk])
                else:
                    if tb > 0:
                        P.op("vector", CP(hf[:, :, 0:1], hf[:, :, TB:TB + 1]), reads=["hf"], writes=["hf"])
                    for c in range(16):
                        P.op("vector", STT(hf[:, c, 1:TB + 1], x_[:, c, :], gv[:, gidx, c:c + 1], rstd[:], ALU.mult, ALU.mult),
                             reads=[xk, "gv", "rstd"], writes=["hf"])
                    for c in range(16):
                        d_ = dx[c % 2]
                        dk = "dx%d" % (c % 2)
                        P.op("gpsimd", TT(d_[:], hf[:, c, 0:TB], hf[:, c, 1:TB + 1], ALU.subtract), reads=["hf"], writes=[dk])
                        for i in range(6):
                            P.op("vector", STT(xo[i][:, c, :], d_[:], gv[:, mu_base + i, c:c + 1], hf[:, c, 1:TB + 1],
                                               ALU.mult, ALU.add), reads=[dk, "hf", "gv"], writes=["xo%d" % i])
                    for i in range(6):
                        P.dma(dsts[i][:, :, sl], xo[i][:], reads=["xo%d" % i])
            P.flush()

    def load_w(self, wst, wbf, key, src_ap, KC, ncols, col0=0, ndma=4):
        P = self.P
        v = src_ap.rearrange("(c p) m -> p c m", p=128)
        step = (KC + ndma - 1) // ndma
        for c0 in range(0, KC, step):
            c1 = min(KC, c0 + step)
            P.dma(wst[:, c0:c1, col0:col0 + ncols], v[:, c0:c1, :], writes=[key + "s"])
        P.op("gpsimd", CP(wbf[:, 0:KC, col0:col0 + ncols], wst[:, 0:KC, col0:col0 + ncols]),
             reads=[key + "s"], writes=[key])

    def load_hT(self, hT, src, key="hT", KC=16, t0=0, tn=T):
        v = src.rearrange("(c p) t -> p c t", p=128)
        for c0 in range(0, KC, 4):
            c1 = min(KC, c0 + 4)
            self.P.dma(hT[:, c0:c1, 0:tn], v[:, c0:c1, t0:t0 + tn], writes=[key])

    FFN_A_NEW = False
    FFN_B_NEW = True

    def ffn_phase(self, l):
        self.ffn_A_v3(l)
        self.ffn_B_v2(l)

    def ffn_A_new(self, l):
        P = self.P
        with ExitStack() as st:
            hT = self.sb(st, "hT", [128, 16, T], BF16)
            wst = [self.sb(st, "wst", [128, 16, 256], F32) for _ in range(2)]
            wbf = [self.sb(st, "wbf", [128, 16, 256], BF16) for _ in range(2)]
            cv = [self.sb(st, "cv", [128, 1024], F32) for _ in range(2)]
            sl = [self.sb(st, "sl", [128, 1024], F32) for _ in range(2)]
            go = [self.sb(st, "go", [128, 1024], BF16) for _ in range(2)]
            bnd = self.sb(st, "bnd", [128, 2], F32)
            cp = self.sb(st, "cp", [128, 4, 4, 44], F32)
            psG = [self.pp(st, "psG", [128, 1024], F32) for _ in range(2)]
            psU = [self.pp(st, "psU", [128, 1024], F32) for _ in range(2)]
            P.dma(cp[:], self.convp, writes=["cp"])
            self.load_hT(hT, self.hbuf)

            def loadW(fc):
                b = fc % 2
                self.load_w(wst[b], wbf[b], "w%d" % b, self.w_up[l][:, fc * 128:(fc + 1) * 128], 16, 128, col0=0, ndma=2)
                self.load_w(wst[b], wbf[b], "w%du" % b, self.w_up[l][:, DFF + fc * 128:DFF + (fc + 1) * 128], 16, 128, col0=128, ndma=2)
            loadW(0)
            for fc in range(44):
                b = fc % 2
                wk = "w%d" % b
                if fc + 1 < 44:
                    loadW(fc + 1)
                w0 = cp[:, l, 0, fc:fc + 1]
                w1 = cp[:, l, 1, fc:fc + 1]
                w2 = cp[:, l, 2, fc:fc + 1]
                bb = cp[:, l, 3, fc:fc + 1]
                for th in range(2):
                    g_, u_ = psG[th], psU[th]
                    gk, uk = "psG%d" % th, "psU%d" % th
                    for tb in range(2):
                        ts = slice(tb * 512, (tb + 1) * 512)
                        hs = slice(th * 1024 + tb * 512, th * 1024 + (tb + 1) * 512)
                        P.op("tensor", MM([(g_[:, ts], wbf[b][:, c, 0:128], hT[:, c, hs], c == 0, c == 15) for c in range(16)]),
                             reads=[wk, "hT"], writes=[gk])
                    for tb in range(2):
                        ts = slice(tb * 512, (tb + 1) * 512)
                        hs = slice(th * 1024 + tb * 512, th * 1024 + (tb + 1) * 512)
                        P.op("tensor", MM([(u_[:, ts], wbf[b][:, c, 128:256], hT[:, c, hs], c == 0, c == 15) for c in range(16)]),
                             reads=[wk + "u", "hT"], writes=[uk])
                    c_ = cv[th]
                    ck = "cv%d" % th
                    P.op("vector", TS(c_[:], g_[:], w2, bb, ALU.mult, ALU.add), reads=[gk, "cp"], writes=[ck])
                    P.op("vector", STT(c_[:, 1:1024], g_[:, 0:1023], w1, c_[:, 1:1024], ALU.mult, ALU.add), reads=[gk, "cp", ck], writes=[ck])
                    P.op("vector", STT(c_[:, 2:1024], g_[:, 0:1022], w0, c_[:, 2:1024], ALU.mult, ALU.add), reads=[gk, "cp", ck], writes=[ck])
                    if th == 0:
                        P.op("scalar", ACP(bnd[:], g_[:, 1022:1024]), reads=[gk], writes=["bnd"])
                    else:
                        P.op("vector", STT(c_[:, 0:1], bnd[:, 1:2], w1, c_[:, 0:1], ALU.mult, ALU.add), reads=["bnd", "cp", ck], writes=[ck])
                        P.op("vector", STT(c_[:, 0:2], bnd[:, 0:2], w0, c_[:, 0:2], ALU.mult, ALU.add), reads=["bnd", "cp", ck], writes=[ck])
                    P.op("scalar", ACT(sl[th][:], c_[:], AF.Silu), reads=[ck], writes=["sl%d" % th])
                    P.op("vector", TT(go[th][:], sl[th][:], u_[:], ALU.mult), reads=["sl%d" % th, uk], writes=["go%d" % th])
                    P.dma(self.gT[fc * 128:(fc + 1) * 128, th * 1024:(th + 1) * 1024], go[th][:], reads=["go%d" % th])
            P.flush()
    def ffn_B_new(self, l):
        P = self.P
        with ExitStack() as st:
            gTs = self.sb(st, "gTs", [128, 44, 1024], BF16)
            wst = [self.sb(st, "wst", [128, 44, 128], F32) for _ in range(2)]
            wbf = [self.sb(st, "wbf", [128, 44, 128], BF16) for _ in range(2)]
            xr = [self.sb(st, "xr", [128, 1024], F32) for _ in range(2)]
            ps = [self.pp(st, "ps", [128, 1024], F32) for _ in range(2)]
            gv = self.gT.rearrange("(c p) t -> p c t", p=128)

            def loadB(j):
                th, dc = j // 16, j % 16
                b = j % 2
                self.load_w(wst[b], wbf[b], "w%d" % b, self.w_down[l][:, dc * 128:(dc + 1) * 128], 44, 128, ndma=8)
                P.dma(xr[b][:], self.xres[dc * 128:(dc + 1) * 128, th * 1024:(th + 1) * 1024], writes=["xr%d" % b])
            loadB(0)
            for j in range(32):
                th, dc = j // 16, j % 16
                b = j % 2
                if dc == 0:
                    for c0 in range(0, 44, 4):
                        P.dma(gTs[:, c0:c0 + 4, :], gv[:, c0:c0 + 4, th * 1024:(th + 1) * 1024], writes=["gTs"])
                if j + 1 < 32:
                    loadB(j + 1)
                for tb in range(2):
                    ts = slice(tb * 512, (tb + 1) * 512)
                    P.op("tensor", MM([(ps[b][:, ts], wbf[b][:, c, :], gTs[:, c, ts], c == 0, c == 43) for c in range(44)]),
                         reads=["w%d" % b, "gTs"], writes=["ps%d" % b])
                xs = self.xres[dc * 128:(dc + 1) * 128, th * 1024:(th + 1) * 1024]
                P.op("vector", TT(xr[b][:], xr[b][:], ps[b][:], ALU.add), reads=["xr%d" % b, "ps%d" % b], writes=["xr%d" % b])
                P.dma(xs, xr[b][:], reads=["xr%d" % b])
            P.flush()

    def ffn_B_v2(self, l):
        P = self.P
        with ExitStack() as st:
            gTs = self.sb(st, "gTs", [128, 44, 1024], BF16)
            wst = self.sb(st, "wst", [128, 44, 256], F32)
            wbf = [self.sb(st, "wbf", [128, 44, 256], BF16) for _ in range(2)]
            xr = [self.sb(st, "xr", [128, 1024], F32) for _ in range(2)]
            ps = [self.pp(st, "ps", [128, 1024], F32) for _ in range(2)]
            gv = self.gT.rearrange("(c p) t -> p c t", p=128)

            def loadW(j2):
                dp = j2 % 8
                b = j2 % 2
                v = self.w_down[l][:, dp * 256:(dp + 1) * 256].rearrange("(c p) m -> p c m", p=128)
                for c0 in range(0, 44, 4):
                    P.dma(wst[:, c0:c0 + 4, :], v[:, c0:c0 + 4, :], writes=["wst"])
                P.op("gpsimd", CP(wbf[b][:, 0:22, :], wst[:, 0:22, :]), reads=["wst"], writes=["w%da" % b])
                P.op("scalar", ACP(wbf[b][:, 22:44, :], wst[:, 22:44, :]), reads=["wst"], writes=["w%db" % b])

            def loadX(n):
                th, dc = n // 16, n % 16
                P.dma(xr[n % 2][:], self.xres[dc * 128:(dc + 1) * 128, th * 1024:(th + 1) * 1024], writes=["xr%d" % (n % 2)])
            loadW(0)
            loadX(0)
            for j2 in range(16):
                th, dp = j2 // 8, j2 % 8
                b = j2 % 2
                if dp == 0:
                    for c0 in range(0, 44, 4):
                        P.dma(gTs[:, c0:c0 + 4, :], gv[:, c0:c0 + 4, th * 1024:(th + 1) * 1024], writes=["gTs"])
                if j2 + 1 < 16:
                    loadW(j2 + 1)
                for mc in range(2):
                    n = th * 16 + dp * 2 + mc
                    dc = dp * 2 + mc
                    pb = n % 2
                    if n + 1 < 32:
                        loadX(n + 1)
                    for tb in range(2):
                        ts = slice(tb * 512, (tb + 1) * 512)
                        P.op("tensor", MM([(ps[pb][:, ts], wbf[b][:, c, mc * 128:(mc + 1) * 128], gTs[:, c, ts], c == 0, c == 43) for c in range(44)]),
                             reads=["w%da" % b, "w%db" % b, "gTs"], writes=["ps%d" % pb])
                    xs = self.xres[dc * 128:(dc + 1) * 128, th * 1024:(th + 1) * 1024]
                    P.op("vector", TT(xr[pb][:], xr[pb][:], ps[pb][:], ALU.add), reads=["xr%d" % pb, "ps%d" % pb], writes=["xr%d" % pb])
                    P.dma(xs, xr[pb][:], reads=["xr%d" % pb])
            P.flush()

    def ffn_A_old(self, l):
        P = self.P
        with ExitStack() as st:
            hT = self.sb(st, "hT", [128, 16, T], BF16)
            wst = [self.sb(st, "wst", [128, 16, 256], F32) for _ in range(2)]
            wbf = [self.sb(st, "wbf", [128, 16, 256], BF16) for _ in range(2)]
            cv = self.sb(st, "cv", [128, T], F32)
            sl = self.sb(st, "sl", [128, T], F32)
            go = [self.sb(st, "go", [128, T], BF16) for _ in range(2)]
            cp = self.sb(st, "cp", [128, 4, 4, 44], F32)
            psA = self.pp(st, "psA", [128, T], F32)
            psB = self.pp(st, "psB", [128, T], F32)
            P.dma(cp[:], self.convp, writes=["cp"])
            self.load_hT(hT, self.hbuf)
            for fc in range(44):
                b = fc % 2
                wk = "w%d" % b
                self.load_w(wst[b], wbf[b], wk, self.w_up[l][:, fc * 128:(fc + 1) * 128], 16, 128, col0=0, ndma=2)
                self.load_w(wst[b], wbf[b], wk + "u", self.w_up[l][:, DFF + fc * 128:DFF + (fc + 1) * 128], 16, 128, col0=128, ndma=2)
                for tb in range(4):
                    ts = slice(tb * 512, (tb + 1) * 512)
                    P.op("tensor", MM([(psA[:, ts], wbf[b][:, c, 0:128], hT[:, c, ts], c == 0, c == 15) for c in range(16)]),
                         reads=[wk, "hT"], writes=["psA"])
                for tb in range(4):
                    ts = slice(tb * 512, (tb + 1) * 512)
                    P.op("tensor", MM([(psB[:, ts], wbf[b][:, c, 128:256], hT[:, c, ts], c == 0, c == 15) for c in range(16)]),
                         reads=[wk + "u", "hT"], writes=["psB"])
                P.op("vector", TS(cv[:], psA[:], cp[:, l, 2, fc:fc + 1], cp[:, l, 3, fc:fc + 1], ALU.mult, ALU.add),
                     reads=["psA", "cp"], writes=["cv"])
                P.op("vector", STT(cv[:, 1:T], psA[:, 0:T - 1], cp[:, l, 1, fc:fc + 1], cv[:, 1:T], ALU.mult, ALU.add),
                     reads=["psA", "cp", "cv"], writes=["cv"])
                P.op("vector", STT(cv[:, 2:T], psA[:, 0:T - 2], cp[:, l, 0, fc:fc + 1], cv[:, 2:T], ALU.mult, ALU.add),
                     reads=["psA", "cp", "cv"], writes=["cv"])
                P.op("scalar", ACT(sl[:], cv[:], AF.Silu), reads=["cv"], writes=["sl"])
                gk = "go%d" % b
                P.op("vector", TT(go[b][:], sl[:], psB[:], ALU.mult), reads=["sl", "psB"], writes=[gk])
                P.dma(self.gT[fc * 128:(fc + 1) * 128, :], go[b][:], reads=[gk])
            P.flush()
    def ffn_A_v3(self, l):
        P = self.P
        with ExitStack() as st:
            hT = self.sb(st, "hT", [128, 16, T], BF16)
            wst = [self.sb(st, "wst", [128, 16, 256], F32) for _ in range(2)]
            wbf = [self.sb(st, "wbf", [128, 16, 256], BF16) for _ in range(2)]
            cv = self.sb(st, "cv", [128, T], F32)
            sl = self.sb(st, "sl", [128, T], F32)
            go = [self.sb(st, "go", [128, T], BF16) for _ in range(2)]
            cp = self.sb(st, "cp", [128, 4, 4, 44], F32)
            psA = self.pp(st, "psA", [128, T], F32)
            psB = self.pp(st, "psB", [128, T], F32)
            P.dma(cp[:], self.convp, writes=["cp"])
            self.load_hT(hT, self.hbuf)
            def loadW(fc):
                b = fc % 2
                self.load_w(wst[b], wbf[b], "w%d" % b, self.w_up[l][:, fc * 128:(fc + 1) * 128], 16, 128, col0=0, ndma=2)
                self.load_w(wst[b], wbf[b], "w%du" % b, self.w_up[l][:, DFF + fc * 128:DFF + (fc + 1) * 128], 16, 128, col0=128, ndma=2)
            loadW(0)
            for fc in range(44):
                b = fc % 2
                wk = "w%d" % b
                if fc + 1 < 44:
                    loadW(fc + 1)
                for tb in range(4):
                    ts = slice(tb * 512, (tb + 1) * 512)
                    P.op("tensor", MM([(psA[:, ts], wbf[b][:, c, 0:128], hT[:, c, ts], c == 0, c == 15) for c in range(16)]),
                         reads=[wk, "hT"], writes=["psA"])
                for tb in range(4):
                    ts = slice(tb * 512, (tb + 1) * 512)
                    P.op("tensor", MM([(psB[:, ts], wbf[b][:, c, 128:256], hT[:, c, ts], c == 0, c == 15) for c in range(16)]),
                         reads=[wk + "u", "hT"], writes=["psB"])
                P.op("vector", TS(cv[:], psA[:], cp[:, l, 2, fc:fc + 1], cp[:, l, 3, fc:fc + 1], ALU.mult, ALU.add),
                     reads=["psA", "cp"], writes=["cv"])
                P.op("vector", STT(cv[:, 1:T], psA[:, 0:T - 1], cp[:, l, 1, fc:fc + 1], cv[:, 1:T], ALU.mult, ALU.add),
                     reads=["psA", "cp", "cv"], writes=["cv"])
                P.op("vector", STT(cv[:, 2:T], psA[:, 0:T - 2], cp[:, l, 0, fc:fc + 1], cv[:, 2:T], ALU.mult, ALU.add),
                     reads=["psA", "cp", "cv"], writes=["cv"])
                P.op("scalar", ACT(sl[:], cv[:], AF.Silu), reads=["cv"], writes=["sl"])
                gk = "go%d" % b
                P.op("vector", TT(go[b][:], sl[:], psB[:], ALU.mult), reads=["sl", "psB"], writes=[gk])
                P.dma(self.gT[fc * 128:(fc + 1) * 128, :], go[b][:], reads=[gk])
            P.flush()
    def ffn_B_old(self, l):
        P = self.P
        with ExitStack() as st:
            gTs = self.sb(st, "gTs", [128, 44, 1024], BF16)
            wst = [self.sb(st, "wst", [128, 44, 128], F32) for _ in range(2)]
            wbf = [self.sb(st, "wbf", [128, 44, 128], BF16) for _ in range(2)]
            xr = [self.sb(st, "xr", [128, 1024], F32) for _ in range(2)]
            ps = [self.pp(st, "ps", [128, 1024], F32) for _ in range(2)]
            gv = self.gT.rearrange("(c p) t -> p c t", p=128)
            cnt = 0
            for th in range(2):
                for c0 in range(0, 44, 4):
                    P.dma(gTs[:, c0:c0 + 4, :], gv[:, c0:c0 + 4, th * 1024:(th + 1) * 1024], writes=["gTs"])
                for dc in range(16):
                    b = cnt % 2
                    cnt += 1
                    wk = "w%d" % b
                    self.load_w(wst[b], wbf[b], wk, self.w_down[l][:, dc * 128:(dc + 1) * 128], 44, 128, ndma=8)
                    for tb in range(2):
                        ts = slice(tb * 512, (tb + 1) * 512)
                        P.op("tensor", MM([(ps[b][:, ts], wbf[b][:, c, :], gTs[:, c, ts], c == 0, c == 43) for c in range(44)]),
                             reads=[wk, "gTs"], writes=["ps%d" % b])
                    xs = self.xres[dc * 128:(dc + 1) * 128, th * 1024:(th + 1) * 1024]
                    P.dma(xr[b][:], xs, writes=["xr%d" % b])
                    P.op("vector", TT(xr[b][:], xr[b][:], ps[b][:], ALU.add), reads=["xr%d" % b, "ps%d" % b], writes=["xr%d" % b])
                    P.dma(xs, xr[b][:], reads=["xr%d" % b])
            P.flush()

    def proj_residual(self, src, w_ap):
        P = self.P
        with ExitStack() as st:
            hT = self.sb(st, "hT", [128, 16, T], BF16)
            wst = [self.sb(st, "wst", [128, 16, 256], F32) for _ in range(2)]
            wbf = [self.sb(st, "wbf", [128, 16, 256], BF16) for _ in range(2)]
            xr = [self.sb(st, "xr", [128, T], F32) for _ in range(2)]
            ps = [self.pp(st, "ps", [128, T], F32) for _ in range(2)]
            self.load_hT(hT, src)

            def loadW(cb):
                b = cb % 2
                self.load_w(wst[b], wbf[b], "w%d" % b, w_ap[:, cb * 256:(cb + 1) * 256], 16, 256)

            def loadX(dc):
                P.dma(xr[dc % 2][:], self.xres[dc * 128:(dc + 1) * 128, :], writes=["xr%d" % (dc % 2)])
            loadW(0)
            loadX(0)
            for cb in range(8):
                b = cb % 2
                wk = "w%d" % b
                if cb + 1 < 8:
                    loadW(cb + 1)
                for mc in range(2):
                    dc = cb * 2 + mc
                    pb = dc % 2
                    if dc + 1 < 16:
                        loadX(dc + 1)
                    for tb in range(4):
                        ts = slice(tb * 512, (tb + 1) * 512)
                        P.op("tensor", MM([(ps[pb][:, ts], wbf[b][:, c, mc * 128:(mc + 1) * 128], hT[:, c, ts], c == 0, c == 15)
                                           for c in range(16)]), reads=[wk, "hT"], writes=["ps%d" % pb])
                    xs = self.xres[dc * 128:(dc + 1) * 128, :]
                    P.op("vector", TT(xr[pb][:], xr[pb][:], ps[pb][:], ALU.add), reads=["xr%d" % pb, "ps%d" % pb], writes=["xr%d" % pb])
                    P.dma(xs, xr[pb][:], reads=["xr%d" % pb])
            P.flush()

    def even_proj(self, i):
        P = self.P
        win = self.w_in[i]
        with ExitStack() as st:
            hT = self.sb(st, "hT", [128, 16, T], BF16)
            wst = [self.sb(st, "wst", [128, 16, 256], F32) for _ in range(2)]
            wbf = [self.sb(st, "wbf", [128, 16, 256], BF16) for _ in range(2)]
            rc = self.sb(st, "ropec", [128, T], F32)
            rs = self.sb(st, "ropes", [128, T], F32)
            t1 = self.sb(st, "t1", [128, T], F32)
            t2 = self.sb(st, "t2", [128, T], F32)
            ob = [self.sb(st, "ob", [128, T], BF16) for _ in range(2)]
            lng = self.sb(st, "lng", [128, 2, 1024], F32)
            st8 = self.sb(st, "st8", [128, 8, 2], F32)
            st9 = self.sb(st, "st9", [128, 8, 2], F32)
            eps = self.sb(st, "eps", [128, 1], F32)
            psA = self.pp(st, "psA", [128, T], F32)
            psB = self.pp(st, "psB", [128, T], F32)
            P.dma(rc[:], self.rope[0], writes=["rc"])
            P.dma(rs[:], self.rope[1], writes=["rs"])
            for a in range(2):
                P.dma(lng[:, a, :], self.sgu_ln[i, a, :].partition_broadcast(128), writes=["lng"])
            P.op("vector", MSET(eps[:], 1e-5), writes=["eps"])
            self.load_hT(hT, self.hbuf)
            jobs = []
            pss = [psA, psB]
            state = {"pcnt": 0}

            def mk_qk(c_main, c_perm, dst, j):
                def load(b):
                    self.load_w(wst[b], wbf[b], "w%d" % b, win[:, c_main + j * 128:c_main + (j + 1) * 128], 16, 128, col0=0, ndma=2)
                    self.load_w(wst[b], wbf[b], "w%du" % b, win[:, c_perm + j * 128:c_perm + (j + 1) * 128], 16, 128, col0=128, ndma=2)

                def comp(b):
                    wk = "w%d" % b
                    for tb in range(4):
                        ts = slice(tb * 512, (tb + 1) * 512)
                        P.op("tensor", MM([(psA[:, ts], wbf[b][:, c, 0:128], hT[:, c, ts], c == 0, c == 15) for c in range(16)]),
                             reads=[wk, "hT"], writes=["psA"])
                    for tb in range(4):
                        ts = slice(tb * 512, (tb + 1) * 512)
                        P.op("tensor", MM([(psB[:, ts], wbf[b][:, c, 128:256], hT[:, c, ts], c == 0, c == 15) for c in range(16)]),
                             reads=[wk + "u", "hT"], writes=["psB"])
                    P.op("vector", TT(t1[:], psA[:], rc[:], ALU.mult), reads=["psA", "rc"], writes=["t1"])
                    P.op("vector", TT(t2[:], psB[:], rs[:], ALU.mult), reads=["psB", "rs"], writes=["t2"])
                    P.op("gpsimd", TT(ob[b][:], t1[:], t2[:], ALU.add), reads=["t1", "t2"], writes=["ob%d" % b])
                    P.dma(dst[j * 128:(j + 1) * 128, :], ob[b][:], reads=["ob%d" % b])
                return load, comp

            def mk_u(j):
                def load(b):
                    self.load_w(wst[b], wbf[b], "w%d" % b, win[:, 3072 + j * 128:3072 + (j + 1) * 128], 16, 128, col0=0, ndma=2)

                def comp(b):
                    wk = "w%d" % b
                    ps_ = pss[j % 2]
                    pk = "psA" if j % 2 == 0 else "psB"
                    for tb in range(4):
                        ts = slice(tb * 512, (tb + 1) * 512)
                        P.op("tensor", MM([(ps_[:, ts], wbf[b][:, c, 0:128], hT[:, c, ts], c == 0, c == 15) for c in range(16)]),
                             reads=[wk, "hT"], writes=[pk])
                    P.op("scalar", ACT(ob[b][:], ps_[:], AF.Gelu), reads=[pk], writes=["ob%d" % b])
                    P.dma(self.uT[j * 128:(j + 1) * 128, :], ob[b][:], reads=["ob%d" % b])
                return load, comp

            def mk_tm(which, c0, dst, mb):
                dv = dst.rearrange("(i p) m -> p i m", p=128)

                def load(b):
                    self.load_w(wst[b], wbf[b], "w%d" % b, win[:, c0 + mb * 256:c0 + (mb + 1) * 256], 16, 256)

                def comp(b):
                    wk = "w%d" % b
                    for half in range(2):
                        pcnt = state["pcnt"]
                        ps_ = pss[pcnt % 2]
                        pk = "psA" if pcnt % 2 == 0 else "psB"
                        state["pcnt"] = pcnt + 1
                        for i8 in range(8):
                            tt = half * 8 + i8
                            P.op("tensor", MM([(ps_[:, i8 * 256:(i8 + 1) * 256], hT[:, c, tt * 128:(tt + 1) * 128], wbf[b][:, c, :], c == 0, c == 15)
                                               for c in range(16)]), reads=[wk, "hT"], writes=[pk])
                        o_ = ob[pcnt % 2]
                        ok_ = "ob%d" % (pcnt % 2)
                        if which == 0:
                            P.op("scalar", ACP(o_[:], ps_[:]), reads=[pk], writes=[ok_])
                        else:
                            P.op("scalar", ACT(t1[:], ps_[:], AF.Gelu), reads=[pk], writes=["t1"])
                            v4 = t1[:].rearrange("p (a g d) -> p a g d", a=8, g=2)
                            w4 = t2[:].rearrange("p (a g d) -> p a g d", a=8, g=2)
                            P.op("vector", RSUM(st8[:], v4), reads=["t1"], writes=["st8"])
                            P.op("vector", TS(st8[:], st8[:], 1.0 / 128.0), reads=["st8"], writes=["st8"])
                            P.op("vector", TT(w4, v4, st8[:].unsqueeze(3).to_broadcast([128, 8, 2, 128]), ALU.subtract),
                                 reads=["t1", "st8"], writes=["t2"])
                            P.op("gpsimd", TT(t1[:], t2[:], t2[:], ALU.mult), reads=["t2"], writes=["t1"])
                            P.op("vector", RSUM(st9[:], v4), reads=["t1"], writes=["st9"])
                            P.op("scalar", ACT(st9[:], st9[:], AF.Sqrt, bias=eps[:], scale=1.0 / 128.0), reads=["st9", "eps"], writes=["st9"])
                            P.op("vector", RECIP(st9[:], st9[:]), reads=["st9"], writes=["st9"])
                            P.op("vector", TT(w4, w4, st9[:].unsqueeze(3).to_broadcast([128, 8, 2, 128]), ALU.mult),
                                 reads=["t2", "st9"], writes=["t2"])
                            w3 = t2[:].rearrange("p (a m) -> p a m", a=8)
                            gsl = lng[:, 0, mb * 256:(mb + 1) * 256].unsqueeze(1).to_broadcast([128, 8, 256])
                            bsl = lng[:, 1, mb * 256:(mb + 1) * 256].unsqueeze(1).to_broadcast([128, 8, 256])
                            P.op("vector", TT(w3, w3, gsl, ALU.mult), reads=["t2", "lng"], writes=["t2"])
                            P.op("vector", TT(o_[:].rearrange("p (a m) -> p a m", a=8), w3, bsl, ALU.add), reads=["t2", "lng"], writes=[ok_])
                        P.dma(dv[:, half * 8:(half + 1) * 8, mb * 256:(mb + 1) * 256], o_[:].rearrange("p (a m) -> p a m", a=8), reads=[ok_])
                return load, comp

            for (c_main, c_perm, dst) in ((0, 5120, self.qT), (1024, 6144, self.kT)):
                for j in range(8):
                    jobs.append(mk_qk(c_main, c_perm, dst, j))
            for j in range(8):
                jobs.append(mk_u(j))
            for which, (c0, dst) in enumerate(((2048, self.vtm), (4096, self.vn))):
                for mb in range(4):
                    jobs.append(mk_tm(which, c0, dst, mb))
            jobs[0][0](0)
            for n, (ld, cmp_) in enumerate(jobs):
                if n + 1 < len(jobs):
                    jobs[n + 1][0]((n + 1) % 2)
                cmp_(n % 2)
            P.flush()

    def even_attn(self, i):
        P = self.P
        with ExitStack() as st:
            V = self.sb(st, "V", [128, 16, 1024], BF16)
            att = self.sb(st, "att", [128, 16, 1024], BF16)
            cst = self.sb(st, "cst", [128, 8, 128], F32)
            ident = self.sb(st, "ident", [128, 128], BF16)
            keep = self.sb(st, "keep", [128, 16, 16], F32)
            gmask = self.sb(st, "gmask", [128, 16, 8], F32)
            qh = [self.sb(st, "qh", [64, T], BF16) for _ in range(2)]
            kh = [self.sb(st, "kh", [64, T], BF16) for _ in range(2)]
            km = self.sb(st, "km", [64, 8], F32)
            kmb = self.sb(st, "kmb", [64, 8], BF16)
            gate = self.sb(st, "gate", [128, 16, 8], F32)
            m8 = self.sb(st, "m8", [128, 16, 8], F32)
            bias8 = self.sb(st, "bias8", [128, 16, 8], F32)
            bias16 = self.sb(st, "bias16", [128, 16, 16], F32)
            sc = [self.sb(st, "sc", [128, T], F32) for _ in range(2)]
            pb = [self.sb(st, "pb", [128, T], BF16) for _ in range(2)]
            pT = [self.sb(st, "pT", [128, 16, 128], BF16) for _ in range(2)]
            mx = [self.sb(st, "mx", [128, 4], F32) for _ in range(3)]
            psS = self.pp(st, "psS", [128, T], F32)
            psT = self.pp(st, "psT", [128, T], BF16)
            psG = self.pp(st, "psG", [128, 512], F32)
            psO = self.pp(st, "psO", [128, 512], F32)
            P.dma(V[:], self.vtm.rearrange("(i p) m -> p i m", p=128), writes=["V"])
            P.dma(cst[:], self.consts, writes=["cst"])
            P.dma(keep[:], self.keep, writes=["keep"])
            P.dma(gmask[:], self.gmask, writes=["gmask"])
            P.op("vector", CP(ident[:], cst[:, 0, :]), reads=["cst"], writes=["ident"])
            causal = cst[:, 4, :]

            def load_qk(h):
                b = h % 2
                P.dma(qh[b][:], self.qT[h * 64:(h + 1) * 64, :], writes=["qh%d" % b])
                P.dma(kh[b][:], self.kT[h * 64:(h + 1) * 64, :], writes=["kh%d" % b])

            def S1(h, qt):
                b = h % 2
                par = qt % 2
                qk, kk_ = "qh%d" % b, "kh%d" % b
                sc_, mx_ = sc[par], mx[qt % 3]
                sk, mk = "sc%d" % par, "mx%d" % (qt % 3)
                nk = (qt + 1) * 128
                q_sl = qh[b][:, qt * 128:(qt + 1) * 128]
                mms = []
                for n0 in range(0, nk, 512):
                    n1 = min(nk, n0 + 512)
                    mms.append((psS[:, n0:n1], q_sl, kh[b][:, n0:n1], True, True))
                P.op("tensor", MM(mms), reads=[qk, kk_], writes=["psS"])
                if qt > 0:
                    P.op("vector", STT(sc_[:, 0:qt * 128].rearrange("p (a k) -> p a k", k=128),
                                       psS[:, 0:qt * 128].rearrange("p (a k) -> p a k", k=128), 0.125,
                                       bias16[:, qt, 0:qt].unsqueeze(2).to_broadcast([128, qt, 128]), ALU.mult, ALU.add),
                         reads=["psS", "bias16"], writes=[sk])
                P.op("vector", STT(sc_[:, qt * 128:nk], psS[:, qt * 128:nk], 0.125, causal, ALU.mult, ALU.add),
                     reads=["psS", "cst"], writes=[sk])
                P.op("vector", RMAX(mx_[:, 0:1], sc_[:, 0:nk]), reads=[sk], writes=[mk])
                P.op("vector", TS(mx_[:, 1:2], mx_[:, 0:1], -1.0), reads=[mk], writes=[mk])

            def S2(h, qt):
                par = qt % 2
                sc_, pb_, mx_ = sc[par], pb[par], mx[qt % 3]
                sk, pk, mk = "sc%d" % par, "pb%d" % par, "mx%d" % (qt % 3)
                nk = (qt + 1) * 128
                P.op("scalar", ACT(pb_[:, 0:nk], sc_[:, 0:nk], AF.Exp, bias=mx_[:, 1:2], accum=mx_[:, 2:3]), reads=[sk, mk], writes=[pk, mk])
                P.op("vector", RECIP(mx_[:, 3:4], mx_[:, 2:3]), reads=[mk], writes=[mk])
                P.op("tensor", TRS([(psT[:, kt * 128:(kt + 1) * 128], pb_[:, kt * 128:(kt + 1) * 128], ident[:]) for kt in range(qt + 1)]),
                     reads=[pk, "ident"], writes=["psT"])
                P.op("scalar", ACP(pT[par][:, 0:qt + 1, :], psT[:, 0:nk].rearrange("p (a k) -> p a k", k=128)), reads=["psT"], writes=["pT%d" % par])

            def S3(h, qt):
                par = qt % 2
                mx_ = mx[qt % 3]
                mk = "mx%d" % (qt % 3)
                P.op("tensor", MM([(psO[:, 0:64], pT[par][:, kt, :], V[:, kt, h * 64:(h + 1) * 64], kt == 0, kt == qt) for kt in range(qt + 1)]),
                     reads=["pT%d" % par, "V"], writes=["psO"])
                P.op("vector", TS(att[:, qt, h * 64:(h + 1) * 64], psO[:, 0:64], mx_[:, 3:4]), reads=["psO", mk], writes=["att"])

            load_qk(0)
            par = 0
            for h in range(16):
                b = h % 2
                qk, kk_ = "qh%d" % b, "kh%d" % b
                if h + 1 < 16:
                    load_qk(h + 1)
                P.op("vector", RSUM(km[:], kh[b][:].rearrange("p (n k) -> p n k", k=256)), reads=[kk_], writes=["km"])
                P.op("vector", TS(kmb[:], km[:], 1.0 / 256.0), reads=["km"], writes=["kmb"])
                P.op("tensor", MM([(psG[:, qt * 8:(qt + 1) * 8], qh[b][:, qt * 128:(qt + 1) * 128], kmb[:], True, True) for qt in range(16)]),
                     reads=[qk, "kmb"], writes=["psG"])
                P.op("vector", TT(gate[:], psG[:, 0:128].rearrange("p (a n) -> p a n", n=8), gmask[:], ALU.add),
                     reads=["psG", "gmask"], writes=["gate"])
                for qt in range(16):
                    P.op("vector", MAX8(m8[:, qt, :], gate[:, qt, :]), reads=["gate"], writes=["m8"])
                P.op("vector", TT(bias8[:], gate[:], m8[:, :, 2:3].to_broadcast([128, 16, 8]), ALU.is_ge), reads=["gate", "m8"], writes=["bias8"])
                P.op("vector", TS(bias8[:], bias8[:], -1.0, -NEG, ALU.add, ALU.mult), reads=["bias8"], writes=["bias8"])
                P.op("vector", CP(bias16[:].rearrange("p a (n two) -> p a n two", two=2), bias8[:].unsqueeze(3).to_broadcast([128, 16, 8, 2])),
                     reads=["bias8"], writes=["bias16"])
                P.op("vector", TT(bias16[:], bias16[:], keep[:], ALU.mult), reads=["bias16", "keep"], writes=["bias16"])
                S1(h, 0)
                S1(h, 1)
                S2(h, 0)
                for qt in range(16):
                    if qt + 2 < 16:
                        S1(h, qt + 2)
                    if qt + 1 < 16:
                        S2(h, qt + 1)
                    S3(h, qt)
            yv = self.ycat.rearrange("(c p) t -> p c t", p=128)
            aT = [self.sb(st, "aT", [128, T], BF16) for _ in range(2)]
            for c in range(8):
                P.op("tensor", TRS([(psT[:, qt * 128:(qt + 1) * 128], att[:, qt, c * 128:(c + 1) * 128], ident[:]) for qt in range(16)]),
                     reads=["att", "ident"], writes=["psT"])
                P.op("scalar", ACP(aT[c % 2][:], psT[:]), reads=["psT"], writes=["aT%d" % (c % 2)])
                P.dma(yv[:, c, :], aT[c % 2][:], reads=["aT%d" % (c % 2)])
            P.flush()

    def even_sgu(self, i):
        P = self.P
        with ExitStack() as st:
            vn = self.sb(st, "vn", [128, 16, 1024], BF16)
            uT = self.sb(st, "uT", [128, 8, T], BF16)
            wsf = self.sb(st, "wsf", [128, 8, 128], F32)
            wsb = self.sb(st, "wsb", [128, 8, 128], BF16)
            cst = self.sb(st, "cst", [128, 8, 128], F32)
            bsb = self.sb(st, "bsb", [128, 8, 128], F32)
            tmp = self.sb(st, "tmp", [128, T], F32)
            ob = [self.sb(st, "ob", [128, T], BF16) for _ in range(2)]
            ps = [self.pp(st, "ps", [128, T], F32) for _ in range(2)]
            P.dma(vn[:], self.vn.rearrange("(i p) m -> p i m", p=128), writes=["vn"])
            P.dma(uT[:], self.uT.rearrange("(c p) t -> p c t", p=128), writes=["uT"])
            P.dma(wsf[:], self.sgu_wT[i].rearrange("g s t -> s g t"), writes=["wsf"])
            P.dma(cst[:], self.consts, writes=["cst"])
            P.dma(bsb[:].rearrange("p g t -> p (g t)"), self.sgu_b[i, :].partition_broadcast(128), writes=["bsb"])
            P.op("vector", TT(wsb[:], wsf[:], cst[:, 3, :].unsqueeze(1).to_broadcast([128, 8, 128]), ALU.mult),
                 reads=["wsf", "cst"], writes=["wsb"])
            yv = self.ycat.rearrange("(c p) t -> p c t", p=128)
            for g in range(8):
                b = g % 2
                P.op("tensor", MM([(ps[b][:, c * 128:(c + 1) * 128], vn[:, c, g * 128:(g + 1) * 128], wsb[:, g, :], True, True) for c in range(16)]),
                     reads=["vn", "wsb"], writes=["ps%d" % b])
                P.op("vector", TT(tmp[:].rearrange("p (c t) -> p c t", t=128), ps[b][:].rearrange("p (c t) -> p c t", t=128),
                                  bsb[:, g, :].unsqueeze(1).to_broadcast([128, 16, 128]), ALU.add), reads=["ps%d" % b, "bsb"], writes=["tmp"])
                P.op("vector", TT(ob[b][:], tmp[:], uT[:, g, :], ALU.mult), reads=["tmp", "uT"], writes=["ob%d" % b])
                P.dma(yv[:, 8 + g, :], ob[b][:], reads=["ob%d" % b])
            P.flush()

    def lin_tm(self, src, w_ap, dst, KC=16):
        P = self.P
        with ExitStack() as st:
            hT = self.sb(st, "hT", [128, 16, T], BF16)
            wst = [self.sb(st, "wst", [128, 16, 256], F32) for _ in range(2)]
            wbf = [self.sb(st, "wbf", [128, 16, 256], BF16) for _ in range(2)]
            ob = [self.sb(st, "ob", [128, T], F32) for _ in range(2)]
            ps = [self.pp(st, "ps", [128, T], F32) for _ in range(2)]
            self.load_hT(hT, src)
            dv = dst.rearrange("(i p) m -> p i m", p=128)
            pcnt = 0
            self.load_w(wst[0], wbf[0], "w0", w_ap[:, 0:256], 16, 256)
            for mb in range(8):
                b = mb % 2
                wk = "w%d" % b
                if mb + 1 < 8:
                    self.load_w(wst[1 - b], wbf[1 - b], "w%d" % (1 - b), w_ap[:, (mb + 1) * 256:(mb + 2) * 256], 16, 256)
                for half in range(2):
                    pbk = pcnt % 2
                    pcnt += 1
                    for i8 in range(8):
                        tt = half * 8 + i8
                        P.op("tensor", MM([(ps[pbk][:, i8 * 256:(i8 + 1) * 256], hT[:, c, tt * 128:(tt + 1) * 128], wbf[b][:, c, :], c == 0, c == 15)
                                           for c in range(16)]), reads=[wk, "hT"], writes=["ps%d" % pbk])
                    eng = "scalar" if pbk == 0 else "vector"
                    P.op(eng, (ACP if eng == "scalar" else CP)(ob[pbk][:], ps[pbk][:]), reads=["ps%d" % pbk], writes=["ob%d" % pbk])
                    P.dma(dv[:, half * 8:(half + 1) * 8, mb * 256:(mb + 1) * 256], ob[pbk][:].rearrange("p (a m) -> p a m", a=8), reads=["ob%d" % pbk])
            P.flush()

    def lora_tm(self, src, w1_ap, R, func, w2_ap, dst):
        P = self.P
        RC = (R + 127) // 128
        rows = [min(128, R - rc * 128) for rc in range(RC)]
        with ExitStack() as st:
            hT = self.sb(st, "hT", [128, 16, T], BF16)
            w1s = self.sb(st, "w1s", [128, 16, R], F32)
            w1b = self.sb(st, "w1b", [128, 16, R], BF16)
            w2s = self.sb(st, "w2s", [128, RC, D], F32)
            w2b = self.sb(st, "w2b", [128, RC, D], BF16)
            lT = self.sb(st, "lT", [128, RC, T], BF16)
            ob = [self.sb(st, "ob", [128, T], F32) for _ in range(2)]
            ps = [self.pp(st, "ps", [128, T], F32) for _ in range(2)]
            self.load_hT(hT, src)
            self.load_w(w1s, w1b, "w1", w1_ap, 16, R)
            for rc in range(RC):
                P.dma(w2s[0:rows[rc], rc, :], w2_ap[rc * 128:rc * 128 + rows[rc], :], writes=["w2s"])
                P.op("gpsimd", CP(w2b[0:rows[rc], rc, :], w2s[0:rows[rc], rc, :]), reads=["w2s"], writes=["w2b"])
            for rc in range(RC):
                pbk = rc % 2
                for tb in range(4):
                    ts = slice(tb * 512, (tb + 1) * 512)
                    P.op("tensor", MM([(ps[pbk][0:rows[rc], ts], w1b[:, c, rc * 128:rc * 128 + rows[rc]], hT[:, c, ts], c == 0, c == 15)
                                       for c in range(16)]), reads=["w1", "hT"], writes=["ps%d" % pbk])
                P.op("scalar", ACT(lT[0:rows[rc], rc, :], ps[pbk][0:rows[rc], :], func), reads=["ps%d" % pbk], writes=["lT"])
            dv = dst.rearrange("(i p) m -> p i m", p=128)
            pcnt = 0
            for mb in range(8):
                for half in range(2):
                    pbk = pcnt % 2
                    pcnt += 1
                    for i8 in range(8):
                        tt = half * 8 + i8
                        P.op("tensor", MM([(ps[pbk][:, i8 * 256:(i8 + 1) * 256], lT[0:rows[rc], rc, tt * 128:(tt + 1) * 128],
                                            w2b[0:rows[rc], rc, mb * 256:(mb + 1) * 256], rc == 0, rc == RC - 1) for rc in range(RC)]),
                             reads=["lT", "w2b"], writes=["ps%d" % pbk])
                    eng = "scalar" if pbk == 0 else "vector"
                    P.op(eng, (ACP if eng == "scalar" else CP)(ob[pbk][:], ps[pbk][:]), reads=["ps%d" % pbk], writes=["ob%d" % pbk])
                    P.dma(dv[:, half * 8:(half + 1) * 8, mb * 256:(mb + 1) * 256], ob[pbk][:].rearrange("p (a m) -> p a m", a=8), reads=["ob%d" % pbk])
            P.flush()

    def rwkv_scan(self, i, use_vres):
        P = self.P
        W = 512
        vsrc = self.v_tm
        with ExitStack() as st:
            cst = self.sb(st, "cst", [128, 8, 128], F32)
            ident = self.sb(st, "ident", [128, 128], BF16)
            mL = self.sb(st, "mL", [128, 128], BF16)
            mU = self.sb(st, "mU", [128, 128], BF16)
            mUi = self.sb(st, "mUi", [128, 128], BF16)
            negcol = self.sb(st, "negcol", [128, 1], F32)
            eps = self.sb(st, "eps", [128, 1], F32)
            P.dma(cst[:], self.consts, writes=["cst"])
            P.op("vector", CP(ident[:], cst[:, 0, :]), reads=["cst"], writes=["ident"])
            P.op("vector", CP(mL[:], cst[:, 1, :]), reads=["cst"], writes=["mL"])
            P.op("vector", CP(mU[:], cst[:, 2, :]), reads=["cst"], writes=["mU"])
            P.op("vector", CP(mUi[:], cst[:, 3, :]), reads=["cst"], writes=["mUi"])
            P.op("vector", MSET(negcol[:], -EXPM05), writes=["negcol"])
            P.op("vector", MSET(eps[:], 64e-5), writes=["eps"])
            triN = cst[:, 5, :]
            allN = cst[:, 6, :]
            ogv = self.ycat.rearrange("(c p) t -> p c t", p=128)

            def h3(ap):
                return ap.rearrange("p (h d) -> p h d", d=64)

            def bc8(ap, n=64):
                return ap.unsqueeze(2).to_broadcast([128, 8, n])

            def mb8(m):
                return m[:].unsqueeze(1).to_broadcast([128, 8, 128])

            def v3(ap):
                return ap.rearrange("p (h t) -> p h t", t=128)

            sets = []
            for si in range(2):
                d = {}
                for nm in ("r_t", "k_t", "v_t", "zw", "za", "g_t", "cl", "E1", "E2", "tmp", "tmp2", "kk", "kmod", "b_", "Ut"):
                    d[nm] = self.sb(st, nm, [128, W], F32)
                if use_vres:
                    d["zv"] = self.sb(st, "zv", [128, W], F32)
                    d["vf"] = self.sb(st, "vf", [128, W], F32)
                for nm in ("At", "Bt", "Kt", "Rt", "Bh", "Kh", "Vb", "AkV", "Ub", "og"):
                    d[nm] = self.sb(st, nm, [128, W], BF16)
                for nm in ("ATf", "BTf", "KTf", "RTf", "WTf"):
                    d[nm] = self.sb(st, nm, [64, 8, 128], BF16)
                for nm in ("Mm0", "Mm1", "MT0", "MT1", "TT", "AakT", "ArbT", "ArkT"):
                    d[nm] = self.sb(st, nm, [128, 8, 128], BF16)
                d["bc"] = self.sb(st, "bc", [128, 8, W], F32)
                d["s8"] = self.sb(st, "s8", [128, 8, 4], F32)
                d["PC"] = self.sb(st, "PC", [64, 8], F32)
                d["S"] = self.sb(st, "S", [64, 8, 64], F32)
                d["Sb0"] = self.sb(st, "Sb0", [64, 8, 64], BF16)
                d["Sb1"] = self.sb(st, "Sb1", [64, 8, 64], BF16)
                d["ogT"] = self.sb(st, "ogT", [128, 4, 128], BF16)
                d["psM"] = self.pp(st, "psM", [128, 1024], F32)
                d["psW"] = self.pp(st, "psW", [128, 512], F32)
                d["psT"] = self.pp(st, "psT", [128, 1024], BF16)
                sets.append(d)

            def body(hg, si):
                d = sets[si]
                sfx = "_%d" % si

                def k(*names):
                    return [n + sfx if n not in ("cst", "ident", "mL", "mU", "mUi", "negcol", "eps") else n for n in names]

                def OP(eng, fn, r, w):
                    P.op(eng, fn, reads=k(*r), writes=k(*w))
                r_t, k_t, v_t, zw, za, g_t = d["r_t"], d["k_t"], d["v_t"], d["zw"], d["za"], d["g_t"]
                cl, E1, E2, tmp, tmp2, kk, kmod, b_, Ut = d["cl"], d["E1"], d["E2"], d["tmp"], d["tmp2"], d["kk"], d["kmod"], d["b_"], d["Ut"]
                At, Bt, Kt, Rt, Bh, Kh, Vb, AkV, Ub, og = (d[n] for n in ("At", "Bt", "Kt", "Rt", "Bh", "Kh", "Vb", "AkV", "Ub", "og"))
                ATf, BTf, KTf, RTf, WTf = (d[n] for n in ("ATf", "BTf", "KTf", "RTf", "WTf"))
                Mm = [d["Mm0"], d["Mm1"]]
                MT = [d["MT0"], d["MT1"]]
                TTm, AakT, ArbT, ArkT = d["TT"], d["AakT"], d["ArbT"], d["ArkT"]
                bc, s8, PC, S, ogT = d["bc"], d["s8"], d["PC"], d["S"], d["ogT"]
                Sb = [d["Sb0"], d["Sb1"]]
                psM, psW, psT = d["psM"], d["psW"], d["psT"]
                sg, a_, E3, E4, scr, o_ = zw, za, tmp, tmp2, cl, Ut
                cs = slice(hg * W, (hg + 1) * W)
                for j in range(8):
                    P.dma(bc[:, j, :], self.rv[i, j, cs].partition_broadcast(128), writes=k("bc"))
                OP("vector", MSET(S[:], 0.0), [], ["S"])
                OP("vector", MSET(Sb[0][:], 0.0), [], ["Sb0"])
                yield
                for tt in range(16):
                    rs_ = slice(tt * 128, (tt + 1) * 128)
                    for (tile_, src, key) in ((r_t, self.r_tm, "r_t"), (k_t, self.k_tm, "k_t"), (v_t, vsrc, "v_t"),
                                              (zw, self.zw_tm, "zw"), (za, self.za_tm, "za"), (g_t, self.g_tm, "g_t")):
                        P.dma(tile_[:], src[rs_, cs], writes=k(key))
                    if use_vres:
                        P.dma(d["zv"][:], self.zv_tm[rs_, cs], writes=k("zv"))
                        P.dma(d["vf"][:], self.vf_tm[rs_, cs], writes=k("vf"))
                    yield
                    OP("vector", TT(zw[:], zw[:], bc[:, 0, :], ALU.add), ["zw", "bc"], ["zw"])
                    OP("scalar", ACT(sg[:], zw[:], AF.Sigmoid), ["zw"], ["zw"])
                    yield
                    OP("tensor", MM([(psW[:], triN, sg[:], True, True)]), ["cst", "zw"], ["psW"])
                    OP("scalar", ACT(cl[:], psW[:], AF.Identity), ["psW"], ["cl"])
                    yield
                    OP("tensor", MM([(psW[:], allN, sg[:], True, True)]), ["cst", "zw"], ["psW"])
                    OP("vector", TT(tmp2[:], psW[:], cl[:], ALU.subtract), ["psW", "cl"], ["tmp2"])
                    OP("scalar", ACT(E4[:], tmp2[:], AF.Exp), ["tmp2"], ["tmp2"])
                    yield
                    OP("tensor", MM([(psW[0:64, hh:hh + 1], sg[:, hh * 64:(hh + 1) * 64], negcol[:], True, True) for hh in range(8)]),
                       ["zw", "negcol"], ["psW"])
                    OP("scalar", ACT(PC[:], psW[0:64, 0:8], AF.Exp), ["psW"], ["PC"])
                    yield
                    OP("scalar", ACT(E1[:], cl[:], AF.Exp), ["cl"], ["E1"])
                    OP("scalar", ACT(E2[:], cl[:], AF.Exp, scale=-1.0), ["cl"], ["E2"])
                    OP("vector", STT(tmp[:], sg[:], EXPM05, cl[:], ALU.mult, ALU.add), ["zw", "cl"], ["tmp"])
                    OP("scalar", ACT(E3[:], tmp[:], AF.Exp), ["tmp"], ["tmp"])
                    yield
                    OP("vector", TT(za[:], za[:], bc[:, 1, :], ALU.add), ["za", "bc"], ["za"])
                    OP("scalar", ACT(a_[:], za[:], AF.Sigmoid), ["za"], ["za"])
                    OP("vector", TT(kk[:], k_t[:], bc[:, 2, :], ALU.mult), ["k_t", "bc"], ["kk"])
                    yield
                    OP("gpsimd", TT(scr[:], kk[:], kk[:], ALU.mult), ["kk"], ["cl"])
                    OP("vector", RSUM(s8[:, :, 0], h3(scr[:])), ["cl"], ["s8"])
                    OP("scalar", ACT(s8[:, :, 0], s8[:, :, 0], AF.Sqrt), ["s8"], ["s8"])
                    yield
                    OP("vector", TS(s8[:, :, 0], s8[:, :, 0], 1e-12, None, ALU.max), ["s8"], ["s8"])
                    OP("vector", RECIP(s8[:, :, 0], s8[:, :, 0]), ["s8"], ["s8"])
                    OP("vector", TT(h3(kk[:]), h3(kk[:]), bc8(s8[:, :, 0]), ALU.mult), ["kk", "s8"], ["kk"])
                    yield
                    OP("vector", STT(scr[:], a_[:], -1.0, bc[:, 3, :], ALU.add, ALU.mult), ["za", "bc"], ["cl"])
                    OP("vector", STT(kmod[:], scr[:], 1.0, k_t[:], ALU.add, ALU.mult), ["cl", "k_t"], ["kmod"])
                    OP("gpsimd", TT(b_[:], kk[:], a_[:], ALU.mult), ["kk", "za"], ["b_"])
                    yield
                    if use_vres:
                        zv, vf = d["zv"], d["vf"]
                        OP("vector", TT(zv[:], zv[:], bc[:, 7, :], ALU.add), ["zv", "bc"], ["zv"])
                        OP("scalar", ACT(zv[:], zv[:], AF.Sigmoid), ["zv"], ["zv"])
                        OP("gpsimd", TT(vf[:], vf[:], v_t[:], ALU.subtract), ["vf", "v_t"], ["vf"])
                        yield
                        OP("vector", TT(vf[:], vf[:], zv[:], ALU.mult), ["vf", "zv"], ["vf"])
                        OP("vector", TT(v_t[:], v_t[:], vf[:], ALU.add), ["vf", "v_t"], ["v_t"])
                        yield
                    OP("vector", STT(At[:], kk[:], -1.0, E3[:], ALU.mult, ALU.mult), ["kk", "tmp"], ["At"])
                    OP("gpsimd", TT(Bt[:], b_[:], E2[:], ALU.mult), ["b_", "E2"], ["Bt"])
                    yield
                    OP("gpsimd", TT(Kt[:], kmod[:], E2[:], ALU.mult), ["kmod", "E2"], ["Kt"])
                    OP("gpsimd", TT(Rt[:], r_t[:], E1[:], ALU.mult), ["r_t", "E1"], ["Rt"])
                    yield
                    OP("vector", TT(Bh[:], b_[:], E4[:], ALU.mult), ["b_", "tmp2"], ["Bh"])
                    OP("gpsimd", TT(Kh[:], kmod[:], E4[:], ALU.mult), ["kmod", "tmp2"], ["Kh"])
                    OP("scalar", ACP(Vb[:], v_t[:]), ["v_t"], ["Vb"])
                    yield
                    for (src_, dst_, sk, dk) in ((At, ATf, "At", "ATf"), (Bt, BTf, "Bt", "BTf"), (Kt, KTf, "Kt", "KTf"), (Rt, RTf, "Rt", "RTf")):
                        OP("tensor", TRS([(psT[0:64, hh * 128:(hh + 1) * 128], src_[:, hh * 64:(hh + 1) * 64], ident[:]) for hh in range(8)]),
                           [sk, "ident"], ["psT"])
                        OP("scalar", ACP(dst_[:], v3(psT[0:64, 0:1024])), ["psT"], [dk])
                        yield
                    for (lf, rf, lk, rk_, mask, mk, dst_, dk, eng) in (
                            (ATf, BTf, "ATf", "BTf", mL, "mL", Mm[0], "Mm0", "vector"),
                            (BTf, ATf, "BTf", "ATf", mU, "mU", MT[0], "MT0", "gpsimd"),
                            (KTf, ATf, "KTf", "ATf", mU, "mU", AakT, "AakT", "vector"),
                            (BTf, RTf, "BTf", "RTf", mUi, "mUi", ArbT, "ArbT", "vector"),
                            (KTf, RTf, "KTf", "RTf", mUi, "mUi", ArkT, "ArkT", "vector")):
                        OP("tensor", MM([(psM[:, hh * 128:(hh + 1) * 128], lf[:, hh, :], rf[:, hh, :], True, True) for hh in range(8)]),
                           [lk, rk_], ["psM"])
                        if eng == "gpsimd":
                            OP("scalar", ACP(dst_[:], v3(psM[:])), ["psM"], [dk])
                            OP("gpsimd", TT(dst_[:], dst_[:], mb8(mask), ALU.mult), [dk, mk], [dk])
                        else:
                            OP("vector", TT(dst_[:], v3(psM[:]), mb8(mask), ALU.mult), ["psM", mk], [dk])
                        yield
                    OP("gpsimd", TT(TTm[:], MT[0][:], ident[:].unsqueeze(1).to_broadcast([128, 8, 128]), ALU.add), ["MT0", "ident"], ["TT"])
                    yield
                    cur = 0
                    for lev in range(1, 7):
                        nxt = 1 - cur
                        OP("tensor", MM([(psM[:, hh * 128:(hh + 1) * 128], MT[cur][:, hh, :], Mm[cur][:, hh, :], True, True) for hh in range(8)]),
                           ["MT%d" % cur, "Mm%d" % cur], ["psM"])
                        OP("scalar", ACP(Mm[nxt][:], v3(psM[:])), ["psM"], ["Mm%d" % nxt])
                        yield
                        if lev < 6:
                            OP("tensor", MM([(psM[:, hh * 128:(hh + 1) * 128], Mm[cur][:, hh, :], MT[cur][:, hh, :], True, True) for hh in range(8)]),
                               ["MT%d" % cur, "Mm%d" % cur], ["psM"])
                            OP("scalar", ACP(MT[nxt][:], v3(psM[:])), ["psM"], ["MT%d" % nxt])
                            yield
                        OP("tensor", MM([(psM[:, hh * 128:(hh + 1) * 128], Mm[nxt][:, hh, :], TTm[:, hh, :], True, True) for hh in range(8)]),
                           ["Mm%d" % nxt, "TT"], ["psM"])
                        OP("vector", TT(TTm[:], v3(psM[:]), TTm[:], ALU.add), ["psM", "TT"], ["TT"])
                        yield
                        cur = nxt
                    OP("tensor", MM([(psM[0:64, hh * 128:(hh + 1) * 128], At[:, hh * 64:(hh + 1) * 64], TTm[:, hh, :], True, True) for hh in range(8)]),
                       ["At", "TT"], ["psM"])
                    OP("scalar", ACP(WTf[:], v3(psM[0:64, :])), ["psM"], ["WTf"])
                    yield
                    OP("tensor", MM([(psW[:, hh * 64:(hh + 1) * 64], AakT[:, hh, :], Vb[:, hh * 64:(hh + 1) * 64], True, True) for hh in range(8)]),
                       ["AakT", "Vb"], ["psW"])
                    OP("scalar", ACP(AkV[:], psW[:]), ["psW"], ["AkV"])
                    yield
                    OP("tensor", MM([(psW[:, hh * 64:(hh + 1) * 64], TTm[:, hh, :], AkV[:, hh * 64:(hh + 1) * 64], True, True) for hh in range(8)]),
                       ["TT", "AkV"], ["psW"])
                    OP("scalar", ACP(Ut[:], psW[:]), ["psW"], ["Ut"])
                    yield
                    so = "Sb%d" % (tt % 2)
                    sn = "Sb%d" % ((tt + 1) % 2)
                    Sold = Sb[tt % 2]
                    Snew = Sb[(tt + 1) % 2]
                    OP("tensor", MM([(psW[:, hh * 64:(hh + 1) * 64], WTf[:, hh, :], Sold[:, hh, :], True, True) for hh in range(8)]),
                       ["WTf", so], ["psW"])
                    OP("vector", TT(Ub[:], psW[:], Ut[:], ALU.add), ["psW", "Ut"], ["Ub"])
                    yield
                    mm = []
                    for hh in range(8):
                        hs = slice(hh * 64, (hh + 1) * 64)
                        mm.append((psM[:, hs], RTf[:, hh, :], Sold[:, hh, :], True, False))
                        mm.append((psM[:, hs], ArbT[:, hh, :], Ub[:, hs], False, False))
                        mm.append((psM[:, hs], ArkT[:, hh, :], Vb[:, hs], False, True))
                    OP("tensor", MM(mm), ["RTf", so, "ArbT", "Ub", "ArkT", "Vb"], ["psM"])
                    mm = []
                    for hh in range(8):
                        hs = slice(hh * 64, (hh + 1) * 64)
                        mm.append((psW[0:64, hs], Kh[:, hs], Vb[:, hs], True, False))
                        mm.append((psW[0:64, hs], Bh[:, hs], Ub[:, hs], False, True))
                    OP("tensor", MM(mm), ["Kh", "Vb", "Bh", "Ub"], ["psW"])
                    yield
                    OP("vector", TT(S[:], S[:], PC[:].unsqueeze(2).to_broadcast([64, 8, 64]), ALU.mult), ["S", "PC"], ["S"])
                    OP("vector", TT(S[:], S[:], psW[0:64, :].rearrange("p (h v) -> p h v", v=64), ALU.add), ["S", "psW"], ["S"])
                    OP("scalar", ACP(Snew[:], S[:]), ["S"], [sn])
                    yield
                    OP("scalar", ACP(o_[:], psM[:, 0:512]), ["psM"], ["Ut"])
                    OP("vector", RSUM(s8[:, :, 1], h3(o_[:])), ["Ut"], ["s8"])
                    OP("vector", TS(s8[:, :, 1], s8[:, :, 1], 1.0 / 64.0), ["s8"], ["s8"])
                    yield
                    OP("vector", TT(h3(o_[:]), h3(o_[:]), bc8(s8[:, :, 1]), ALU.subtract), ["Ut", "s8"], ["Ut"])
                    OP("gpsimd", TT(tmp[:], o_[:], o_[:], ALU.mult), ["Ut"], ["tmp"])
                    OP("vector", RSUM(s8[:, :, 2], h3(tmp[:])), ["tmp"], ["s8"])
                    yield
                    OP("scalar", ACT(s8[:, :, 2], s8[:, :, 2], AF.Sqrt, bias=eps[:], scale=1.0 / 64.0), ["s8", "eps"], ["s8"])
                    OP("vector", RECIP(s8[:, :, 2], s8[:, :, 2]), ["s8"], ["s8"])
                    OP("vector", TT(h3(o_[:]), h3(o_[:]), bc8(s8[:, :, 2]), ALU.mult), ["Ut", "s8"], ["Ut"])
                    yield
                    OP("gpsimd", TT(tmp2[:], r_t[:], kmod[:], ALU.mult), ["r_t", "kmod"], ["tmp2"])
                    OP("gpsimd", TT(tmp2[:], tmp2[:], bc[:, 4, :], ALU.mult), ["tmp2", "bc"], ["tmp2"])
                    OP("vector", TT(o_[:], o_[:], bc[:, 5, :], ALU.mult), ["Ut", "bc"], ["Ut"])
                    OP("vector", TT(o_[:], o_[:], bc[:, 6, :], ALU.add), ["Ut", "bc"], ["Ut"])
                    yield
                    OP("vector", RSUM(s8[:, :, 3], h3(tmp2[:])), ["tmp2"], ["s8"])
                    OP("vector", TT(h3(tmp2[:]), h3(v_t[:]), bc8(s8[:, :, 3]), ALU.mult), ["v_t", "s8"], ["tmp2"])
                    OP("vector", TT(o_[:], o_[:], tmp2[:], ALU.add), ["Ut", "tmp2"], ["Ut"])
                    OP("vector", TT(og[:], o_[:], g_t[:], ALU.mult), ["Ut", "g_t"], ["og"])
                    yield
                    OP("tensor", TRS([(psT[:, c * 128:(c + 1) * 128], og[:, c * 128:(c + 1) * 128], ident[:]) for c in range(4)]),
                       ["og", "ident"], ["psT"])
                    OP("scalar", ACP(ogT[:], psT[:, 0:512].rearrange("p (c t) -> p c t", t=128)), ["psT"], ["ogT"])
                    P.dma(ogv[:, hg * 4:(hg + 1) * 4, rs_], ogT[:], reads=k("ogT"))
                    yield

            for pair in ((0, 1), (2, 3)):
                gens = [body(pair[0], 0), body(pair[1], 1)]
                alive = [True, True]
                while any(alive):
                    for gi in range(2):
                        if alive[gi]:
                            try:
                                next(gens[gi])
                            except StopIteration:
                                alive[gi] = False
            P.flush()

    def rwkv_layer(self, layer):
        i = layer // 2
        mu_base = 9 + 6 * i
        self.norm_phase(layer, "rwkv", mu_base=mu_base)
        self.lin_tm(self.xmix[0], self.w_rkv[i, 0], self.r_tm)
        self.lin_tm(self.xmix[2], self.w_rkv[i, 1], self.k_tm)
        self.lin_tm(self.xmix[3], self.w_rkv[i, 2], self.vf_tm if i == 0 else self.v_tm)
        self.lora_tm(self.xmix[1], self.w1[i], 96, AF.Tanh, self.w2[i], self.zw_tm)
        self.lora_tm(self.xmix[4], self.a1[i], 96, AF.Identity, self.a2[i], self.za_tm)
        self.lora_tm(self.xmix[5], self.g1[i], 256, AF.Sigmoid, self.g2[i], self.g_tm)
        if i > 0:
            self.lora_tm(self.xmix[3], self.v1[i - 1], 64, AF.Identity, self.v2[i - 1], self.zv_tm)

    def build(self):
        self.declare()
        with ExitStack() as gst:
            self.P = Prog(self.nc, gst)
            self.P.max_blocks = getattr(self, "max_blocks", 10 ** 9)
            plan = self.plan
            if "copy" in plan:
                self.copy_in()
            for layer in range(4):
                i = layer // 2
                if ("mix%d" % layer) in plan:
                    if layer % 2 == 0:
                        self.norm_phase(layer, "plain")
                        self.even_proj(i)
                        self.even_attn(i)
                        self.even_sgu(i)
                        self.proj_residual(self.ycat, self.w_out[i])
                    else:
                        self.rwkv_layer(layer)
                        if i == 0:
                            self._vsrc_first = True
                        self.rwkv_scan_wrap(i)
                        self.proj_residual(self.ycat, self.w_o[i])
                if ("ffn%d" % layer) in plan:
                    self.norm_phase(4 + layer, "plain")
                    self.ffn_phase(layer)
            if "final" in plan:
                self.norm_phase(8, "final")
        return self.nc

    def rwkv_scan_wrap(self, i):
        if i == 0:
            save = self.v_tm
            self.v_tm = self.vf_tm
            self.rwkv_scan(i, use_vres=False)
            self.v_tm = save
        else:
            self.rwkv_scan(i, use_vres=True)


def _fm16(v):
    return np.ascontiguousarray(v.reshape(-1, 128).T)


def make_shared(inp):
    f = np.float32
    sh = {}
    vfm = np.zeros((128, 21, 16), f)
    for l in range(4):
        vfm[:, l, :] = _fm16(inp["mix_norm_g"][l])
        vfm[:, 4 + l, :] = _fm16(inp["ffn_norm_g"][l])
    vfm[:, 8, :] = _fm16(inp["final_norm_g"])
    for i in range(2):
        for j in range(6):
            vfm[:, 9 + 6 * i + j, :] = _fm16(inp["rwkv_mu"][i, j])
    sh["vfm"] = vfm
    convp = np.zeros((128, 4, 4, 44), f)
    for l in range(4):
        for j in range(3):
            convp[:, l, j, :] = _fm16(inp["ffn_conv_w"][l, j])
        convp[:, l, 3, :] = _fm16(inp["ffn_conv_b"][l])
    sh["convp"] = convp
    rv = np.zeros((2, 8, D), f)
    for i in range(2):
        for j, nm in enumerate(["rwkv_w0", "rwkv_a0", "rwkv_k_k", "rwkv_k_a", "rwkv_r_k", "rwkv_gn_g", "rwkv_gn_b"]):
            rv[i, j] = inp[nm][i]
    rv[1, 7] = inp["rwkv_v0"][0]
    sh["rv"] = rv
    sh["sgu_ln"] = np.ascontiguousarray(np.stack([inp["sgu_ln_g"], inp["sgu_ln_b"]], axis=1)).astype(f)
    sh["sgu_wT"] = np.ascontiguousarray(np.transpose(inp["sgu_w"], (0, 1, 3, 2))).astype(f)
    sh["sgu_b"] = np.ascontiguousarray(inp["sgu_b"].reshape(2, 1024)).astype(f)
    perm = np.arange(1024)
    for h in range(16):
        base = h * 64
        perm[base:base + 8] = base + 8 + np.arange(8)
        perm[base + 8:base + 16] = base + np.arange(8)
    w_in = inp["even_w_in"]
    sh["w_in"] = np.ascontiguousarray(np.concatenate([w_in, w_in[:, :, 0:1024][:, :, perm], w_in[:, :, 1024:2048][:, :, perm]], axis=2))
    sh["w_out"] = inp["even_w_out"]
    sh["w_rkv"] = inp["rwkv_w_rkv"]
    sh["w_o"] = inp["rwkv_w_o"]
    for a, b in (("w1", "rwkv_w1"), ("w2", "rwkv_w2"), ("a1", "rwkv_a1"), ("a2", "rwkv_a2"), ("g1", "rwkv_g1"), ("g2", "rwkv_g2"),
                 ("v1", "rwkv_v1"), ("v2", "rwkv_v2"), ("w_up", "ffn_w_up"), ("w_down", "ffn_w_down")):
        sh[a] = inp[b]
    p = np.arange(128)[:, None]
    q = np.arange(128)[None, :]
    consts = np.zeros((128, 8, 128), f)
    consts[:, 0] = (p == q)
    consts[:, 1] = (p > q)
    consts[:, 2] = (q > p)
    consts[:, 3] = (q >= p)
    consts[:, 4] = np.where(q <= p, 0.0, NEG)
    consts[:, 5] = np.where(p <= q, -EXPM05, 0.0)
    consts[:, 6] = -EXPM05
    sh["consts"] = consts
    inv = np.power(np.float32(500000.0), -np.arange(0, 16, 2, dtype=f) / np.float32(16)).astype(f)
    ang = np.arange(T, dtype=f)[:, None] * inv[None, :]
    cos = np.cos(ang).astype(f).T
    sin = np.sin(ang).astype(f).T
    rope = np.zeros((2, 128, T), f)
    rope[0] = 1.0
    for hh in range(2):
        b = hh * 64
        rope[0, b:b + 8] = cos
        rope[0, b + 8:b + 16] = cos
        rope[1, b:b + 8] = -sin
        rope[1, b + 8:b + 16] = sin
    sh["rope"] = rope
    keep = np.ones((128, 16, 16), f)
    for qt in range(1, 16, 2):
        keep[:, qt, qt - 1] = 0.0
    sh["keepm"] = keep
    gm = np.zeros((128, 16, 8), f)
    for qt in range(16):
        gm[:, qt, qt // 2:] = -1e30
    sh["gmask"] = gm
    return {k: np.ascontiguousarray(v, dtype=np.float32) for k, v in sh.items()}


FULL_PLAN = ["copy"] + ["mix%d" % l for l in range(4)] + ["ffn%d" % l for l in range(4)] + ["final"]
_CACHE = {}


def run_plan(inp, plan, ncores=NCORES, max_blocks=None):
    key = tuple(plan) + (max_blocks,)
    if key not in _CACHE:
        bld = Builder(plan)
        if max_blocks is not None:
            bld.max_blocks = max_blocks
        _CACHE[key] = bld.build()
    nc = _CACHE[key]
    sh = make_shared(inp)
    x = inp["x"]
    in_maps = []
    for c in range(ncores):
        m = dict(sh)
        m["xT"] = np.ascontiguousarray(x[c].T)
        in_maps.append(m)
    res = run_bass_kernel_spmd(nc, in_maps, core_ids=list(range(ncores)))
    return res


def kernel(**inputs):
    inp = {k: np.asarray(v) for k, v in inputs.items()}
    res = run_plan(inp, FULL_PLAN)
    out = np.stack([np.ascontiguousarray(res.results[c]["out"].T) for c in range(NCORES)], axis=0)
    return out.astype(np.float32)
```

```python
import numpy as np
import ml_dtypes
import concourse.bass as bass
import concourse.mybir as mybir
from concourse.bass_utils import run_bass_kernel_spmd
from contextlib import ExitStack

F32 = mybir.dt.float32
BF16 = mybir.dt.bfloat16
ALU = mybir.AluOpType
AF = mybir.ActivationFunctionType
AX = mybir.AxisListType

D = 2048
T = 2048
DFF = 5632
NCORES = 8
COMPUTE = ("tensor", "vector", "scalar", "gpsimd")
SEM_ROLL = 30000
NEG = -30000.0
EXPM05 = float(np.exp(-0.5))


class Prog:
    def __init__(self, nc, stack, n_dma_sems=40):
        self.nc = nc
        self.stack = stack
        self.engs = {"tensor": nc.tensor, "vector": nc.vector, "scalar": nc.scalar,
                     "gpsimd": nc.gpsimd, "sync": nc.sync}
        self.sem_id = 0
        self.eng_sem = {}
        self.eng_cnt = {}
        for e in COMPUTE:
            self.eng_sem[e] = self._new_sem("c_" + e)
            self.eng_cnt[e] = 0
        self.dma_pool = {"sync": [[self._new_sem("d_sync%d" % i), 0] for i in range(n_dma_sems)],
                         "scalar": [[self._new_sem("d_act%d" % i), 0] for i in range(8)]}
        self.dma_rr = {"sync": 0, "scalar": 0}
        self.known = {e: {} for e in self.engs}
        self._reset()
        self.n_ops = 0
        self.n_waits = 0

    def _reset(self):
        self.ops = {e: [] for e in self.engs}
        self.last_write = {}
        self.readers = {}

    def _new_sem(self, name):
        self.sem_id += 1
        return self.stack.enter_context(self.nc.semaphore("%s_%d" % (name, self.sem_id)))

    def _deps(self, reads, writes):
        deps = set()
        for k in reads:
            deps |= self.last_write.get(k, set())
        for k in writes:
            deps |= self.last_write.get(k, set())
            deps |= self.readers.get(k, set())
        return deps

    def _waits(self, eng, deps):
        best = {}
        for (sem, val) in deps:
            if id(sem) not in best or best[id(sem)][1] < val:
                best[id(sem)] = (sem, val)
        out = []
        kn = self.known[eng]
        for sid, (sem, val) in best.items():
            if kn.get(sid, 0) >= val:
                continue
            kn[sid] = val
            out.append((sem, val))
        return out

    def _commit(self, ev, reads, writes):
        for k in reads:
            self.readers.setdefault(k, set()).add(ev)
        for k in writes:
            self.last_write[k] = {ev}
            self.readers[k] = set()

    def op(self, eng, fn, reads=(), writes=()):
        deps = self._deps(reads, writes)
        own = self.eng_sem[eng]
        if eng == "tensor":
            deps = {d for d in deps if d[0] is not own}
        waits = self._waits(eng, deps)
        if self.eng_cnt[eng] >= SEM_ROLL:
            self.eng_sem[eng] = self._new_sem("c_" + eng)
            self.eng_cnt[eng] = 0
        self.eng_cnt[eng] += 1
        ev = (self.eng_sem[eng], self.eng_cnt[eng])
        self.ops[eng].append((waits, fn, ev[0], 1))
        self._commit(ev, reads, writes)
        self.n_waits += len(waits)
        self.n_ops += 1
        return ev

    def dma(self, out, in_, reads=(), writes=(), q="sync"):
        deps = set(self._deps(reads, writes))
        pool = self.dma_pool[q]
        i = self.dma_rr[q]
        self.dma_rr[q] = (i + 1) % len(pool)
        slot = pool[i]
        if slot[1] > 0:
            deps.add((slot[0], slot[1]))
        if slot[1] >= SEM_ROLL:
            slot[0] = self._new_sem("d_%s" % q)
            slot[1] = 0
        sem = slot[0]
        waits = self._waits(q, deps)
        slot[1] += 16
        ev = (sem, slot[1])

        def fn(e, out=out, in_=in_):
            return e.dma_start(out=out, in_=in_)
        self.ops[q].append((waits, fn, sem, 16))
        self._commit(ev, reads, writes)
        self.n_waits += len(waits)
        self.n_ops += 1
        return ev

    def flush(self):
        self.n_blocks = getattr(self, "n_blocks", 0) + 1
        if self.n_blocks > getattr(self, "max_blocks", 10 ** 9):
            self._reset()
            return
        finals = set()
        for q, pool in self.dma_pool.items():
            for sem, cnt in pool:
                if cnt > 0:
                    finals.add((sem, cnt))
        for e in COMPUTE:
            if self.eng_cnt[e] > 0:
                finals.add((self.eng_sem[e], self.eng_cnt[e]))
        fw = self._waits("sync", finals)
        self.ops["sync"].append((fw, None, None, 0))
        ops = self.ops
        with self.nc.Block() as block:
            def mk(ename):
                def body(e):
                    for (waits, fn, sem, inc) in ops[ename]:
                        for (s, v) in waits:
                            e.wait_ge(s, v)
                        if fn is not None:
                            fn(e).then_inc(sem, inc)
                return body
            for ename in ("sync", "gpsimd", "scalar", "vector", "tensor"):
                if ops[ename]:
                    getattr(block, ename)(mk(ename))
        self._reset()


def TT(out, in0, in1, op):
    return lambda e: e.tensor_tensor(out=out, in0=in0, in1=in1, op=op)


def TS(out, in0, s1, s2=None, op0=ALU.mult, op1=None):
    if op1 is None:
        return lambda e: e.tensor_scalar(out=out, in0=in0, scalar1=s1, scalar2=None, op0=op0)
    return lambda e: e.tensor_scalar(out=out, in0=in0, scalar1=s1, scalar2=s2, op0=op0, op1=op1)


def STT(out, in0, scalar, in1, op0, op1):
    return lambda e: e.scalar_tensor_tensor(out=out, in0=in0, scalar=scalar, in1=in1, op0=op0, op1=op1)


def ACT(out, in_, func, bias=None, scale=None, accum=None):
    kw = {}
    if bias is not None:
        kw["bias"] = bias
    if scale is not None:
        kw["scale"] = scale
    if accum is not None:
        kw["accum_out"] = accum
    return lambda e: e.activation(out=out, in_=in_, func=func, **kw)


def CP(out, in_):
    return lambda e: e.tensor_copy(out=out, in_=in_)


def ACP(out, in_):
    return lambda e: e.copy(out=out, in_=in_)


def MM(lst):
    def fn(e):
        ins = None
        for (out, lhsT, rhs, start, stop) in lst:
            ins = e.matmul(out, lhsT=lhsT, rhs=rhs, start=start, stop=stop)
        return ins
    return fn


def TRS(lst):
    def fn(e):
        ins = None
        for (out, in_, ident) in lst:
            ins = e.transpose(out, in_, ident)
        return ins
    return fn


def RSUM(out, in_):
    return lambda e: e.reduce_sum(out=out, in_=in_, axis=AX.X)


def RMAX(out, in_):
    return lambda e: e.reduce_max(out=out, in_=in_, axis=AX.X)


def MSET(ap, val):
    return lambda e: e.memset(ap, val)


def RECIP(out, in_):
    return lambda e: e.reciprocal(out=out, in_=in_)


def MAX8(out, in_):
    return lambda e: e.max(out=out, in_=in_)


class Builder:
    def __init__(self, plan):
        self.plan = plan
        self.nc = bass.Bass("TRN2", target_bir_lowering=False)
        self.inputs = {}

    def din(self, name, shape, dt=F32):
        t = self.nc.dram_tensor(name, list(shape), dt, kind="ExternalInput").ap()
        self.inputs[name] = t
        return t

    def dscr(self, name, shape, dt):
        return self.nc.dram_tensor(name, list(shape), dt, kind="Internal").ap()

    def sb(self, st, name, shape, dt):
        self._uid += 1
        return st.enter_context(self.nc.sbuf_tensor("%s_%d" % (name, self._uid), list(shape), dt))

    def pp(self, st, name, shape, dt):
        self._uid += 1
        return st.enter_context(self.nc.psum_tensor("%s_%d" % (name, self._uid), list(shape), dt))

    def declare(self):
        nc = self.nc
        self._uid = 0
        self.xT = self.din("xT", [D, T])
        self.vfm = self.din("vfm", [128, 21, 16])
        self.convp = self.din("convp", [128, 4, 4, 44])
        self.rv = self.din("rv", [2, 8, D])
        self.sgu_ln = self.din("sgu_ln", [2, 2, 1024])
        self.sgu_wT = self.din("sgu_wT", [2, 8, 128, 128])
        self.sgu_b = self.din("sgu_b", [2, 1024])
        self.w_in = self.din("w_in", [2, D, 7168])
        self.w_out = self.din("w_out", [2, D, D])
        self.w_rkv = self.din("w_rkv", [2, 3, D, D])
        self.w_o = self.din("w_o", [2, D, D])
        self.w1 = self.din("w1", [2, D, 96])
        self.w2 = self.din("w2", [2, 96, D])
        self.a1 = self.din("a1", [2, D, 96])
        self.a2 = self.din("a2", [2, 96, D])
        self.g1 = self.din("g1", [2, D, 256])
        self.g2 = self.din("g2", [2, 256, D])
        self.v1 = self.din("v1", [1, D, 64])
        self.v2 = self.din("v2", [1, 64, D])
        self.w_up = self.din("w_up", [4, D, 2 * DFF])
        self.w_down = self.din("w_down", [4, DFF, D])
        self.consts = self.din("consts", [128, 8, 128])
        self.rope = self.din("rope", [2, 128, T])
        self.keep = self.din("keepm", [128, 16, 16])
        self.gmask = self.din("gmask", [128, 16, 8])
        self.out = nc.dram_tensor("out", [D, T], F32, kind="ExternalOutput").ap()
        self.xres = self.dscr("xres", [D, T], F32)
        self.hbuf = self.dscr("hbuf", [D, T], BF16)
        self.gT = self.dscr("gT", [DFF, T], BF16)
        self.qT = self.dscr("qT", [1024, T], BF16)
        self.kT = self.dscr("kT", [1024, T], BF16)
        self.vtm = self.dscr("vtm", [T, 1024], BF16)
        self.uT = self.dscr("uT", [1024, T], BF16)
        self.vn = self.dscr("vn", [T, 1024], BF16)
        self.ycat = self.dscr("ycat", [D, T], BF16)
        self.xmix = [self.dscr("xmix%d" % i, [D, T], BF16) for i in range(6)]
        self.r_tm = self.dscr("r_tm", [T, D], F32)
        self.k_tm = self.dscr("k_tm", [T, D], F32)
        self.v_tm = self.dscr("v_tm", [T, D], F32)
        self.vf_tm = self.dscr("vf_tm", [T, D], F32)
        self.zw_tm = self.dscr("zw_tm", [T, D], F32)
        self.za_tm = self.dscr("za_tm", [T, D], F32)
        self.zv_tm = self.dscr("zv_tm", [T, D], F32)
        self.g_tm = self.dscr("g_tm", [T, D], F32)

    def copy_in(self):
        P = self.P
        with ExitStack() as st:
            bufs = [self.sb(st, "cpy", [128, 16, 256], F32) for _ in range(2)]
            src = self.xT.rearrange("(c p) t -> p c t", p=128)
            dst = self.xres.rearrange("(c p) t -> p c t", p=128)
            for i in range(8):
                b = bufs[i % 2]
                P.dma(b[:], src[:, :, i * 256:(i + 1) * 256], writes=["cp%d" % (i % 2)])
                P.dma(dst[:, :, i * 256:(i + 1) * 256], b[:], reads=["cp%d" % (i % 2)])
            P.flush()

    def norm_phase(self, gidx, mode, mu_base=None):
        P = self.P
        TB = 256
        with ExitStack() as st:
            xt = [self.sb(st, "nx", [128, 16, TB], F32) for _ in range(2)]
            sq = self.sb(st, "nsq", [128, 16, TB], BF16)
            rstd = self.sb(st, "nrstd", [128, TB], F32)
            ones = self.sb(st, "nones", [128, 128], BF16)
            gv = self.sb(st, "ngv", [128, 21, 16], F32)
            eps = self.sb(st, "neps", [128, 1], F32)
            ps = self.pp(st, "nps", [128, 512], F32)
            P.op("vector", MSET(ones[:], 1.0), writes=["ones"])
            P.op("vector", MSET(eps[:], 1e-6), writes=["eps"])
            P.dma(gv[:], self.vfm, writes=["gv"])
            src = self.xres.rearrange("(c p) t -> p c t", p=128)
            if mode == "plain":
                ho = [self.sb(st, "nho", [128, 16, TB], BF16) for _ in range(2)]
                dst = self.hbuf.rearrange("(c p) t -> p c t", p=128)
            elif mode == "final":
                ho = [self.sb(st, "nho", [128, 16, TB], F32) for _ in range(2)]
                dst = self.out.rearrange("(c p) t -> p c t", p=128)
            else:
                hf = self.sb(st, "nhf", [128, 16, TB + 1], F32)
                dx = [self.sb(st, "ndx", [128, TB], F32) for _ in range(2)]
                xo = [self.sb(st, "nxo", [128, 16, TB], BF16) for _ in range(6)]
                dsts = [m.rearrange("(c p) t -> p c t", p=128) for m in self.xmix]
                P.op("vector", MSET(hf[:], 0.0), writes=["hf"])
            P.dma(xt[0][:], src[:, :, 0:TB], writes=["nx0"])
            for tb in range(T // TB):
                x_ = xt[tb % 2]
                xk = "nx%d" % (tb % 2)
                sl = slice(tb * TB, (tb + 1) * TB)
                if tb + 1 < T // TB:
                    P.dma(xt[(tb + 1) % 2][:], src[:, :, (tb + 1) * TB:(tb + 2) * TB], writes=["nx%d" % ((tb + 1) % 2)])
                P.op("scalar", ACT(sq[:], x_[:], AF.Square), reads=[xk], writes=["sq"])
                P.op("tensor", MM([(ps[:, 0:TB], ones[:], sq[:, c, :], c == 0, c == 15) for c in range(16)]),
                     reads=["ones", "sq"], writes=["ps"])
                P.op("scalar", ACT(rstd[:], ps[:, 0:TB], AF.Sqrt, bias=eps[:], scale=1.0 / D),
                     reads=["ps", "eps"], writes=["rstd"])
                P.op("vector", RECIP(rstd[:], rstd[:]), reads=["rstd"], writes=["rstd"])
                if mode in ("plain", "final"):
                    h_ = ho[tb % 2]
                    hk = "ho%d" % (tb % 2)
                    for c in range(16):
                        P.op("vector", STT(h_[:, c, :], x_[:, c, :], gv[:, gidx, c:c + 1], rstd[:], ALU.mult, ALU.mult),
                             reads=[xk, "gv", "rstd"], writes=[hk])
                    P.dma(dst[:, :, sl], h_[:], reads=[hk])
                else:
                    if tb > 0:
                        P.op("vector", CP(hf[:, :, 0:1], hf[:, :, TB:TB + 1]), reads=["hf"], writes=["hf"])
                    for c in range(16):
                        P.op("vector", STT(hf[:, c, 1:TB + 1], x_[:, c, :], gv[:, gidx, c:c + 1], rstd[:], ALU.mult, ALU.mult),
                             reads=[xk, "gv", "rstd"], writes=["hf"])
                    for c in range(16):
                        d_ = dx[c % 2]
                        dk = "dx%d" % (c % 2)
                        P.op("gpsimd", TT(d_[:], hf[:, c, 0:TB], hf[:, c, 1:TB + 1], ALU.subtract), reads=["hf"], writes=[dk])
                        for i in range(6):
                            P.op("vector", STT(xo[i][:, c, :], d_[:], gv[:, mu_base + i, c:c + 1], hf[:, c, 1:TB + 1],
                                               ALU.mult, ALU.add), reads=[dk, "hf", "gv"], writes=["xo%d" % i])
                    for i in range(6):
                        P.dma(dsts[i][:, :, sl], xo[i][:], reads=["xo%d" % i])
            P.flush()

    def load_w(self, wst, wbf, key, src_ap, KC, ncols, col0=0, ndma=4):
        P = self.P
        v = src_ap.rearrange("(c p) m -> p c m", p=128)
        step = (KC + ndma - 1) // ndma
        for c0 in range(0, KC, step):
            c1 = min(KC, c0 + step)
            P.dma(wst[:, c0:c1, col0:col0 + ncols], v[:, c0:c1, :], writes=[key + "s"])
        P.op("gpsimd", CP(wbf[:, 0:KC, col0:col0 + ncols], wst[:, 0:KC, col0:col0 + ncols]),
             reads=[key + "s"], writes=[key])

    def load_hT(self, hT, src, key="hT", KC=16, t0=0, tn=T):
        v = src.rearrange("(c p) t -> p c t", p=128)
        for c0 in range(0, KC, 4):
            c1 = min(KC, c0 + 4)
            self.P.dma(hT[:, c0:c1, 0:tn], v[:, c0:c1, t0:t0 + tn], writes=[key])

    FFN_A_NEW = False
    FFN_B_NEW = True

    def ffn_phase(self, l):
        self.ffn_A_v3(l)
        self.ffn_B_v2(l)

    def ffn_A_new(self, l):
        P = self.P
        with ExitStack() as st:
            hT = self.sb(st, "hT", [128, 16, T], BF16)
            wst = [self.sb(st, "wst", [128, 16, 256], F32) for _ in range(2)]
            wbf = [self.sb(st, "wbf", [128, 16, 256], BF16) for _ in range(2)]
            cv = [self.sb(st, "cv", [128, 1024], F32) for _ in range(2)]
            sl = [self.sb(st, "sl", [128, 1024], F32) for _ in range(2)]
            go = [self.sb(st, "go", [128, 1024], BF16) for _ in range(2)]
            bnd = self.sb(st, "bnd", [128, 2], F32)
            cp = self.sb(st, "cp", [128, 4, 4, 44], F32)
            psG = [self.pp(st, "psG", [128, 1024], F32) for _ in range(2)]
            psU = [self.pp(st, "psU", [128, 1024], F32) for _ in range(2)]
            P.dma(cp[:], self.convp, writes=["cp"])
            self.load_hT(hT, self.hbuf)

            def loadW(fc):
                b = fc % 2
                self.load_w(wst[b], wbf[b], "w%d" % b, self.w_up[l][:, fc * 128:(fc + 1) * 128], 16, 128, col0=0, ndma=2)
                self.load_w(wst[b], wbf[b], "w%du" % b, self.w_up[l][:, DFF + fc * 128:DFF + (fc + 1) * 128], 16, 128, col0=128, ndma=2)
            loadW(0)
            for fc in range(44):
                b = fc % 2
                wk = "w%d" % b
                if fc + 1 < 44:
                    loadW(fc + 1)
                w0 = cp[:, l, 0, fc:fc + 1]
                w1 = cp[:, l, 1, fc:fc + 1]
                w2 = cp[:, l, 2, fc:fc + 1]
                bb = cp[:, l, 3, fc:fc + 1]
                for th in range(2):
                    g_, u_ = psG[th], psU[th]
                    gk, uk = "psG%d" % th, "psU%d" % th
                    for tb in range(2):
                        ts = slice(tb * 512, (tb + 1) * 512)
                        hs = slice(th * 1024 + tb * 512, th * 1024 + (tb + 1) * 512)
                        P.op("tensor", MM([(g_[:, ts], wbf[b][:, c, 0:128], hT[:, c, hs], c == 0, c == 15) for c in range(16)]),
                             reads=[wk, "hT"], writes=[gk])
                    for tb in range(2):
                        ts = slice(tb * 512, (tb + 1) * 512)
                        hs = slice(th * 1024 + tb * 512, th * 1024 + (tb + 1) * 512)
                        P.op("tensor", MM([(u_[:, ts], wbf[b][:, c, 128:256], hT[:, c, hs], c == 0, c == 15) for c in range(16)]),
                             reads=[wk + "u", "hT"], writes=[uk])
                    c_ = cv[th]
                    ck = "cv%d" % th
                    P.op("vector", TS(c_[:], g_[:], w2, bb, ALU.mult, ALU.add), reads=[gk, "cp"], writes=[ck])
                    P.op("vector", STT(c_[:, 1:1024], g_[:, 0:1023], w1, c_[:, 1:1024], ALU.mult, ALU.add), reads=[gk, "cp", ck], writes=[ck])
                    P.op("vector", STT(c_[:, 2:1024], g_[:, 0:1022], w0, c_[:, 2:1024], ALU.mult, ALU.add), reads=[gk, "cp", ck], writes=[ck])
                    if th == 0:
                        P.op("scalar", ACP(bnd[:], g_[:, 1022:1024]), reads=[gk], writes=["bnd"])
                    else:
                        P.op("vector", STT(c_[:, 0:1], bnd[:, 1:2], w1, c_[:, 0:1], ALU.mult, ALU.add), reads=["bnd", "cp", ck], writes=[ck])
                        P.op("vector", STT(c_[:, 0:2], bnd[:, 0:2], w0, c_[:, 0:2], ALU.mult, ALU.add), reads=["bnd", "cp", ck], writes=[ck])
                    P.op("scalar", ACT(sl[th][:], c_[:], AF.Silu), reads=[ck], writes=["sl%d" % th])
                    P.op("vector", TT(go[th][:], sl[th][:], u_[:], ALU.mult), reads=["sl%d" % th, uk], writes=["go%d" % th])
                    P.dma(self.gT[fc * 128:(fc + 1) * 128, th * 1024:(th + 1) * 1024], go[th][:], reads=["go%d" % th])
            P.flush()
    def ffn_B_new(self, l):
        P = self.P
        with ExitStack() as st:
            gTs = self.sb(st, "gTs", [128, 44, 1024], BF16)
            wst = [self.sb(st, "wst", [128, 44, 128], F32) for _ in range(2)]
            wbf = [self.sb(st, "wbf", [128, 44, 128], BF16) for _ in range(2)]
            xr = [self.sb(st, "xr", [128, 1024], F32) for _ in range(2)]
            ps = [self.pp(st, "ps", [128, 1024], F32) for _ in range(2)]
            gv = self.gT.rearrange("(c p) t -> p c t", p=128)

            def loadB(j):
                th, dc = j // 16, j % 16
                b = j % 2
                self.load_w(wst[b], wbf[b], "w%d" % b, self.w_down[l][:, dc * 128:(dc + 1) * 128], 44, 128, ndma=8)
                P.dma(xr[b][:], self.xres[dc * 128:(dc + 1) * 128, th * 1024:(th + 1) * 1024], writes=["xr%d" % b])
            loadB(0)
            for j in range(32):
                th, dc = j // 16, j % 16
                b = j % 2
                if dc == 0:
                    for c0 in range(0, 44, 4):
                        P.dma(gTs[:, c0:c0 + 4, :], gv[:, c0:c0 + 4, th * 1024:(th + 1) * 1024], writes=["gTs"])
                if j + 1 < 32:
                    loadB(j + 1)
                for tb in range(2):
                    ts = slice(tb * 512, (tb + 1) * 512)
                    P.op("tensor", MM([(ps[b][:, ts], wbf[b][:, c, :], gTs[:, c, ts], c == 0, c == 43) for c in range(44)]),
                         reads=["w%d" % b, "gTs"], writes=["ps%d" % b])
                xs = self.xres[dc * 128:(dc + 1) * 128, th * 1024:(th + 1) * 1024]
                P.op("vector", TT(xr[b][:], xr[b][:], ps[b][:], ALU.add), reads=["xr%d" % b, "ps%d" % b], writes=["xr%d" % b])
                P.dma(xs, xr[b][:], reads=["xr%d" % b])
            P.flush()

    def ffn_B_v2(self, l):
        P = self.P
        with ExitStack() as st:
            gTs = self.sb(st, "gTs", [128, 44, 1024], BF16)
            wst = self.sb(st, "wst", [128, 44, 256], F32)
            wbf = [self.sb(st, "wbf", [128, 44, 256], BF16) for _ in range(2)]
            xr = [self.sb(st, "xr", [128, 1024], F32) for _ in range(2)]
            ps = [self.pp(st, "ps", [128, 1024], F32) for _ in range(2)]
            gv = self.gT.rearrange("(c p) t -> p c t", p=128)

            def loadW(j2):
                dp = j2 % 8
                b = j2 % 2
                v = self.w_down[l][:, dp * 256:(dp + 1) * 256].rearrange("(c p) m -> p c m", p=128)
                for c0 in range(0, 44, 4):
                    P.dma(wst[:, c0:c0 + 4, :], v[:, c0:c0 + 4, :], writes=["wst"])
                P.op("gpsimd", CP(wbf[b][:, 0:22, :], wst[:, 0:22, :]), reads=["wst"], writes=["w%da" % b])
                P.op("scalar", ACP(wbf[b][:, 22:44, :], wst[:, 22:44, :]), reads=["wst"], writes=["w%db" % b])

            def loadX(n):
                th, dc = n // 16, n % 16
                P.dma(xr[n % 2][:], self.xres[dc * 128:(dc + 1) * 128, th * 1024:(th + 1) * 1024], writes=["xr%d" % (n % 2)])
            loadW(0)
            loadX(0)
            for j2 in range(16):
                th, dp = j2 // 8, j2 % 8
                b = j2 % 2
                if dp == 0:
                    for c0 in range(0, 44, 4):
                        P.dma(gTs[:, c0:c0 + 4, :], gv[:, c0:c0 + 4, th * 1024:(th + 1) * 1024], writes=["gTs"])
                if j2 + 1 < 16:
                    loadW(j2 + 1)
                for mc in range(2):
                    n = th * 16 + dp * 2 + mc
                    dc = dp * 2 + mc
                    pb = n % 2
                    if n + 1 < 32:
                        loadX(n + 1)
                    for tb in range(2):
                        ts = slice(tb * 512, (tb + 1) * 512)
                        P.op("tensor", MM([(ps[pb][:, ts], wbf[b][:, c, mc * 128:(mc + 1) * 128], gTs[:, c, ts], c == 0, c == 43) for c in range(44)]),
                             reads=["w%da" % b, "w%db" % b, "gTs"], writes=["ps%d" % pb])
                    xs = self.xres[dc * 128:(dc + 1) * 128, th * 1024:(th + 1) * 1024]
                    P.op("vector", TT(xr[pb][:], xr[pb][:], ps[pb][:], ALU.add), reads=["xr%d" % pb, "ps%d" % pb], writes=["xr%d" % pb])
                    P.dma(xs, xr[pb][:], reads=["xr%d" % pb])
            P.flush()

    def ffn_A_old(self, l):
        P = self.P
        with ExitStack() as st:
            hT = self.sb(st, "hT", [128, 16, T], BF16)
            wst = [self.sb(st, "wst", [128, 16, 256], F32) for _ in range(2)]
            wbf = [self.sb(st, "wbf", [128, 16, 256], BF16) for _ in range(2)]
            cv = self.sb(st, "cv", [128, T], F32)
            sl = self.sb(st, "sl", [128, T], F32)
            go = [self.sb(st, "go", [128, T], BF16) for _ in range(2)]
            cp = self.sb(st, "cp", [128, 4, 4, 44], F32)
            psA = self.pp(st, "psA", [128, T], F32)
            psB = self.pp(st, "psB", [128, T], F32)
            P.dma(cp[:], self.convp, writes=["cp"])
            self.load_hT(hT, self.hbuf)
            for fc in range(44):
                b = fc % 2
                wk = "w%d" % b
                self.load_w(wst[b], wbf[b], wk, self.w_up[l][:, fc * 128:(fc + 1) * 128], 16, 128, col0=0, ndma=2)
                self.load_w(wst[b], wbf[b], wk + "u", self.w_up[l][:, DFF + fc * 128:DFF + (fc + 1) * 128], 16, 128, col0=128, ndma=2)
                for tb in range(4):
                    ts = slice(tb * 512, (tb + 1) * 512)
                    P.op("tensor", MM([(psA[:, ts], wbf[b][:, c, 0:128], hT[:, c, ts], c == 0, c == 15) for c in range(16)]),
                         reads=[wk, "hT"], writes=["psA"])
                for tb in range(4):
                    ts = slice(tb * 512, (tb + 1) * 512)
                    P.op("tensor", MM([(psB[:, ts], wbf[b][:, c, 128:256], hT[:, c, ts], c == 0, c == 15) for c in range(16)]),
                         reads=[wk + "u", "hT"], writes=["psB"])
                P.op("vector", TS(cv[:], psA[:], cp[:, l, 2, fc:fc + 1], cp[:, l, 3, fc:fc + 1], ALU.mult, ALU.add),
                     reads=["psA", "cp"], writes=["cv"])
                P.op("vector", STT(cv[:, 1:T], psA[:, 0:T - 1], cp[:, l, 1, fc:fc + 1], cv[:, 1:T], ALU.mult, ALU.add),
                     reads=["psA", "cp", "cv"], writes=["cv"])
                P.op("vector", STT(cv[:, 2:T], psA[:, 0:T - 2], cp[:, l, 0, fc:fc + 1], cv[:, 2:T], ALU.mult, ALU.add),
                     reads=["psA", "cp", "cv"], writes=["cv"])
                P.op("scalar", ACT(sl[:], cv[:], AF.Silu), reads=["cv"], writes=["sl"])
                gk = "go%d" % b
                P.op("vector", TT(go[b][:], sl[:], psB[:], ALU.mult), reads=["sl", "psB"], writes=[gk])
                P.dma(self.gT[fc * 128:(fc + 1) * 128, :], go[b][:], reads=[gk])
            P.flush()
    def ffn_A_v3(self, l):
        P = self.P
        with ExitStack() as st:
            hT = self.sb(st, "hT", [128, 16, T], BF16)
            wst = [self.sb(st, "wst", [128, 16, 256], F32) for _ in range(2)]
            wbf = [self.sb(st, "wbf", [128, 16, 256], BF16) for _ in range(2)]
            cv = self.sb(st, "cv", [128, T], F32)
            sl = self.sb(st, "sl", [128, T], F32)
            go = [self.sb(st, "go", [128, T], BF16) for _ in range(2)]
            cp = self.sb(st, "cp", [128, 4, 4, 44], F32)
            psA = self.pp(st, "psA", [128, T], F32)
            psB = self.pp(st, "psB", [128, T], F32)
            P.dma(cp[:], self.convp, writes=["cp"])
            self.load_hT(hT, self.hbuf)
            def loadW(fc):
                b = fc % 2
                self.load_w(wst[b], wbf[b], "w%d" % b, self.w_up[l][:, fc * 128:(fc + 1) * 128], 16, 128, col0=0, ndma=2)
                self.load_w(wst[b], wbf[b], "w%du" % b, self.w_up[l][:, DFF + fc * 128:DFF + (fc + 1) * 128], 16, 128, col0=128, ndma=2)
            loadW(0)
            for fc in range(44):
                b = fc % 2
                wk = "w%d" % b
                if fc + 1 < 44:
                    loadW(fc + 1)
                for tb in range(4):
                    ts = slice(tb * 512, (tb + 1) * 512)
                    P.op("tensor", MM([(psA[:, ts], wbf[b][:, c, 0:128], hT[:, c, ts], c == 0, c == 15) for c in range(16)]),
                         reads=[wk, "hT"], writes=["psA"])
                for tb in range(4):
                    ts = slice(tb * 512, (tb + 1) * 512)
                    P.op("tensor", MM([(psB[:, ts], wbf[b][:, c, 128:256], hT[:, c, ts], c == 0, c == 15) for c in range(16)]),
                         reads=[wk + "u", "hT"], writes=["psB"])
                P.op("vector", TS(cv[:], psA[:], cp[:, l, 2, fc:fc + 1], cp[:, l, 3, fc:fc + 1], ALU.mult, ALU.add),
                     reads=["psA", "cp"], writes=["cv"])
                P.op("vector", STT(cv[:, 1:T], psA[:, 0:T - 1], cp[:, l, 1, fc:fc + 1], cv[:, 1:T], ALU.mult, ALU.add),
                     reads=["psA", "cp", "cv"], writes=["cv"])
                P.op("vector", STT(cv[:, 2:T], psA[:, 0:T - 2], cp[:, l, 0, fc:fc + 1], cv[:, 2:T], ALU.mult, ALU.add),
                     reads=["psA", "cp", "cv"], writes=["cv"])
                P.op("scalar", ACT(sl[:], cv[:], AF.Silu), reads=["cv"], writes=["sl"])
                gk = "go%d" % b
                P.op("vector", TT(go[b][:], sl[:], psB[:], ALU.mult), reads=["sl", "psB"], writes=[gk])
                P.dma(self.gT[fc * 128:(fc + 1) * 128, :], go[b][:], reads=[gk])
            P.flush()
    def ffn_B_old(self, l):
        P = self.P
        with ExitStack() as st:
            gTs = self.sb(st, "gTs", [128, 44, 1024], BF16)
            wst = [self.sb(st, "wst", [128, 44, 128], F32) for _ in range(2)]
            wbf = [self.sb(st, "wbf", [128, 44, 128], BF16) for _ in range(2)]
            xr = [self.sb(st, "xr", [128, 1024], F32) for _ in range(2)]
            ps = [self.pp(st, "ps", [128, 1024], F32) for _ in range(2)]
            gv = self.gT.rearrange("(c p) t -> p c t", p=128)
            cnt = 0
            for th in range(2):
                for c0 in range(0, 44, 4):
                    P.dma(gTs[:, c0:c0 + 4, :], gv[:, c0:c0 + 4, th * 1024:(th + 1) * 1024], writes=["gTs"])
                for dc in range(16):
                    b = cnt % 2
                    cnt += 1
                    wk = "w%d" % b
                    self.load_w(wst[b], wbf[b], wk, self.w_down[l][:, dc * 128:(dc + 1) * 128], 44, 128, ndma=8)
                    for tb in range(2):
                        ts = slice(tb * 512, (tb + 1) * 512)
                        P.op("tensor", MM([(ps[b][:, ts], wbf[b][:, c, :], gTs[:, c, ts], c == 0, c == 43) for c in range(44)]),
                             reads=[wk, "gTs"], writes=["ps%d" % b])
                    xs = self.xres[dc * 128:(dc + 1) * 128, th * 1024:(th + 1) * 1024]
                    P.dma(xr[b][:], xs, writes=["xr%d" % b])
                    P.op("vector", TT(xr[b][:], xr[b][:], ps[b][:], ALU.add), reads=["xr%d" % b, "ps%d" % b], writes=["xr%d" % b])
                    P.dma(xs, xr[b][:], reads=["xr%d" % b])
            P.flush()

    def proj_residual(self, src, w_ap):
        P = self.P
        with ExitStack() as st:
            hT = self.sb(st, "hT", [128, 16, T], BF16)
            wst = [self.sb(st, "wst", [128, 16, 256], F32) for _ in range(2)]
            wbf = [self.sb(st, "wbf", [128, 16, 256], BF16) for _ in range(2)]
            xr = [self.sb(st, "xr", [128, T], F32) for _ in range(2)]
            ps = [self.pp(st, "ps", [128, T], F32) for _ in range(2)]
            self.load_hT(hT, src)

            def loadW(cb):
                b = cb % 2
                self.load_w(wst[b], wbf[b], "w%d" % b, w_ap[:, cb * 256:(cb + 1) * 256], 16, 256)

            def loadX(dc):
                P.dma(xr[dc % 2][:], self.xres[dc * 128:(dc + 1) * 128, :], writes=["xr%d" % (dc % 2)])
            loadW(0)
            loadX(0)
            for cb in range(8):
                b = cb % 2
                wk = "w%d" % b
                if cb + 1 < 8:
                    loadW(cb + 1)
                for mc in range(2):
                    dc = cb * 2 + mc
                    pb = dc % 2
                    if dc + 1 < 16:
                        loadX(dc + 1)
                    for tb in range(4):
                        ts = slice(tb * 512, (tb + 1) * 512)
                        P.op("tensor", MM([(ps[pb][:, ts], wbf[b][:, c, mc * 128:(mc + 1) * 128], hT[:, c, ts], c == 0, c == 15)
                                           for c in range(16)]), reads=[wk, "hT"], writes=["ps%d" % pb])
                    xs = self.xres[dc * 128:(dc + 1) * 128, :]
                    P.op("vector", TT(xr[pb][:], xr[pb][:], ps[pb][:], ALU.add), reads=["xr%d" % pb, "ps%d" % pb], writes=["xr%d" % pb])
                    P.dma(xs, xr[pb][:], reads=["xr%d" % pb])
            P.flush()

    def even_proj(self, i):
        P = self.P
        win = self.w_in[i]
        with ExitStack() as st:
            hT = self.sb(st, "hT", [128, 16, T], BF16)
            wst = [self.sb(st, "wst", [128, 16, 256], F32) for _ in range(2)]
            wbf = [self.sb(st, "wbf", [128, 16, 256], BF16) for _ in range(2)]
            rc = self.sb(st, "ropec", [128, T], F32)
            rs = self.sb(st, "ropes", [128, T], F32)
            t1 = self.sb(st, "t1", [128, T], F32)
            t2 = self.sb(st, "t2", [128, T], F32)
            ob = [self.sb(st, "ob", [128, T], BF16) for _ in range(2)]
            lng = self.sb(st, "lng", [128, 2, 1024], F32)
            st8 = self.sb(st, "st8", [128, 8, 2], F32)
            st9 = self.sb(st, "st9", [128, 8, 2], F32)
            eps = self.sb(st, "eps", [128, 1], F32)
            psA = self.pp(st, "psA", [128, T], F32)
            psB = self.pp(st, "psB", [128, T], F32)
            P.dma(rc[:], self.rope[0], writes=["rc"])
            P.dma(rs[:], self.rope[1], writes=["rs"])
            for a in range(2):
                P.dma(lng[:, a, :], self.sgu_ln[i, a, :].partition_broadcast(128), writes=["lng"])
            P.op("vector", MSET(eps[:], 1e-5), writes=["eps"])
            self.load_hT(hT, self.hbuf)
            jobs = []
            pss = [psA, psB]
            state = {"pcnt": 0}

            def mk_qk(c_main, c_perm, dst, j):
                def load(b):
                    self.load_w(wst[b], wbf[b], "w%d" % b, win[:, c_main + j * 128:c_main + (j + 1) * 128], 16, 128, col0=0, ndma=2)
                    self.load_w(wst[b], wbf[b], "w%du" % b, win[:, c_perm + j * 128:c_perm + (j + 1) * 128], 16, 128, col0=128, ndma=2)

                def comp(b):
                    wk = "w%d" % b
                    for tb in range(4):
                        ts = slice(tb * 512, (tb + 1) * 512)
                        P.op("tensor", MM([(psA[:, ts], wbf[b][:, c, 0:128], hT[:, c, ts], c == 0, c == 15) for c in range(16)]),
                             reads=[wk, "hT"], writes=["psA"])
                    for tb in range(4):
                        ts = slice(tb * 512, (tb + 1) * 512)
                        P.op("tensor", MM([(psB[:, ts], wbf[b][:, c, 128:256], hT[:, c, ts], c == 0, c == 15) for c in range(16)]),
                             reads=[wk + "u", "hT"], writes=["psB"])
                    P.op("vector", TT(t1[:], psA[:], rc[:], ALU.mult), reads=["psA", "rc"], writes=["t1"])
                    P.op("vector", TT(t2[:], psB[:], rs[:], ALU.mult), reads=["psB", "rs"], writes=["t2"])
                    P.op("gpsimd", TT(ob[b][:], t1[:], t2[:], ALU.add), reads=["t1", "t2"], writes=["ob%d" % b])
                    P.dma(dst[j * 128:(j + 1) * 128, :], ob[b][:], reads=["ob%d" % b])
                return load, comp

            def mk_u(j):
                def load(b):
                    self.load_w(wst[b], wbf[b], "w%d" % b, win[:, 3072 + j * 128:3072 + (j + 1) * 128], 16, 128, col0=0, ndma=2)

                def comp(b):
                    wk = "w%d" % b
                    ps_ = pss[j % 2]
                    pk = "psA" if j % 2 == 0 else "psB"
                    for tb in range(4):
                        ts = slice(tb * 512, (tb + 1) * 512)
                        P.op("tensor", MM([(ps_[:, ts], wbf[b][:, c, 0:128], hT[:, c, ts], c == 0, c == 15) for c in range(16)]),
                             reads=[wk, "hT"], writes=[pk])
                    P.op("scalar", ACT(ob[b][:], ps_[:], AF.Gelu), reads=[pk], writes=["ob%d" % b])
                    P.dma(self.uT[j * 128:(j + 1) * 128, :], ob[b][:], reads=["ob%d" % b])
                return load, comp

            def mk_tm(which, c0, dst, mb):
                dv = dst.rearrange("(i p) m -> p i m", p=128)

                def load(b):
                    self.load_w(wst[b], wbf[b], "w%d" % b, win[:, c0 + mb * 256:c0 + (mb + 1) * 256], 16, 256)

                def comp(b):
                    wk = "w%d" % b
                    for half in range(2):
                        pcnt = state["pcnt"]
                        ps_ = pss[pcnt % 2]
                        pk = "psA" if pcnt % 2 == 0 else "psB"
                        state["pcnt"] = pcnt + 1
                        for i8 in range(8):
                            tt = half * 8 + i8
                            P.op("tensor", MM([(ps_[:, i8 * 256:(i8 + 1) * 256], hT[:, c, tt * 128:(tt + 1) * 128], wbf[b][:, c, :], c == 0, c == 15)
                                               for c in range(16)]), reads=[wk, "hT"], writes=[pk])
                        o_ = ob[pcnt % 2]
                        ok_ = "ob%d" % (pcnt % 2)
                        if which == 0:
                            P.op("scalar", ACP(o_[:], ps_[:]), reads=[pk], writes=[ok_])
                        else:
                            P.op("scalar", ACT(t1[:], ps_[:], AF.Gelu), reads=[pk], writes=["t1"])
                            v4 = t1[:].rearrange("p (a g d) -> p a g d", a=8, g=2)
                            w4 = t2[:].rearrange("p (a g d) -> p a g d", a=8, g=2)
                            P.op("vector", RSUM(st8[:], v4), reads=["t1"], writes=["st8"])
                            P.op("vector", TS(st8[:], st8[:], 1.0 / 128.0), reads=["st8"], writes=["st8"])
                            P.op("vector", TT(w4, v4, st8[:].unsqueeze(3).to_broadcast([128, 8, 2, 128]), ALU.subtract),
                                 reads=["t1", "st8"], writes=["t2"])
                            P.op("gpsimd", TT(t1[:], t2[:], t2[:], ALU.mult), reads=["t2"], writes=["t1"])
                            P.op("vector", RSUM(st9[:], v4), reads=["t1"], writes=["st9"])
                            P.op("scalar", ACT(st9[:], st9[:], AF.Sqrt, bias=eps[:], scale=1.0 / 128.0), reads=["st9", "eps"], writes=["st9"])
                            P.op("vector", RECIP(st9[:], st9[:]), reads=["st9"], writes=["st9"])
                            P.op("vector", TT(w4, w4, st9[:].unsqueeze(3).to_broadcast([128, 8, 2, 128]), ALU.mult),
                                 reads=["t2", "st9"], writes=["t2"])
                            w3 = t2[:].rearrange("p (a m) -> p a m", a=8)
                            gsl = lng[:, 0, mb * 256:(mb + 1) * 256].unsqueeze(1).to_broadcast([128, 8, 256])
                            bsl = lng[:, 1, mb * 256:(mb + 1) * 256].unsqueeze(1).to_broadcast([128, 8, 256])
                            P.op("vector", TT(w3, w3, gsl, ALU.mult), reads=["t2", "lng"], writes=["t2"])
                            P.op("vector", TT(o_[:].rearrange("p (a m) -> p a m", a=8), w3, bsl, ALU.add), reads=["t2", "lng"], writes=[ok_])
                        P.dma(dv[:, half * 8:(half + 1) * 8, mb * 256:(mb + 1) * 256], o_[:].rearrange("p (a m) -> p a m", a=8), reads=[ok_])
                return load, comp

            for (c_main, c_perm, dst) in ((0, 5120, self.qT), (1024, 6144, self.kT)):
                for j in range(8):
                    jobs.append(mk_qk(c_main, c_perm, dst, j))
            for j in range(8):
                jobs.append(mk_u(j))
            for which, (c0, dst) in enumerate(((2048, self.vtm), (4096, self.vn))):
                for mb in range(4):
                    jobs.append(mk_tm(which, c0, dst, mb))
            jobs[0][0](0)
            for n, (ld, cmp_) in enumerate(jobs):
                if n + 1 < len(jobs):
                    jobs[n + 1][0]((n + 1) % 2)
                cmp_(n % 2)
            P.flush()

    def even_attn(self, i):
        P = self.P
        with ExitStack() as st:
            V = self.sb(st, "V", [128, 16, 1024], BF16)
            att = self.sb(st, "att", [128, 16, 1024], BF16)
            cst = self.sb(st, "cst", [128, 8, 128], F32)
            ident = self.sb(st, "ident", [128, 128], BF16)
            keep = self.sb(st, "keep", [128, 16, 16], F32)
            gmask = self.sb(st, "gmask", [128, 16, 8], F32)
            qh = [self.sb(st, "qh", [64, T], BF16) for _ in range(2)]
            kh = [self.sb(st, "kh", [64, T], BF16) for _ in range(2)]
            km = self.sb(st, "km", [64, 8], F32)
            kmb = self.sb(st, "kmb", [64, 8], BF16)
            gate = self.sb(st, "gate", [128, 16, 8], F32)
            m8 = self.sb(st, "m8", [128, 16, 8], F32)
            bias8 = self.sb(st, "bias8", [128, 16, 8], F32)
            bias16 = self.sb(st, "bias16", [128, 16, 16], F32)
            sc = [self.sb(st, "sc", [128, T], F32) for _ in range(2)]
            pb = [self.sb(st, "pb", [128, T], BF16) for _ in range(2)]
            pT = [self.sb(st, "pT", [128, 16, 128], BF16) for _ in range(2)]
            mx = [self.sb(st, "mx", [128, 4], F32) for _ in range(3)]
            psS = self.pp(st, "psS", [128, T], F32)
            psT = self.pp(st, "psT", [128, T], BF16)
            psG = self.pp(st, "psG", [128, 512], F32)
            psO = self.pp(st, "psO", [128, 512], F32)
            P.dma(V[:], self.vtm.rearrange("(i p) m -> p i m", p=128), writes=["V"])
            P.dma(cst[:], self.consts, writes=["cst"])
            P.dma(keep[:], self.keep, writes=["keep"])
            P.dma(gmask[:], self.gmask, writes=["gmask"])
            P.op("vector", CP(ident[:], cst[:, 0, :]), reads=["cst"], writes=["ident"])
            causal = cst[:, 4, :]

            def load_qk(h):
                b = h % 2
                P.dma(qh[b][:], self.qT[h * 64:(h + 1) * 64, :], writes=["qh%d" % b])
                P.dma(kh[b][:], self.kT[h * 64:(h + 1) * 64, :], writes=["kh%d" % b])

            def S1(h, qt):
                b = h % 2
                par = qt % 2
                qk, kk_ = "qh%d" % b, "kh%d" % b
                sc_, mx_ = sc[par], mx[qt % 3]
                sk, mk = "sc%d" % par, "mx%d" % (qt % 3)
                nk = (qt + 1) * 128
                q_sl = qh[b][:, qt * 128:(qt + 1) * 128]
                mms = []
                for n0 in range(0, nk, 512):
                    n1 = min(nk, n0 + 512)
                    mms.append((psS[:, n0:n1], q_sl, kh[b][:, n0:n1], True, True))
                P.op("tensor", MM(mms), reads=[qk, kk_], writes=["psS"])
                if qt > 0:
                    P.op("vector", STT(sc_[:, 0:qt * 128].rearrange("p (a k) -> p a k", k=128),
                                       psS[:, 0:qt * 128].rearrange("p (a k) -> p a k", k=128), 0.125,
                                       bias16[:, qt, 0:qt].unsqueeze(2).to_broadcast([128, qt, 128]), ALU.mult, ALU.add),
                         reads=["psS", "bias16"], writes=[sk])
                P.op("vector", STT(sc_[:, qt * 128:nk], psS[:, qt * 128:nk], 0.125, causal, ALU.mult, ALU.add),
                     reads=["psS", "cst"], writes=[sk])
                P.op("vector", RMAX(mx_[:, 0:1], sc_[:, 0:nk]), reads=[sk], writes=[mk])
                P.op("vector", TS(mx_[:, 1:2], mx_[:, 0:1], -1.0), reads=[mk], writes=[mk])

            def S2(h, qt):
                par = qt % 2
                sc_, pb_, mx_ = sc[par], pb[par], mx[qt % 3]
                sk, pk, mk = "sc%d" % par, "pb%d" % par, "mx%d" % (qt % 3)
                nk = (qt + 1) * 128
                P.op("scalar", ACT(pb_[:, 0:nk], sc_[:, 0:nk], AF.Exp, bias=mx_[:, 1:2], accum=mx_[:, 2:3]), reads=[sk, mk], writes=[pk, mk])
                P.op("vector", RECIP(mx_[:, 3:4], mx_[:, 2:3]), reads=[mk], writes=[mk])
                P.op("tensor", TRS([(psT[:, kt * 128:(kt + 1) * 128], pb_[:, kt * 128:(kt + 1) * 128], ident[:]) for kt in range(qt + 1)]),
                     reads=[pk, "ident"], writes=["psT"])
                P.op("scalar", ACP(pT[par][:, 0:qt + 1, :], psT[:, 0:nk].rearrange("p (a k) -> p a k", k=128)), reads=["psT"], writes=["pT%d" % par])

            def S3(h, qt):
                par = qt % 2
                mx_ = mx[qt % 3]
                mk = "mx%d" % (qt % 3)
                P.op("tensor", MM([(psO[:, 0:64], pT[par][:, kt, :], V[:, kt, h * 64:(h + 1) * 64], kt == 0, kt == qt) for kt in range(qt + 1)]),
                     reads=["pT%d" % par, "V"], writes=["psO"])
                P.op("vector", TS(att[:, qt, h * 64:(h + 1) * 64], psO[:, 0:64], mx_[:, 3:4]), reads=["psO", mk], writes=["att"])

            load_qk(0)
            par = 0
            for h in range(16):
                b = h % 2
                qk, kk_ = "qh%d" % b, "kh%d" % b
                if h + 1 < 16:
                    load_qk(h + 1)
                P.op("vector", RSUM(km[:], kh[b][:].rearrange("p (n k) -> p n k", k=256)), reads=[kk_], writes=["km"])
                P.op("vector", TS(kmb[:], km[:], 1.0 / 256.0), reads=["km"], writes=["kmb"])
                P.op("tensor", MM([(psG[:, qt * 8:(qt + 1) * 8], qh[b][:, qt * 128:(qt + 1) * 128], kmb[:], True, True) for qt in range(16)]),
                     reads=[qk, "kmb"], writes=["psG"])
                P.op("vector", TT(gate[:], psG[:, 0:128].rearrange("p (a n) -> p a n", n=8), gmask[:], ALU.add),
                     reads=["psG", "gmask"], writes=["gate"])
                for qt in range(16):
                    P.op("vector", MAX8(m8[:, qt, :], gate[:, qt, :]), reads=["gate"], writes=["m8"])
                P.op("vector", TT(bias8[:], gate[:], m8[:, :, 2:3].to_broadcast([128, 16, 8]), ALU.is_ge), reads=["gate", "m8"], writes=["bias8"])
                P.op("vector", TS(bias8[:], bias8[:], -1.0, -NEG, ALU.add, ALU.mult), reads=["bias8"], writes=["bias8"])
                P.op("vector", CP(bias16[:].rearrange("p a (n two) -> p a n two", two=2), bias8[:].unsqueeze(3).to_broadcast([128, 16, 8, 2])),
                     reads=["bias8"], writes=["bias16"])
                P.op("vector", TT(bias16[:], bias16[:], keep[:], ALU.mult), reads=["bias16", "keep"], writes=["bias16"])
                S1(h, 0)
                S1(h, 1)
                S2(h, 0)
                for qt in range(16):
                    if qt + 2 < 16:
                        S1(h, qt + 2)
                    if qt + 1 < 16:
                        S2(h, qt + 1)
                    S3(h, qt)
            yv = self.ycat.rearrange("(c p) t -> p c t", p=128)
            aT = [self.sb(st, "aT", [128, T], BF16) for _ in range(2)]
            for c in range(8):
                P.op("tensor", TRS([(psT[:, qt * 128:(qt + 1) * 128], att[:, qt, c * 128:(c + 1) * 128], ident[:]) for qt in range(16)]),
                     reads=["att", "ident"], writes=["psT"])
                P.op("scalar", ACP(aT[c % 2][:], psT[:]), reads=["psT"], writes=["aT%d" % (c % 2)])
                P.dma(yv[:, c, :], aT[c % 2][:], reads=["aT%d" % (c % 2)])
            P.flush()

    def even_sgu(self, i):
        P = self.P
        with ExitStack() as st:
            vn = self.sb(st, "vn", [128, 16, 1024], BF16)
            uT = self.sb(st, "uT", [128, 8, T], BF16)
            wsf = self.sb(st, "wsf", [128, 8, 128], F32)
            wsb = self.sb(st, "wsb", [128, 8, 128], BF16)
            cst = self.sb(st, "cst", [128, 8, 128], F32)
            bsb = self.sb(st, "bsb", [128, 8, 128], F32)
            tmp = self.sb(st, "tmp", [128, T], F32)
            ob = [self.sb(st, "ob", [128, T], BF16) for _ in range(2)]
            ps = [self.pp(st, "ps", [128, T], F32) for _ in range(2)]
            P.dma(vn[:], self.vn.rearrange("(i p) m -> p i m", p=128), writes=["vn"])
            P.dma(uT[:], self.uT.rearrange("(c p) t -> p c t", p=128), writes=["uT"])
            P.dma(wsf[:], self.sgu_wT[i].rearrange("g s t -> s g t"), writes=["wsf"])
            P.dma(cst[:], self.consts, writes=["cst"])
            P.dma(bsb[:].rearrange("p g t -> p (g t)"), self.sgu_b[i, :].partition_broadcast(128), writes=["bsb"])
            P.op("vector", TT(wsb[:], wsf[:], cst[:, 3, :].unsqueeze(1).to_broadcast([128, 8, 128]), ALU.mult),
                 reads=["wsf", "cst"], writes=["wsb"])
            yv = self.ycat.rearrange("(c p) t -> p c t", p=128)
            for g in range(8):
                b = g % 2
                P.op("tensor", MM([(ps[b][:, c * 128:(c + 1) * 128], vn[:, c, g * 128:(g + 1) * 128], wsb[:, g, :], True, True) for c in range(16)]),
                     reads=["vn", "wsb"], writes=["ps%d" % b])
                P.op("vector", TT(tmp[:].rearrange("p (c t) -> p c t", t=128), ps[b][:].rearrange("p (c t) -> p c t", t=128),
                                  bsb[:, g, :].unsqueeze(1).to_broadcast([128, 16, 128]), ALU.add), reads=["ps%d" % b, "bsb"], writes=["tmp"])
                P.op("vector", TT(ob[b][:], tmp[:], uT[:, g, :], ALU.mult), reads=["tmp", "uT"], writes=["ob%d" % b])
                P.dma(yv[:, 8 + g, :], ob[b][:], reads=["ob%d" % b])
            P.flush()

    def lin_tm(self, src, w_ap, dst, KC=16):
        P = self.P
        with ExitStack() as st:
            hT = self.sb(st, "hT", [128, 16, T], BF16)
            wst = [self.sb(st, "wst", [128, 16, 256], F32) for _ in range(2)]
            wbf = [self.sb(st, "wbf", [128, 16, 256], BF16) for _ in range(2)]
            ob = [self.sb(st, "ob", [128, T], F32) for _ in range(2)]
            ps = [self.pp(st, "ps", [128, T], F32) for _ in range(2)]
            self.load_hT(hT, src)
            dv = dst.rearrange("(i p) m -> p i m", p=128)
            pcnt = 0
            self.load_w(wst[0], wbf[0], "w0", w_ap[:, 0:256], 16, 256)
            for mb in range(8):
                b = mb % 2
                wk = "w%d" % b
                if mb + 1 < 8:
                    self.load_w(wst[1 - b], wbf[1 - b], "w%d" % (1 - b), w_ap[:, (mb + 1) * 256:(mb + 2) * 256], 16, 256)
                for half in range(2):
                    pbk = pcnt % 2
                    pcnt += 1
                    for i8 in range(8):
                        tt = half * 8 + i8
                        P.op("tensor", MM([(ps[pbk][:, i8 * 256:(i8 + 1) * 256], hT[:, c, tt * 128:(tt + 1) * 128], wbf[b][:, c, :], c == 0, c == 15)
                                           for c in range(16)]), reads=[wk, "hT"], writes=["ps%d" % pbk])
                    eng = "scalar" if pbk == 0 else "vector"
                    P.op(eng, (ACP if eng == "scalar" else CP)(ob[pbk][:], ps[pbk][:]), reads=["ps%d" % pbk], writes=["ob%d" % pbk])
                    P.dma(dv[:, half * 8:(half + 1) * 8, mb * 256:(mb + 1) * 256], ob[pbk][:].rearrange("p (a m) -> p a m", a=8), reads=["ob%d" % pbk])
            P.flush()

    def lora_tm(self, src, w1_ap, R, func, w2_ap, dst):
        P = self.P
        RC = (R + 127) // 128
        rows = [min(128, R - rc * 128) for rc in range(RC)]
        with ExitStack() as st:
            hT = self.sb(st, "hT", [128, 16, T], BF16)
            w1s = self.sb(st, "w1s", [128, 16, R], F32)
            w1b = self.sb(st, "w1b", [128, 16, R], BF16)
            w2s = self.sb(st, "w2s", [128, RC, D], F32)
            w2b = self.sb(st, "w2b", [128, RC, D], BF16)
            lT = self.sb(st, "lT", [128, RC, T], BF16)
            ob = [self.sb(st, "ob", [128, T], F32) for _ in range(2)]
            ps = [self.pp(st, "ps", [128, T], F32) for _ in range(2)]
            self.load_hT(hT, src)
            self.load_w(w1s, w1b, "w1", w1_ap, 16, R)
            for rc in range(RC):
                P.dma(w2s[0:rows[rc], rc, :], w2_ap[rc * 128:rc * 128 + rows[rc], :], writes=["w2s"])
                P.op("gpsimd", CP(w2b[0:rows[rc], rc, :], w2s[0:rows[rc], rc, :]), reads=["w2s"], writes=["w2b"])
            for rc in range(RC):
                pbk = rc % 2
                for tb in range(4):
                    ts = slice(tb * 512, (tb + 1) * 512)
                    P.op("tensor", MM([(ps[pbk][0:rows[rc], ts], w1b[:, c, rc * 128:rc * 128 + rows[rc]], hT[:, c, ts], c == 0, c == 15)
                                       for c in range(16)]), reads=["w1", "hT"], writes=["ps%d" % pbk])
                P.op("scalar", ACT(lT[0:rows[rc], rc, :], ps[pbk][0:rows[rc], :], func), reads=["ps%d" % pbk], writes=["lT"])
            dv = dst.rearrange("(i p) m -> p i m", p=128)
            pcnt = 0
            for mb in range(8):
                for half in range(2):
                    pbk = pcnt % 2
                    pcnt += 1
                    for i8 in range(8):
                        tt = half * 8 + i8
                        P.op("tensor", MM([(ps[pbk][:, i8 * 256:(i8 + 1) * 256], lT[0:rows[rc], rc, tt * 128:(tt + 1) * 128],
                                            w2b[0:rows[rc], rc, mb * 256:(mb + 1) * 256], rc == 0, rc == RC - 1) for rc in range(RC)]),
                             reads=["lT", "w2b"], writes=["ps%d" % pbk])
                    eng = "scalar" if pbk == 0 else "vector"
                    P.op(eng, (ACP if eng == "scalar" else CP)(ob[pbk][:], ps[pbk][:]), reads=["ps%d" % pbk], writes=["ob%d" % pbk])
                    P.dma(dv[:, half * 8:(half + 1) * 8, mb * 256:(mb + 1) * 256], ob[pbk][:].rearrange("p (a m) -> p a m", a=8), reads=["ob%d" % pbk])
            P.flush()

    def rwkv_scan(self, i, use_vres):
        P = self.P
        W = 512
        vsrc = self.v_tm
        with ExitStack() as st:
            cst = self.sb(st, "cst", [128, 8, 128], F32)
            ident = self.sb(st, "ident", [128, 128], BF16)
            mL = self.sb(st, "mL", [128, 128], BF16)
            mU = self.sb(st, "mU", [128, 128], BF16)
            mUi = self.sb(st, "mUi", [128, 128], BF16)
            negcol = self.sb(st, "negcol", [128, 1], F32)
            eps = self.sb(st, "eps", [128, 1], F32)
            P.dma(cst[:], self.consts, writes=["cst"])
            P.op("vector", CP(ident[:], cst[:, 0, :]), reads=["cst"], writes=["ident"])
            P.op("vector", CP(mL[:], cst[:, 1, :]), reads=["cst"], writes=["mL"])
            P.op("vector", CP(mU[:], cst[:, 2, :]), reads=["cst"], writes=["mU"])
            P.op("vector", CP(mUi[:], cst[:, 3, :]), reads=["cst"], writes=["mUi"])
            P.op("vector", MSET(negcol[:], -EXPM05), writes=["negcol"])
            P.op("vector", MSET(eps[:], 64e-5), writes=["eps"])
            triN = cst[:, 5, :]
            allN = cst[:, 6, :]
            ogv = self.ycat.rearrange("(c p) t -> p c t", p=128)

            def h3(ap):
                return ap.rearrange("p (h d) -> p h d", d=64)

            def bc8(ap, n=64):
                return ap.unsqueeze(2).to_broadcast([128, 8, n])

            def mb8(m):
                return m[:].unsqueeze(1).to_broadcast([128, 8, 128])

            def v3(ap):
                return ap.rearrange("p (h t) -> p h t", t=128)

            sets = []
            for si in range(2):
                d = {}
                for nm in ("r_t", "k_t", "v_t", "zw", "za", "g_t", "cl", "E1", "E2", "tmp", "tmp2", "kk", "kmod", "b_", "Ut"):
                    d[nm] = self.sb(st, nm, [128, W], F32)
                if use_vres:
                    d["zv"] = self.sb(st, "zv", [128, W], F32)
                    d["vf"] = self.sb(st, "vf", [128, W], F32)
                for nm in ("At", "Bt", "Kt", "Rt", "Bh", "Kh", "Vb", "AkV", "Ub", "og"):
                    d[nm] = self.sb(st, nm, [128, W], BF16)
                for nm in ("ATf", "BTf", "KTf", "RTf", "WTf"):
                    d[nm] = self.sb(st, nm, [64, 8, 128], BF16)
                for nm in ("Mm0", "Mm1", "MT0", "MT1", "TT", "AakT", "ArbT", "ArkT"):
                    d[nm] = self.sb(st, nm, [128, 8, 128], BF16)
                d["bc"] = self.sb(st, "bc", [128, 8, W], F32)
                d["s8"] = self.sb(st, "s8", [128, 8, 4], F32)
                d["PC"] = self.sb(st, "PC", [64, 8], F32)
                d["S"] = self.sb(st, "S", [64, 8, 64], F32)
                d["Sb0"] = self.sb(st, "Sb0", [64, 8, 64], BF16)
                d["Sb1"] = self.sb(st, "Sb1", [64, 8, 64], BF16)
                d["ogT"] = self.sb(st, "ogT", [128, 4, 128], BF16)
                d["psM"] = self.pp(st, "psM", [128, 1024], F32)
                d["psW"] = self.pp(st, "psW", [128, 512], F32)
                d["psT"] = self.pp(st, "psT", [128, 1024], BF16)
                sets.append(d)

            def body(hg, si):
                d = sets[si]
                sfx = "_%d" % si

                def k(*names):
                    return [n + sfx if n not in ("cst", "ident", "mL", "mU", "mUi", "negcol", "eps") else n for n in names]

                def OP(eng, fn, r, w):
                    P.op(eng, fn, reads=k(*r), writes=k(*w))
                r_t, k_t, v_t, zw, za, g_t = d["r_t"], d["k_t"], d["v_t"], d["zw"], d["za"], d["g_t"]
                cl, E1, E2, tmp, tmp2, kk, kmod, b_, Ut = d["cl"], d["E1"], d["E2"], d["tmp"], d["tmp2"], d["kk"], d["kmod"], d["b_"], d["Ut"]
                At, Bt, Kt, Rt, Bh, Kh, Vb, AkV, Ub, og = (d[n] for n in ("At", "Bt", "Kt", "Rt", "Bh", "Kh", "Vb", "AkV", "Ub", "og"))
                ATf, BTf, KTf, RTf, WTf = (d[n] for n in ("ATf", "BTf", "KTf", "RTf", "WTf"))
                Mm = [d["Mm0"], d["Mm1"]]
                MT = [d["MT0"], d["MT1"]]
                TTm, AakT, ArbT, ArkT = d["TT"], d["AakT"], d["ArbT"], d["ArkT"]
                bc, s8, PC, S, ogT = d["bc"], d["s8"], d["PC"], d["S"], d["ogT"]
                Sb = [d["Sb0"], d["Sb1"]]
                psM, psW, psT = d["psM"], d["psW"], d["psT"]
                sg, a_, E3, E4, scr, o_ = zw, za, tmp, tmp2, cl, Ut
                cs = slice(hg * W, (hg + 1) * W)
                for j in range(8):
                    P.dma(bc[:, j, :], self.rv[i, j, cs].partition_broadcast(128), writes=k("bc"))
                OP("vector", MSET(S[:], 0.0), [], ["S"])
                OP("vector", MSET(Sb[0][:], 0.0), [], ["Sb0"])
                yield
                for tt in range(16):
                    rs_ = slice(tt * 128, (tt + 1) * 128)
                    for (tile_, src, key) in ((r_t, self.r_tm, "r_t"), (k_t, self.k_tm, "k_t"), (v_t, vsrc, "v_t"),
                                              (zw, self.zw_tm, "zw"), (za, self.za_tm, "za"), (g_t, self.g_tm, "g_t")):
                        P.dma(tile_[:], src[rs_, cs], writes=k(key))
                    if use_vres:
                        P.dma(d["zv"][:], self.zv_tm[rs_, cs], writes=k("zv"))
                        P.dma(d["vf"][:], self.vf_tm[rs_, cs], writes=k("vf"))
                    yield
                    OP("vector", TT(zw[:], zw[:], bc[:, 0, :], ALU.add), ["zw", "bc"], ["zw"])
                    OP("scalar", ACT(sg[:], zw[:], AF.Sigmoid), ["zw"], ["zw"])
                    yield
                    OP("tensor", MM([(psW[:], triN, sg[:], True, True)]), ["cst", "zw"], ["psW"])
                    OP("scalar", ACT(cl[:], psW[:], AF.Identity), ["psW"], ["cl"])
                    yield
                    OP("tensor", MM([(psW[:], allN, sg[:], True, True)]), ["cst", "zw"], ["psW"])
                    OP("vector", TT(tmp2[:], psW[:], cl[:], ALU.subtract), ["psW", "cl"], ["tmp2"])
                    OP("scalar", ACT(E4[:], tmp2[:], AF.Exp), ["tmp2"], ["tmp2"])
                    yield
                    OP("tensor", MM([(psW[0:64, hh:hh + 1], sg[:, hh * 64:(hh + 1) * 64], negcol[:], True, True) for hh in range(8)]),
                       ["zw", "negcol"], ["psW"])
                    OP("scalar", ACT(PC[:], psW[0:64, 0:8], AF.Exp), ["psW"], ["PC"])
                    yield
                    OP("scalar", ACT(E1[:], cl[:], AF.Exp), ["cl"], ["E1"])
                    OP("scalar", ACT(E2[:], cl[:], AF.Exp, scale=-1.0), ["cl"], ["E2"])
                    OP("vector", STT(tmp[:], sg[:], EXPM05, cl[:], ALU.mult, ALU.add), ["zw", "cl"], ["tmp"])
                    OP("scalar", ACT(E3[:], tmp[:], AF.Exp), ["tmp"], ["tmp"])
                    yield
                    OP("vector", TT(za[:], za[:], bc[:, 1, :], ALU.add), ["za", "bc"], ["za"])
                    OP("scalar", ACT(a_[:], za[:], AF.Sigmoid), ["za"], ["za"])
                    OP("vector", TT(kk[:], k_t[:], bc[:, 2, :], ALU.mult), ["k_t", "bc"], ["kk"])
                    yield
                    OP("gpsimd", TT(scr[:], kk[:], kk[:], ALU.mult), ["kk"], ["cl"])
                    OP("vector", RSUM(s8[:, :, 0], h3(scr[:])), ["cl"], ["s8"])
                    OP("scalar", ACT(s8[:, :, 0], s8[:, :, 0], AF.Sqrt), ["s8"], ["s8"])
                    yield
                    OP("vector", TS(s8[:, :, 0], s8[:, :, 0], 1e-12, None, ALU.max), ["s8"], ["s8"])
                    OP("vector", RECIP(s8[:, :, 0], s8[:, :, 0]), ["s8"], ["s8"])
                    OP("vector", TT(h3(kk[:]), h3(kk[:]), bc8(s8[:, :, 0]), ALU.mult), ["kk", "s8"], ["kk"])
                    yield
                    OP("vector", STT(scr[:], a_[:], -1.0, bc[:, 3, :], ALU.add, ALU.mult), ["za", "bc"], ["cl"])
                    OP("vector", STT(kmod[:], scr[:], 1.0, k_t[:], ALU.add, ALU.mult), ["cl", "k_t"], ["kmod"])
                    OP("gpsimd", TT(b_[:], kk[:], a_[:], ALU.mult), ["kk", "za"], ["b_"])
                    yield
                    if use_vres:
                        zv, vf = d["zv"], d["vf"]
                        OP("vector", TT(zv[:], zv[:], bc[:, 7, :], ALU.add), ["zv", "bc"], ["zv"])
                        OP("scalar", ACT(zv[:], zv[:], AF.Sigmoid), ["zv"], ["zv"])
                        OP("gpsimd", TT(vf[:], vf[:], v_t[:], ALU.subtract), ["vf", "v_t"], ["vf"])
                        yield
                        OP("vector", TT(vf[:], vf[:], zv[:], ALU.mult), ["vf", "zv"], ["vf"])
                        OP("vector", TT(v_t[:], v_t[:], vf[:], ALU.add), ["vf", "v_t"], ["v_t"])
                        yield
                    OP("vector", STT(At[:], kk[:], -1.0, E3[:], ALU.mult, ALU.mult), ["kk", "tmp"], ["At"])
                    OP("gpsimd", TT(Bt[:], b_[:], E2[:], ALU.mult), ["b_", "E2"], ["Bt"])
                    yield
                    OP("gpsimd", TT(Kt[:], kmod[:], E2[:], ALU.mult), ["kmod", "E2"], ["Kt"])
                    OP("gpsimd", TT(Rt[:], r_t[:], E1[:], ALU.mult), ["r_t", "E1"], ["Rt"])
                    yield
                    OP("vector", TT(Bh[:], b_[:], E4[:], ALU.mult), ["b_", "tmp2"], ["Bh"])
                    OP("gpsimd", TT(Kh[:], kmod[:], E4[:], ALU.mult), ["kmod", "tmp2"], ["Kh"])
                    OP("scalar", ACP(Vb[:], v_t[:]), ["v_t"], ["Vb"])
                    yield
                    for (src_, dst_, sk, dk) in ((At, ATf, "At", "ATf"), (Bt, BTf, "Bt", "BTf"), (Kt, KTf, "Kt", "KTf"), (Rt, RTf, "Rt", "RTf")):
                        OP("tensor", TRS([(psT[0:64, hh * 128:(hh + 1) * 128], src_[:, hh * 64:(hh + 1) * 64], ident[:]) for hh in range(8)]),
                           [sk, "ident"], ["psT"])
                        OP("scalar", ACP(dst_[:], v3(psT[0:64, 0:1024])), ["psT"], [dk])
                        yield
                    for (lf, rf, lk, rk_, mask, mk, dst_, dk, eng) in (
                            (ATf, BTf, "ATf", "BTf", mL, "mL", Mm[0], "Mm0", "vector"),
                            (BTf, ATf, "BTf", "ATf", mU, "mU", MT[0], "MT0", "gpsimd"),
                            (KTf, ATf, "KTf", "ATf", mU, "mU", AakT, "AakT", "vector"),
                            (BTf, RTf, "BTf", "RTf", mUi, "mUi", ArbT, "ArbT", "vector"),
                            (KTf, RTf, "KTf", "RTf", mUi, "mUi", ArkT, "ArkT", "vector")):
                        OP("tensor", MM([(psM[:, hh * 128:(hh + 1) * 128], lf[:, hh, :], rf[:, hh, :], True, True) for hh in range(8)]),
                           [lk, rk_], ["psM"])
                        if eng == "gpsimd":
                            OP("scalar", ACP(dst_[:], v3(psM[:])), ["psM"], [dk])
                            OP("gpsimd", TT(dst_[:], dst_[:], mb8(mask), ALU.mult), [dk, mk], [dk])
                        else:
                            OP("vector", TT(dst_[:], v3(psM[:]), mb8(mask), ALU.mult), ["psM", mk], [dk])
                        yield
                    OP("gpsimd", TT(TTm[:], MT[0][:], ident[:].unsqueeze(1).to_broadcast([128, 8, 128]), ALU.add), ["MT0", "ident"], ["TT"])
                    yield
                    cur = 0
                    for lev in range(1, 7):
                        nxt = 1 - cur
                        OP("tensor", MM([(psM[:, hh * 128:(hh + 1) * 128], MT[cur][:, hh, :], Mm[cur][:, hh, :], True, True) for hh in range(8)]),
                           ["MT%d" % cur, "Mm%d" % cur], ["psM"])
                        OP("scalar", ACP(Mm[nxt][:], v3(psM[:])), ["psM"], ["Mm%d" % nxt])
                        yield
                        if lev < 6:
                            OP("tensor", MM([(psM[:, hh * 128:(hh + 1) * 128], Mm[cur][:, hh, :], MT[cur][:, hh, :], True, True) for hh in range(8)]),
                               ["MT%d" % cur, "Mm%d" % cur], ["psM"])
                            OP("scalar", ACP(MT[nxt][:], v3(psM[:])), ["psM"], ["MT%d" % nxt])
                            yield
                        OP("tensor", MM([(psM[:, hh * 128:(hh + 1) * 128], Mm[nxt][:, hh, :], TTm[:, hh, :], True, True) for hh in range(8)]),
                           ["Mm%d" % nxt, "TT"], ["psM"])
                        OP("vector", TT(TTm[:], v3(psM[:]), TTm[:], ALU.add), ["psM", "TT"], ["TT"])
                        yield
                        cur = nxt
                    OP("tensor", MM([(psM[0:64, hh * 128:(hh + 1) * 128], At[:, hh * 64:(hh + 1) * 64], TTm[:, hh, :], True, True) for hh in range(8)]),
                       ["At", "TT"], ["psM"])
                    OP("scalar", ACP(WTf[:], v3(psM[0:64, :])), ["psM"], ["WTf"])
                    yield
                    OP("tensor", MM([(psW[:, hh * 64:(hh + 1) * 64], AakT[:, hh, :], Vb[:, hh * 64:(hh + 1) * 64], True, True) for hh in range(8)]),
                       ["AakT", "Vb"], ["psW"])
                    OP("scalar", ACP(AkV[:], psW[:]), ["psW"], ["AkV"])
                    yield
                    OP("tensor", MM([(psW[:, hh * 64:(hh + 1) * 64], TTm[:, hh, :], AkV[:, hh * 64:(hh + 1) * 64], True, True) for hh in range(8)]),
                       ["TT", "AkV"], ["psW"])
                    OP("scalar", ACP(Ut[:], psW[:]), ["psW"], ["Ut"])
                    yield
                    so = "Sb%d" % (tt % 2)
                    sn = "Sb%d" % ((tt + 1) % 2)
                    Sold = Sb[tt % 2]
                    Snew = Sb[(tt + 1) % 2]
                    OP("tensor", MM([(psW[:, hh * 64:(hh + 1) * 64], WTf[:, hh, :], Sold[:, hh, :], True, True) for hh in range(8)]),
                       ["WTf", so], ["psW"])
                    OP("vector", TT(Ub[:], psW[:], Ut[:], ALU.add), ["psW", "Ut"], ["Ub"])
                    yield
                    mm = []
                    for hh in range(8):
                        hs = slice(hh * 64, (hh + 1) * 64)
                        mm.append((psM[:, hs], RTf[:, hh, :], Sold[:, hh, :], True, False))
                        mm.append((psM[:, hs], ArbT[:, hh, :], Ub[:, hs], False, False))
                        mm.append((psM[:, hs], ArkT[:, hh, :], Vb[:, hs], False, True))
                    OP("tensor", MM(mm), ["RTf", so, "ArbT", "Ub", "ArkT", "Vb"], ["psM"])
                    mm = []
                    for hh in range(8):
                        hs = slice(hh * 64, (hh + 1) * 64)
                        mm.append((psW[0:64, hs], Kh[:, hs], Vb[:, hs], True, False))
                        mm.append((psW[0:64, hs], Bh[:, hs], Ub[:, hs], False, True))
                    OP("tensor", MM(mm), ["Kh", "Vb", "Bh", "Ub"], ["psW"])
                    yield
                    OP("vector", TT(S[:], S[:], PC[:].unsqueeze(2).to_broadcast([64, 8, 64]), ALU.mult), ["S", "PC"], ["S"])
                    OP("vector", TT(S[:], S[:], psW[0:64, :].rearrange("p (h v) -> p h v", v=64), ALU.add), ["S", "psW"], ["S"])
                    OP("scalar", ACP(Snew[:], S[:]), ["S"], [sn])
                    yield
                    OP("scalar", ACP(o_[:], psM[:, 0:512]), ["psM"], ["Ut"])
                    OP("vector", RSUM(s8[:, :, 1], h3(o_[:])), ["Ut"], ["s8"])
                    OP("vector", TS(s8[:, :, 1], s8[:, :, 1], 1.0 / 64.0), ["s8"], ["s8"])
                    yield
                    OP("vector", TT(h3(o_[:]), h3(o_[:]), bc8(s8[:, :, 1]), ALU.subtract), ["Ut", "s8"], ["Ut"])
                    OP("gpsimd", TT(tmp[:], o_[:], o_[:], ALU.mult), ["Ut"], ["tmp"])
                    OP("vector", RSUM(s8[:, :, 2], h3(tmp[:])), ["tmp"], ["s8"])
                    yield
                    OP("scalar", ACT(s8[:, :, 2], s8[:, :, 2], AF.Sqrt, bias=eps[:], scale=1.0 / 64.0), ["s8", "eps"], ["s8"])
                    OP("vector", RECIP(s8[:, :, 2], s8[:, :, 2]), ["s8"], ["s8"])
                    OP("vector", TT(h3(o_[:]), h3(o_[:]), bc8(s8[:, :, 2]), ALU.mult), ["Ut", "s8"], ["Ut"])
                    yield
                    OP("gpsimd", TT(tmp2[:], r_t[:], kmod[:], ALU.mult), ["r_t", "kmod"], ["tmp2"])
                    OP("gpsimd", TT(tmp2[:], tmp2[:], bc[:, 4, :], ALU.mult), ["tmp2", "bc"], ["tmp2"])
                    OP("vector", TT(o_[:], o_[:], bc[:, 5, :], ALU.mult), ["Ut", "bc"], ["Ut"])
                    OP("vector", TT(o_[:], o_[:], bc[:, 6, :], ALU.add), ["Ut", "bc"], ["Ut"])
                    yield
                    OP("vector", RSUM(s8[:, :, 3], h3(tmp2[:])), ["tmp2"], ["s8"])
                    OP("vector", TT(h3(tmp2[:]), h3(v_t[:]), bc8(s8[:, :, 3]), ALU.mult), ["v_t", "s8"], ["tmp2"])
                    OP("vector", TT(o_[:], o_[:], tmp2[:], ALU.add), ["Ut", "tmp2"], ["Ut"])
                    OP("vector", TT(og[:], o_[:], g_t[:], ALU.mult), ["Ut", "g_t"], ["og"])
                    yield
                    OP("tensor", TRS([(psT[:, c * 128:(c + 1) * 128], og[:, c * 128:(c + 1) * 128], ident[:]) for c in range(4)]),
                       ["og", "ident"], ["psT"])
                    OP("scalar", ACP(ogT[:], psT[:, 0:512].rearrange("p (c t) -> p c t", t=128)), ["psT"], ["ogT"])
                    P.dma(ogv[:, hg * 4:(hg + 1) * 4, rs_], ogT[:], reads=k("ogT"))
                    yield

            for pair in ((0, 1), (2, 3)):
                gens = [body(pair[0], 0), body(pair[1], 1)]
                alive = [True, True]
                while any(alive):
                    for gi in range(2):
                        if alive[gi]:
                            try:
                                next(gens[gi])
                            except StopIteration:
                                alive[gi] = False
            P.flush()

    def rwkv_layer(self, layer):
        i = layer // 2
        mu_base = 9 + 6 * i
        self.norm_phase(layer, "rwkv", mu_base=mu_base)
        self.lin_tm(self.xmix[0], self.w_rkv[i, 0], self.r_tm)
        self.lin_tm(self.xmix[2], self.w_rkv[i, 1], self.k_tm)
        self.lin_tm(self.xmix[3], self.w_rkv[i, 2], self.vf_tm if i == 0 else self.v_tm)
        self.lora_tm(self.xmix[1], self.w1[i], 96, AF.Tanh, self.w2[i], self.zw_tm)
        self.lora_tm(self.xmix[4], self.a1[i], 96, AF.Identity, self.a2[i], self.za_tm)
        self.lora_tm(self.xmix[5], self.g1[i], 256, AF.Sigmoid, self.g2[i], self.g_tm)
        if i > 0:
            self.lora_tm(self.xmix[3], self.v1[i - 1], 64, AF.Identity, self.v2[i - 1], self.zv_tm)

    def build(self):
        self.declare()
        with ExitStack() as gst:
            self.P = Prog(self.nc, gst)
            self.P.max_blocks = getattr(self, "max_blocks", 10 ** 9)
            plan = self.plan
            if "copy" in plan:
                self.copy_in()
            for layer in range(4):
                i = layer // 2
                if ("mix%d" % layer) in plan:
                    if layer % 2 == 0:
                        self.norm_phase(layer, "plain")
                        self.even_proj(i)
                        self.even_attn(i)
                        self.even_sgu(i)
                        self.proj_residual(self.ycat, self.w_out[i])
                    else:
                        self.rwkv_layer(layer)
                        if i == 0:
                            self._vsrc_first = True
                        self.rwkv_scan_wrap(i)
                        self.proj_residual(self.ycat, self.w_o[i])
                if ("ffn%d" % layer) in plan:
                    self.norm_phase(4 + layer, "plain")
                    self.ffn_phase(layer)
            if "final" in plan:
                self.norm_phase(8, "final")
        return self.nc

    def rwkv_scan_wrap(self, i):
        if i == 0:
            save = self.v_tm
            self.v_tm = self.vf_tm
            self.rwkv_scan(i, use_vres=False)
            self.v_tm = save
        else:
            self.rwkv_scan(i, use_vres=True)


def _fm16(v):
    return np.ascontiguousarray(v.reshape(-1, 128).T)


def make_shared(inp):
    f = np.float32
    sh = {}
    vfm = np.zeros((128, 21, 16), f)
    for l in range(4):
        vfm[:, l, :] = _fm16(inp["mix_norm_g"][l])
        vfm[:, 4 + l, :] = _fm16(inp["ffn_norm_g"][l])
    vfm[:, 8, :] = _fm16(inp["final_norm_g"])
    for i in range(2):
        for j in range(6):
            vfm[:, 9 + 6 * i + j, :] = _fm16(inp["rwkv_mu"][i, j])
    sh["vfm"] = vfm
    convp = np.zeros((128, 4, 4, 44), f)
    for l in range(4):
        for j in range(3):
            convp[:, l, j, :] = _fm16(inp["ffn_conv_w"][l, j])
        convp[:, l, 3, :] = _fm16(inp["ffn_conv_b"][l])
    sh["convp"] = convp
    rv = np.zeros((2, 8, D), f)
    for i in range(2):
        for j, nm in enumerate(["rwkv_w0", "rwkv_a0", "rwkv_k_k", "rwkv_k_a", "rwkv_r_k", "rwkv_gn_g", "rwkv_gn_b"]):
            rv[i, j] = inp[nm][i]
    rv[1, 7] = inp["rwkv_v0"][0]
    sh["rv"] = rv
    sh["sgu_ln"] = np.ascontiguousarray(np.stack([inp["sgu_ln_g"], inp["sgu_ln_b"]], axis=1)).astype(f)
    sh["sgu_wT"] = np.ascontiguousarray(np.transpose(inp["sgu_w"], (0, 1, 3, 2))).astype(f)
    sh["sgu_b"] = np.ascontiguousarray(inp["sgu_b"].reshape(2, 1024)).astype(f)
    perm = np.arange(1024)
    for h in range(16):
        base = h * 64
        perm[base:base + 8] = base + 8 + np.arange(8)
        perm[base + 8:base + 16] = base + np.arange(8)
    w_in = inp["even_w_in"]
    sh["w_in"] = np.ascontiguousarray(np.concatenate([w_in, w_in[:, :, 0:1024][:, :, perm], w_in[:, :, 1024:2048][:, :, perm]], axis=2))
    sh["w_out"] = inp["even_w_out"]
    sh["w_rkv"] = inp["rwkv_w_rkv"]
    sh["w_o"] = inp["rwkv_w_o"]
    for a, b in (("w1", "rwkv_w1"), ("w2", "rwkv_w2"), ("a1", "rwkv_a1"), ("a2", "rwkv_a2"), ("g1", "rwkv_g1"), ("g2", "rwkv_g2"),
                 ("v1", "rwkv_v1"), ("v2", "rwkv_v2"), ("w_up", "ffn_w_up"), ("w_down", "ffn_w_down")):
        sh[a] = inp[b]
    p = np.arange(128)[:, None]
    q = np.arange(128)[None, :]
    consts = np.zeros((128, 8, 128), f)
    consts[:, 0] = (p == q)
    consts[:, 1] = (p > q)
    consts[:, 2] = (q > p)
    consts[:, 3] = (q >= p)
    consts[:, 4] = np.where(q <= p, 0.0, NEG)
    consts[:, 5] = np.where(p <= q, -EXPM05, 0.0)
    consts[:, 6] = -EXPM05
    sh["consts"] = consts
    inv = np.power(np.float32(500000.0), -np.arange(0, 16, 2, dtype=f) / np.float32(16)).astype(f)
    ang = np.arange(T, dtype=f)[:, None] * inv[None, :]
    cos = np.cos(ang).astype(f).T
    sin = np.sin(ang).astype(f).T
    rope = np.zeros((2, 128, T), f)
    rope[0] = 1.0
    for hh in range(2):
        b = hh * 64
        rope[0, b:b + 8] = cos
        rope[0, b + 8:b + 16] = cos
        rope[1, b:b + 8] = -sin
        rope[1, b + 8:b + 16] = sin
    sh["rope"] = rope
    keep = np.ones((128, 16, 16), f)
    for qt in range(1, 16, 2):
        keep[:, qt, qt - 1] = 0.0
    sh["keepm"] = keep
    gm = np.zeros((128, 16, 8), f)
    for qt in range(16):
        gm[:, qt, qt // 2:] = -1e30
    sh["gmask"] = gm
    return {k: np.ascontiguousarray(v, dtype=np.float32) for k, v in sh.items()}


FULL_PLAN = ["copy"] + ["mix%d" % l for l in range(4)] + ["ffn%d" % l for l in range(4)] + ["final"]
_CACHE = {}


def run_plan(inp, plan, ncores=NCORES, max_blocks=None):
    key = tuple(plan) + (max_blocks,)
    if key not in _CACHE:
        bld = Builder(plan)
        if max_blocks is not None:
            bld.max_blocks = max_blocks
        _CACHE[key] = bld.build()
    nc = _CACHE[key]
    sh = make_shared(inp)
    x = inp["x"]
    in_maps = []
    for c in range(ncores):
        m = dict(sh)
        m["xT"] = np.ascontiguousarray(x[c].T)
        in_maps.append(m)
    res = run_bass_kernel_spmd(nc, in_maps, core_ids=list(range(ncores)))
    return res


def kernel(**inputs):
    inp = {k: np.asarray(v) for k, v in inputs.items()}
    res = run_plan(inp, FULL_PLAN)
    out = np.stack([np.ascontiguousarray(res.results[c]["out"].T) for c in range(NCORES)], axis=0)
    return out.astype(np.float32)
```
